# Optimizing a Trainium2 kernel written in Bass

```python
import jax
import jax.numpy as jnp
from jax import lax
import numpy as np

D_MODEL = 2048
BATCH = 2
SEQ = 4096
DEPTH = 1
DEC_BATCH = 32
DEC_SEQ = 8
PAST_LEN = 16384
PAGE_SIZE = 128

N_HEADS = 16
HEAD_DIM = 64
KV_HEADS = 4
Q_PER_KV = N_HEADS // KV_HEADS
ATT_WIDTH = N_HEADS * HEAD_DIM
KV_WIDTH = KV_HEADS * HEAD_DIM
CMP_BLOCK = 32
SEL_BLOCK = 64
CMP_PER_SEL = SEL_BLOCK // CMP_BLOCK
N_SEL = 16
WINDOW = 512
Q_BLOCK = 128
SCALE = HEAD_DIM ** -0.5
D_RNN = 1024
RNN_BLOCKS = 16
RNN_BLOCK_DIM = D_RNN // RNN_BLOCKS
CONV_W = 4
LRU_C = 8.0
IN_SIZES = (ATT_WIDTH, 6 * KV_WIDTH, 3 * N_HEADS, ATT_WIDTH, D_RNN, D_RNN, D_MODEL, D_MODEL)
IN_COLS = 2 * ATT_WIDTH + 6 * KV_WIDTH + 3 * N_HEADS + 2 * D_RNN + 2 * D_MODEL
EPS = 1e-6
NEG = -1e30
TINY = 1e-30

kernel_name = 'nsa_rglru_gated_hybrid_step'


def _rms(x, g):
    xf = x.astype(jnp.float32)
    y = xf * lax.rsqrt(jnp.mean(xf * xf, axis=-1, keepdims=True) + EPS)
    return (y * g.astype(jnp.float32)).astype(x.dtype)


def _masked_softmax(s, mask):
    s = jnp.where(mask, s.astype(jnp.float32), NEG)
    e = jnp.where(mask, jnp.exp(s - s.max(axis=-1, keepdims=True)), 0.0)
    return e / jnp.maximum(e.sum(axis=-1, keepdims=True), TINY)


def _project(x, norm_g, w_in, q_norm_g, k_norm_g):
    B, T, _ = x.shape
    z = jnp.einsum('btd,dc->btc', _rms(x, norm_g), w_in)
    parts, o = [], 0
    for size in IN_SIZES:
        parts.append(z[..., o:o + size])
        o += size
    q, kv, g_nsa, g_att, x_rnn, g_rnn, gm_att, gm_rnn = parts
    q = _rms(q.reshape(B, T, KV_HEADS, Q_PER_KV, HEAD_DIM), q_norm_g)
    kv = kv.reshape(B, T, 3, 2, KV_HEADS, HEAD_DIM)
    kv_cmp = kv[:, :, 0]
    kv_sel = jnp.stack([_rms(kv[:, :, 1, 0], k_norm_g[1]), kv[:, :, 1, 1]], axis=2)
    kv_win = jnp.stack([_rms(kv[:, :, 2, 0], k_norm_g[2]), kv[:, :, 2, 1]], axis=2)
    g_nsa = jax.nn.sigmoid(g_nsa.reshape(B, T, 3, KV_HEADS, Q_PER_KV, 1))
    return q, kv_cmp, kv_sel, kv_win, g_nsa, g_att, x_rnn, g_rnn, gm_att, gm_rnn


def _compress(kv_cmp, w_cmp, k_gain):
    B, L = kv_cmp.shape[:2]
    blocks = kv_cmp.reshape(B, L // CMP_BLOCK, CMP_BLOCK, 2, KV_HEADS, HEAD_DIM)
    c = jnp.einsum('bnjsgd,jsd->bnsgd', blocks, w_cmp).astype(kv_cmp.dtype)
    return _rms(c[:, :, 0], k_gain), c[:, :, 1]


def _cmp_branch(q, k_c, v_c, q_pos):
    n = jnp.arange(k_c.shape[1])
    vis = ((n + 1) * CMP_BLOCK - 1)[None, :] <= q_pos[:, None]
    s = jnp.einsum('btgrd,bngd->btgrn', q, k_c) * SCALE
    p = _masked_softmax(s, vis[None, :, None, None, :])
    o = jnp.einsum('btgrn,bngd->btgrd', p, v_c)
    imp = p.sum(axis=3)
    B, T, G, N = imp.shape
    imp = imp.reshape(B, T, G, N // CMP_PER_SEL, CMP_PER_SEL).sum(axis=-1)
    return o.astype(q.dtype), imp


def _select_blocks(imp, q_pos):
    nsb = imp.shape[-1]
    j = jnp.arange(nsb)[None, :]
    cur = (q_pos // SEL_BLOCK)[:, None]
    cand = (j < cur)[None, :, None, :]
    forced = ((j == 0) | (j == cur - 1))[None, :, None, :]
    score = jnp.where(cand, jnp.where(forced, jnp.inf, imp), -jnp.inf)
    if nsb < N_SEL - 1:
        score = jnp.pad(score, ((0, 0), (0, 0), (0, 0), (0, N_SEL - 1 - nsb)), constant_values=-jnp.inf)
    _, idx = lax.top_k(score, N_SEL - 1)
    valid = idx < cur[None, :, :, None]
    return jnp.where(valid, idx, 0), valid


def _sel_win_prompt(q, kv_sel, kv_win, sel_idx, sel_valid):
    B, T = q.shape[:2]
    nsb = T // SEL_BLOCK
    blocks = kv_sel.reshape(B, nsb, SEL_BLOCK, 2, KV_HEADS, HEAD_DIM)
    win_pad = jnp.pad(kv_win, ((0, 0), (WINDOW, 0), (0, 0), (0, 0), (0, 0)))
    b_ix = jnp.arange(B)[:, None, None, None]
    g_ix = jnp.arange(KV_HEADS)[None, None, :, None]
    offs = jnp.arange(SEL_BLOCK)
    band = jnp.arange(WINDOW + Q_BLOCK)

    def one_block(start):
        t = start + jnp.arange(Q_BLOCK)
        qb = lax.dynamic_slice_in_dim(q, start, Q_BLOCK, axis=1)
        cur = jnp.broadcast_to((t // SEL_BLOCK)[None, :, None, None], (B, Q_BLOCK, KV_HEADS, 1))
        idx = jnp.concatenate([lax.dynamic_slice_in_dim(sel_idx, start, Q_BLOCK, axis=1), cur], axis=-1)
        val = jnp.concatenate([lax.dynamic_slice_in_dim(sel_valid, start, Q_BLOCK, axis=1),
                               jnp.ones(cur.shape, bool)], axis=-1)
        kv = blocks[b_ix, idx, :, :, g_ix]
        kv = kv.reshape(B, Q_BLOCK, KV_HEADS, N_SEL * SEL_BLOCK, 2, HEAD_DIM)
        kpos = (idx[..., None] * SEL_BLOCK + offs).reshape(B, Q_BLOCK, KV_HEADS, N_SEL * SEL_BLOCK)
        smask = jnp.repeat(val, SEL_BLOCK, axis=-1) & (kpos <= t[None, :, None, None])
        s = jnp.einsum('bqgrd,bqgkd->bqgrk', qb, kv[..., 0, :]) * SCALE
        p = _masked_softmax(s, smask[:, :, :, None, :])
        o_sel = jnp.einsum('bqgrk,bqgkd->bqgrd', p, kv[..., 1, :])
        kw = lax.dynamic_slice_in_dim(win_pad, start, WINDOW + Q_BLOCK, axis=1)
        wpos = start - WINDOW + band
        wmask = (wpos[None, :] >= 0) & (wpos[None, :] <= t[:, None]) & (wpos[None, :] > t[:, None] - WINDOW)
        s = jnp.einsum('bqgrd,bkgd->bqgrk', qb, kw[:, :, 0]) * SCALE
        p = _masked_softmax(s, wmask[None, :, None, None, :])
        o_win = jnp.einsum('bqgrk,bkgd->bqgrd', p, kw[:, :, 1])
        return o_sel.astype(q.dtype), o_win.astype(q.dtype)

    o_sel, o_win = lax.map(one_block, jnp.arange(T // Q_BLOCK) * Q_BLOCK)
    return jnp.moveaxis(o_sel, 0, 1).reshape(q.shape), jnp.moveaxis(o_win, 0, 1).reshape(q.shape)


def _sel_win_sample(q, kv_sel_new, kv_win_new, sel_idx, sel_valid, cache_sel, page_table, cache_win, q_pos):
    DB, T = q.shape[:2]
    bpp = PAGE_SIZE // SEL_BLOCK
    pool = cache_sel.reshape(cache_sel.shape[0], bpp, SEL_BLOCK, 2, KV_HEADS, HEAD_DIM)
    b_ix = jnp.arange(DB)[:, None, None, None]
    g_ix = jnp.arange(KV_HEADS)[None, None, :, None]
    phys = page_table[b_ix, sel_idx // bpp]
    kv = pool[phys, sel_idx % bpp, :, :, g_ix]
    kv = kv.reshape(DB, T, KV_HEADS, (N_SEL - 1) * SEL_BLOCK, 2, HEAD_DIM)
    s_past = jnp.einsum('btgrd,btgkd->btgrk', q, kv[..., 0, :])
    s_new = jnp.einsum('btgrd,bsgd->btgrs', q, kv_sel_new[:, :, 0])
    causal = q_pos[None, :] <= q_pos[:, None]
    mask = jnp.concatenate([
        jnp.broadcast_to(jnp.repeat(sel_valid, SEL_BLOCK, axis=-1)[:, :, :, None, :], s_past.shape),
        jnp.broadcast_to(causal[None, :, None, None, :], s_new.shape)], axis=-1)
    p = _masked_softmax(jnp.concatenate([s_past, s_new], axis=-1) * SCALE, mask)
    n_past = s_past.shape[-1]
    o_sel = (jnp.einsum('btgrk,btgkd->btgrd', p[..., :n_past], kv[..., 1, :])
             + jnp.einsum('btgrs,bsgd->btgrd', p[..., n_past:], kv_sel_new[:, :, 1]))
    wb = cache_win.shape[1]
    kw = jnp.concatenate([cache_win, kv_win_new], axis=1)
    wpos = PAST_LEN - wb + jnp.arange(wb + T)
    wmask = (wpos[None, :] <= q_pos[:, None]) & (wpos[None, :] > q_pos[:, None] - WINDOW)
    s = jnp.einsum('btgrd,bkgd->btgrk', q, kw[:, :, 0]) * SCALE
    p = _masked_softmax(s, wmask[None, :, None, None, :])
    o_win = jnp.einsum('btgrk,bkgd->btgrd', p, kw[:, :, 1])
    return o_sel.astype(q.dtype), o_win.astype(q.dtype), kw[:, -wb:]


def _rglru(x_ext, h0, conv_w, conv_b, w_rg, b_rg, w_ig, b_ig, lru_lambda):
    B, L, _ = x_ext.shape
    T = L - (CONV_W - 1)
    xc = conv_b
    for k in range(CONV_W):
        xc = xc + x_ext[:, k:k + T] * conv_w[k]
    xf = xc.astype(jnp.float32)
    xb = xf.reshape(B, T, RNN_BLOCKS, RNN_BLOCK_DIM)
    r = jax.nn.sigmoid(jnp.einsum('btnd,nde->btne', xb, w_rg).reshape(B, T, D_RNN) + b_rg)
    i = jax.nn.sigmoid(jnp.einsum('btnd,nde->btne', xb, w_ig).reshape(B, T, D_RNN) + b_ig)
    log_a = -LRU_C * r * jax.nn.softplus(-lru_lambda.astype(jnp.float32))
    a = jnp.exp(log_a)
    u = jnp.sqrt(-jnp.expm1(2.0 * log_a)) * (i * xf)

    def step(h, au):
        h = au[0] * h + au[1]
        return h, h

    h_last, hs = lax.scan(step, h0.astype(jnp.float32), (jnp.swapaxes(a, 0, 1), jnp.swapaxes(u, 0, 1)))
    return jnp.swapaxes(hs, 0, 1).astype(x_ext.dtype), h_last.astype(x_ext.dtype), x_ext[:, T:]


def _merge_out(x, o_cmp, o_sel, o_win, g_nsa, g_att, h_rnn, g_rnn, gm_att, gm_rnn, w_att_out, w_rnn_out, w_out):
    B, T, _ = x.shape
    o_att = (g_nsa[:, :, 0] * o_cmp + g_nsa[:, :, 1] * o_sel + g_nsa[:, :, 2] * o_win).reshape(B, T, ATT_WIDTH)
    u_att = jnp.einsum('btc,cd->btd', o_att * jax.nn.silu(g_att), w_att_out)
    u_rnn = jnp.einsum('btc,cd->btd', h_rnn * jax.nn.silu(g_rnn), w_rnn_out)
    m = jax.nn.sigmoid(gm_att) * u_att + jax.nn.sigmoid(gm_rnn) * u_rnn
    return x + jnp.einsum('btd,de->bte', m, w_out)


def _prompt_layer(x, prm):
    (norm_g, w_in, q_norm_g, k_norm_g, w_cmp, conv_w, conv_b, w_rg, b_rg, w_ig, b_ig,
     lru_lambda, w_att_out, w_rnn_out, w_out) = prm
    B, T, _ = x.shape
    q, kv_cmp, kv_sel, kv_win, g_nsa, g_att, x_rnn, g_rnn, gm_att, gm_rnn = _project(x, norm_g, w_in, q_norm_g, k_norm_g)
    q_pos = jnp.arange(T)
    k_c, v_c = _compress(kv_cmp, w_cmp, k_norm_g[0])
    o_cmp, imp = _cmp_branch(q, k_c, v_c, q_pos)
    sel_idx, sel_valid = _select_blocks(imp, q_pos)
    o_sel, o_win = _sel_win_prompt(q, kv_sel, kv_win, sel_idx, sel_valid)
    x_ext = jnp.pad(x_rnn, ((0, 0), (CONV_W - 1, 0), (0, 0)))
    h_rnn, h_last, conv_state = _rglru(x_ext, jnp.zeros((B, D_RNN), x.dtype), conv_w, conv_b,
                                       w_rg, b_rg, w_ig, b_ig, lru_lambda)
    y = _merge_out(x, o_cmp, o_sel, o_win, g_nsa, g_att, h_rnn, g_rnn, gm_att, gm_rnn, w_att_out, w_rnn_out, w_out)
    return y, (kv_cmp, kv_sel, kv_win[:, -min(WINDOW, T):], h_last, conv_state)


def _sample_layer(x, cache_cmp, cache_sel, cache_win, state_h, state_conv, page_table, prm):
    (norm_g, w_in, q_norm_g, k_norm_g, w_cmp, conv_w, conv_b, w_rg, b_rg, w_ig, b_ig,
     lru_lambda, w_att_out, w_rnn_out, w_out) = prm
    DB, T, _ = x.shape
    q, kv_cmp, kv_sel, kv_win, g_nsa, g_att, x_rnn, g_rnn, gm_att, gm_rnn = _project(x, norm_g, w_in, q_norm_g, k_norm_g)
    q_pos = PAST_LEN + jnp.arange(T)
    past_cmp = cache_cmp[page_table].reshape(DB, -1, 2, KV_HEADS, HEAD_DIM)
    k_c, v_c = _compress(past_cmp, w_cmp, k_norm_g[0])
    o_cmp, imp = _cmp_branch(q, k_c, v_c, q_pos)
    sel_idx, sel_valid = _select_blocks(imp, q_pos)
    o_sel, o_win, win_state = _sel_win_sample(q, kv_sel, kv_win, sel_idx, sel_valid,
                                              cache_sel, page_table, cache_win, q_pos)
    x_ext = jnp.concatenate([state_conv, x_rnn], axis=1)
    h_rnn, h_last, conv_state = _rglru(x_ext, state_h, conv_w, conv_b, w_rg, b_rg, w_ig, b_ig, lru_lambda)
    y = _merge_out(x, o_cmp, o_sel, o_win, g_nsa, g_att, h_rnn, g_rnn, gm_att, gm_rnn, w_att_out, w_rnn_out, w_out)
    return y, (kv_cmp, kv_sel, win_state, h_last, conv_state)


def setup_inputs(seed: int = 0) -> dict:
    key = jax.random.key(seed)
    ks = jax.random.split(key, 24)
    f32 = jnp.float32
    n_pages = PAST_LEN // PAGE_SIZE
    n_used = DEC_BATCH * n_pages
    n_pool = (5 * n_used) // 4
    wbuf = min(WINDOW, PAST_LEN)

    def nrm(k, shape, scale):
        return scale * jax.random.normal(k, shape, f32)

    page_table = jax.random.permutation(ks[0], n_pool)[:n_used].reshape(DEC_BATCH, n_pages).astype(jnp.int32)
    a0 = jax.random.uniform(ks[1], (DEPTH, D_RNN), f32, 0.9, 0.999) ** (1.0 / LRU_C)
    lru_lambda = jnp.log(a0) - jnp.log1p(-a0)
    return {
        'x_prompt': nrm(ks[2], (BATCH, SEQ, D_MODEL), 1.0),
        'x_sample': nrm(ks[3], (DEC_BATCH, DEC_SEQ, D_MODEL), 1.0),
        'cache_cmp': nrm(ks[4], (DEPTH, n_pool, PAGE_SIZE, 2, KV_HEADS, HEAD_DIM), 1.0),
        'cache_sel': nrm(ks[5], (DEPTH, n_pool, PAGE_SIZE, 2, KV_HEADS, HEAD_DIM), 1.0),
        'cache_win': nrm(ks[6], (DEPTH, DEC_BATCH, wbuf, 2, KV_HEADS, HEAD_DIM), 1.0),
        'state_h': nrm(ks[7], (DEPTH, DEC_BATCH, D_RNN), 0.5),
        'state_conv': nrm(ks[8], (DEPTH, DEC_BATCH, CONV_W - 1, D_RNN), 1.0),
        'page_table': page_table,
        'norm_g': 1.0 + nrm(ks[9], (DEPTH, D_MODEL), 0.02),
        'w_in': nrm(ks[10], (DEPTH, D_MODEL, IN_COLS), D_MODEL ** -0.5),
        'q_norm_g': 1.0 + nrm(ks[11], (DEPTH, HEAD_DIM), 0.02),
        'k_norm_g': 1.0 + nrm(ks[12], (DEPTH, 3, HEAD_DIM), 0.02),
        'w_cmp': (1.0 + nrm(ks[13], (DEPTH, CMP_BLOCK, 2, HEAD_DIM), 0.1)) / CMP_BLOCK,
        'conv_w': nrm(ks[14], (DEPTH, CONV_W, D_RNN), CONV_W ** -0.5),
        'conv_b': nrm(ks[15], (DEPTH, D_RNN), 0.01),
        'w_rg': nrm(ks[16], (DEPTH, RNN_BLOCKS, RNN_BLOCK_DIM, RNN_BLOCK_DIM), RNN_BLOCK_DIM ** -0.5),
        'b_rg': nrm(ks[17], (DEPTH, D_RNN), 0.01),
        'w_ig': nrm(ks[18], (DEPTH, RNN_BLOCKS, RNN_BLOCK_DIM, RNN_BLOCK_DIM), RNN_BLOCK_DIM ** -0.5),
        'b_ig': nrm(ks[19], (DEPTH, D_RNN), 0.01),
        'lru_lambda': lru_lambda,
        'w_att_out': nrm(ks[20], (DEPTH, ATT_WIDTH, D_MODEL), ATT_WIDTH ** -0.5),
        'w_rnn_out': nrm(ks[21], (DEPTH, D_RNN, D_MODEL), D_RNN ** -0.5),
        'w_out': nrm(ks[22], (DEPTH, D_MODEL, D_MODEL), D_MODEL ** -0.5),
    }


def reference(x_prompt, x_sample, cache_cmp, cache_sel, cache_win, state_h, state_conv, page_table,
              norm_g, w_in, q_norm_g, k_norm_g, w_cmp, conv_w, conv_b, w_rg, b_rg, w_ig, b_ig,
              lru_lambda, w_att_out, w_rnn_out, w_out):
    yp, ys = x_prompt, x_sample
    outs_p, outs_s = [], []
    for l in range(DEPTH):
        prm = (norm_g[l], w_in[l], q_norm_g[l], k_norm_g[l], w_cmp[l], conv_w[l], conv_b[l], w_rg[l], b_rg[l],
               w_ig[l], b_ig[l], lru_lambda[l], w_att_out[l], w_rnn_out[l], w_out[l])
        yp, st_p = _prompt_layer(yp, prm)
        ys, st_s = _sample_layer(ys, cache_cmp[l], cache_sel[l], cache_win[l], state_h[l], state_conv[l],
                                 page_table, prm)
        outs_p.append(st_p)
        outs_s.append(st_s)
    cmp_p, sel_p, win_p, h_p, conv_p = [jnp.stack(a) for a in zip(*outs_p)]
    cmp_s, sel_s, win_s, h_s, conv_s = [jnp.stack(a) for a in zip(*outs_s)]
    return (yp, ys, cmp_p, sel_p, win_p, h_p, conv_p, cmp_s, sel_s, win_s, h_s, conv_s)
```

```python
import numpy as np
import ml_dtypes
from contextlib import ExitStack
import concourse.bass as bass
import concourse.mybir as mybir
from concourse.bass_utils import run_bass_kernel_spmd

F32 = mybir.dt.float32
BF16 = mybir.dt.bfloat16
I32 = mybir.dt.int32
AF = mybir.ActivationFunctionType
ALU = mybir.AluOpType
AX = mybir.AxisListType


class Res:
    __slots__ = ("name", "lw", "rd", "ap")

    def __init__(self, name, ap=None):
        self.name = name
        self.lw = None
        self.rd = []
        self.ap = ap


class _Rec:
    def __getattr__(self, name):
        def f(*a, **k):
            self.__dict__["call"] = (name, a, k)
            return None
        return f


class Prog:
    ENG = ("pe", "act", "dve", "pool", "sp")

    def __init__(self, nc):
        self.nc = nc
        self.ops = []
        self.es = ExitStack()
        self.nsem = 0

    SB_WORDS = 52736

    def _init_mem(self):
        self.sb_all = self.es.enter_context(self.nc.sbuf_tensor("sb_all", [128, self.SB_WORDS], F32))
        self.ps_all = self.es.enter_context(self.nc.psum_tensor("ps_all", [128, 4096], F32))
        self.sb_off = 0
        self.sb_peak = 0
        self.ps_off = 0

    def _carve(self, base, off, shape, dt):
        esz = 2 if dt == BF16 else 4
        n = int(np.prod(shape[1:]))
        words = (n * esz + 3) // 4
        words = (words + 7) // 8 * 8
        ap = base[0:shape[0], off:off + words]
        if dt != F32:
            ap = ap.bitcast(dt)
        ap = ap[:, 0:n]
        if len(shape) > 2:
            names = " ".join(f"d{i}" for i in range(1, len(shape)))
            kw = {f"d{i}": int(shape[i]) for i in range(1, len(shape))}
            ap = ap.rearrange(f"p ({names}) -> p {names}", **kw)
        return ap, words

    def sb(self, name, shape, dt):
        if not hasattr(self, "sb_all"):
            self._init_mem()
        ap, words = self._carve(self.sb_all, self.sb_off, shape, dt)
        self.sb_off += words
        self.sb_peak = max(self.sb_peak, self.sb_off)
        assert self.sb_off <= self.SB_WORDS, f"SBUF overflow allocating {name}: {self.sb_off * 4} bytes"
        return Res(name, ap)

    def ps(self, name, shape, dt, bank=None):
        if not hasattr(self, "sb_all"):
            self._init_mem()
        if bank is not None:
            ap, words = self._carve(self.ps_all, bank * 512, shape, dt)
            assert words <= 512
            return Res(name, ap)
        ap, words = self._carve(self.ps_all, self.ps_off, shape, dt)
        self.ps_off += (words + 511) // 512 * 512
        assert self.ps_off <= 4096, "PSUM overflow"
        return Res(name, ap)

    def scope(self):
        return _Scope(self)

    def bank(self, b, shape, dt, off=0):
        ap, words = self._carve(self.ps_all, b * 512 + off, shape, dt)
        assert off + words <= 512
        return ap

    def init_banks(self):
        if not hasattr(self, "sb_all"):
            self._init_mem()
        self.B = [Res(f"bank{b}") for b in range(8)]
        self.fz = self.sb("fz", [128, 16], BF16)
        self.fz2 = self.sb("fz2", [128, 16], F32)
        self.fence = {e: Res("fence_" + e) for e in self.ENG}
        self.dma_since = []
        self.op("pool", lambda e: e.memset(self.fz.ap, 0.0), writes=[self.fz])
        self.op("pool", lambda e: e.memset(self.fz2.ap, 0.0), writes=[self.fz2])

    def barrier(self):
        fz, fz2 = self.fz, self.fz2
        a = {}
        a["pe"] = self._add("pe", lambda e: e.matmul(self.bank(7, [1, 2], F32, off=496), fz.ap[0:1, 0:1], fz.ap[0:1, 2:4],
                                                     start=True, stop=True), [fz], [self.fence["pe"], self.B[7]])
        a["act"] = self._add("act", lambda e: e.copy(fz2.ap[0:1, 0:1], fz2.ap[0:1, 1:2]), [], [self.fence["act"]])
        a["dve"] = self._add("dve", lambda e: e.tensor_copy(fz2.ap[0:1, 2:3], fz2.ap[0:1, 3:4]), [], [self.fence["dve"]])
        a["pool"] = self._add("pool", lambda e: e.tensor_copy(fz2.ap[0:1, 4:5], fz2.ap[0:1, 5:6]), [], [self.fence["pool"]])
        extra = {d: "raw" for d in self.dma_since}
        self.dma_since = []
        fl = list(self.fence.values())
        for eng, fn in (("pe", lambda e: e.matmul(self.bank(7, [1, 2], F32, off=496), fz.ap[0:1, 0:1], fz.ap[0:1, 2:4], start=True, stop=True)),
                        ("act", lambda e: e.copy(fz2.ap[0:1, 6:7], fz2.ap[0:1, 7:8])),
                        ("dve", lambda e: e.tensor_copy(fz2.ap[0:1, 8:9], fz2.ap[0:1, 9:10])),
                        ("pool", lambda e: e.tensor_copy(fz2.ap[0:1, 10:11], fz2.ap[0:1, 11:12])),
                        ("sp", lambda e: e.nop())):
            i = self._add(eng, fn, [f for f in fl if f.lw is not None], [])
            self.ops[i]["deps"].update(extra)
            self.ops[i]["force"] = True
        for f in fl:
            f.rd = []

    def _add(self, eng, fn, reads, writes, dma=False, semkey=None, out=False):
        deps = {}
        for r in reads:
            if r.lw is not None:
                deps[r.lw] = "raw"
        for w in writes:
            if w.lw is not None:
                deps[w.lw] = "waw"
            for o in w.rd:
                deps.setdefault(o, "war")
        rec = _Rec()
        fn(rec)
        name_, a_, k_ = rec.call
        fn = lambda e: getattr(e, name_)(*a_, **k_)
        i = len(self.ops)
        for r in reads:
            r.rd.append(i)
        for w in writes:
            w.lw = i
            w.rd = []
        self.ops.append(dict(eng=eng, fn=fn, deps=deps, dma=dma, semkey=semkey, out=out, sig=dma, val=None))
        if dma and hasattr(self, "dma_since"):
            self.dma_since.append(i)
        return i

    def op(self, eng, fn, reads=(), writes=()):
        return self._add(eng, fn, list(reads), list(writes))

    def act(self, out, in_, func, reads=(), writes=(), **kw):
        return self._add("act", lambda e: e.activation(out, in_, func, **kw), list(reads), list(writes))

    def dma(self, q, dst, src, reads=(), writes=(), semkey=None, out=False, **kw):
        if semkey is None:
            semkey = ("w", id(writes[0])) if writes else ("r", id(reads[0]))
        return self._add(q, lambda e: e.dma_start(dst, src, **kw), list(reads), list(writes), dma=True,
                         semkey=semkey, out=out)

    def dma_fn(self, q, fn, reads=(), writes=(), semkey=None, out=False):
        if semkey is None:
            semkey = ("w", id(writes[0])) if writes else ("r", id(reads[0]))
        return self._add(q, fn, list(reads), list(writes), dma=True, semkey=semkey, out=out)

    def finish(self):
        nc = self.nc
        ops = self.ops
        for i, o in enumerate(ops):
            need = {}
            for d, kind in o["deps"].items():
                od = ops[d]
                if not od["dma"] and od["eng"] == o["eng"] and not o["dma"]:
                    if o["eng"] == "pe" or kind == "war" or o.get("force"):
                        continue
                need[d] = kind
                od["sig"] = True
            o["need"] = need
        semkeys = {}
        counts = {}
        for o in ops:
            if not o["sig"]:
                continue
            key = o["semkey"] if o["dma"] else ("eng", o["eng"])
            o["key"] = key
            if key not in semkeys:
                semkeys[key] = None
            counts[key] = counts.get(key, 0) + (16 if o["dma"] else 1)
            o["val"] = counts[key]
        assert len(semkeys) <= 100, f"too many semaphores {len(semkeys)}"
        for k in semkeys:
            semkeys[k] = self.es.enter_context(nc.semaphore(f"s{len([1 for v in semkeys.values() if v is not None])}"))
        self.nsem = len(semkeys)
        streams = {e: [] for e in self.ENG}
        for i, o in enumerate(ops):
            streams[o["eng"]].append(i)
        final_waits = {}
        for o in ops:
            if o["dma"] and o["out"]:
                final_waits[o["key"]] = (o["eng"], counts[o["key"]])

        def emit(engname, eng):
            waited = {}
            for i in streams[engname]:
                o = ops[i]
                wl = {}
                for d in o["need"]:
                    od = ops[d]
                    k = od["key"]
                    if od["val"] > wl.get(k, 0):
                        wl[k] = od["val"]
                for k, v in wl.items():
                    if waited.get(k, 0) >= v:
                        continue
                    eng.wait_ge(semkeys[k], v)
                    waited[k] = v
                ins = o["fn"](eng)
                if o["sig"]:
                    ins.then_inc(semkeys[o["key"]], 16 if o["dma"] else 1)
            for k, (qe, v) in final_waits.items():
                if qe == engname and waited.get(k, 0) < v:
                    eng.wait_ge(semkeys[k], v)

        with nc.Block() as block:
            @block.tensor
            def _(e):
                emit("pe", e)

            @block.scalar
            def _(e):
                emit("act", e)

            @block.vector
            def _(e):
                emit("dve", e)

            @block.gpsimd
            def _(e):
                emit("pool", e)

            @block.sync
            def _(e):
                emit("sp", e)
        self.es.close()


class _Scope:
    def __init__(self, P):
        self.P = P

    def __enter__(self):
        if not hasattr(self.P, "sb_all"):
            self.P._init_mem()
        self.saved = self.P.sb_off
        return self

    def __exit__(self, *a):
        self.P.sb_off = self.saved
        if a[0] is None:
            self.P.barrier()
        return False


D = 2048
NH, HD, G = 16, 64, 4
DR = 1024
EPS = 1e-6
SCALE = HD ** -0.5
NBLK = 32
NS = 32
C_Q, C_KV, C_GN, C_GA, C_XR, C_GR, C_MA, C_MR = 0, 1024, 2560, 2608, 3632, 4656, 5680, 7728
NEGB = -30000.0


class Ctx:
    pass


def dram_in(nc, name, shape, dt=F32):
    return nc.dram_tensor(name, list(shape), dt, kind="ExternalInput").ap()


def dram_out(nc, name, shape, dt=F32):
    return nc.dram_tensor(name, list(shape), dt, kind="ExternalOutput").ap()


def bc(ap, shape, axis):
    return ap.unsqueeze(axis).broadcast_to(list(shape))


def norm_T(P, c, blk, n, keep_x=False):
    t = c.nt[blk % 2]
    xt, xn, ss, xnT = c.nt[0]["xt"], t["xn"], t["ss"], t["xnT"]
    P.dma("sp", xt.ap[0:n, :], c.xv[blk * 128:blk * 128 + n, :], writes=[xt])
    P.act(xn.ap[0:n, :], xt.ap[0:n, :], AF.Square, accum_out=ss.ap[0:n, 0:1], reads=[xt], writes=[xn, ss])
    P.op("dve", lambda e: e.tensor_scalar(ss.ap[0:n, 1:2], ss.ap[0:n, 0:1], 1.0 / D, EPS, ALU.mult, ALU.add), [ss], [ss])
    P.act(ss.ap[0:n, 2:3], ss.ap[0:n, 1:2], AF.Sqrt, reads=[ss], writes=[ss])
    P.op("dve", lambda e: e.reciprocal(ss.ap[0:n, 3:4], ss.ap[0:n, 2:3]), [ss], [ss])
    xs = t["xs"] if keep_x else xt
    P.act(xs.ap[0:n, :], xt.ap[0:n, :], AF.Copy, scale=ss.ap[0:n, 3:4], reads=[xt, ss], writes=[xs])
    P.op("dve", lambda e: e.tensor_tensor(xn.ap[0:n, :], xs.ap[0:n, :], c.g_rep.ap[0:n, :], ALU.mult), [xs, c.g_rep], [xn])
    for h in range(2):
        bk = P.B[h]
        for k in range(8):
            kk = h * 8 + k
            P.op("pe", lambda e, kk=kk, k=k, h=h: e.transpose(P.bank(h, [128, 8, 128], BF16)[:, k, 0:n],
                                                          xn.ap[0:n, kk * 128:(kk + 1) * 128], c.ident.ap[0:n, 0:n]),
                 [xn, c.ident], [bk])
        eng = "act" if h == 0 else "dve"
        if eng == "act":
            P.op("act", lambda e, h=h: e.copy(xnT.ap[:, h * 8:(h + 1) * 8, 0:n], P.bank(h, [128, 8, 128], BF16)[:, :, 0:n]), [bk], [xnT])
        else:
            P.op("dve", lambda e, h=h: e.tensor_copy(xnT.ap[:, h * 8:(h + 1) * 8, 0:n], P.bank(h, [128, 8, 128], BF16)[:, :, 0:n]), [bk], [xnT])
    return xnT, xt


def rms_rows(P, c, src_ap, n, ng, gain_ap, dst_ap, tmp, reads, writes):
    sq, st = tmp["sq"], tmp["st"]
    P.act(sq.ap[0:n, 0:ng * 64], src_ap, AF.Square, reads=reads, writes=[sq])
    P.op("dve", lambda e: e.tensor_reduce(st.ap[0:n, 0:ng], sq.ap[0:n, 0:ng * 64].rearrange("p (g d) -> p g d", g=ng), AX.X, ALU.add), [sq], [st])
    P.op("dve", lambda e: e.tensor_scalar(st.ap[0:n, 16:16 + ng], st.ap[0:n, 0:ng], 1.0 / HD, EPS, ALU.mult, ALU.add), [st], [st])
    P.act(st.ap[0:n, 32:32 + ng], st.ap[0:n, 16:16 + ng], AF.Sqrt, reads=[st], writes=[st])
    P.op("dve", lambda e: e.reciprocal(st.ap[0:n, 48:48 + ng], st.ap[0:n, 32:32 + ng]), [st], [st])
    s3 = src_ap.rearrange("p (g d) -> p g d", g=ng)
    d3 = dst_ap.rearrange("p (g d) -> p g d", g=ng)
    P.op("dve", lambda e: e.tensor_tensor(d3, s3, bc(st.ap[0:n, 48:48 + ng], [n, ng, 64], 2), ALU.mult), list(reads) + [st], list(writes))
    P.op("dve", lambda e: e.tensor_tensor(d3, d3, bc(gain_ap[0:n, :], [n, ng, 64], 1), ALU.mult), list(writes) + [c.kg_rep, c.qg_rep], list(writes))


def pass_k1(P, c):
    nc = P.nc
    with P.scope():
        wkv = P.sb("wkv", [128, 16, 1536], BF16)
        for k in range(16):
            P.dma("pool", wkv.ap[:, k, :], c.w_in[k * 128:(k + 1) * 128, C_KV:C_KV + 1536], writes=[wkv], semkey="wload")
        kvc = [P.sb(f"kvc{i}", [128, 512], F32) for i in range(1)] * 2
        kvw = [P.sb(f"kvw{i}", [128, 512], F32) for i in range(1)] * 2
        rows = [[P.sb(f"rows{b}", [128, 512], F32)] for b in range(2)]
        knb2 = [[P.sb(f"knb{b}{i}", [128, 2, 2, 64], BF16) for i in range(2)] for b in range(2)]
        tmp = dict(sq=P.sb("k1sq", [128, 256], F32), st=P.sb("k1st", [128, 64], F32))
        nxt = norm_T(P, c, 0, 128)
        for blk in range(NBLK + 1):
            n = 128 if blk < NBLK else NS
            own = (blk % 4 == 3) and blk < NBLK
            smp = blk == NBLK
            xnT, _ = nxt
            if blk + 1 <= NBLK:
                nxt = norm_T(P, c, blk + 1, 128 if blk + 1 < NBLK else NS)
            for br in range(3):
                bk = P.B[2 + br]
                for k in range(16):
                    P.op("pe", lambda e, k=k, br=br: e.matmul(P.bank(2 + br, [128, 512], F32)[0:n, :], xnT.ap[:, k, 0:n],
                                                            wkv.ap[:, k, br * 512:(br + 1) * 512], start=(k == 0), stop=(k == 15)),
                         [xnT, wkv], [bk])
            kc = kvc[blk % 2]
            P.op("act", lambda e: e.copy(kc.ap[0:n, :], P.bank(2, [128, 512], F32)[0:n, :]), [P.B[2]], [kc])
            if own:
                P.dma("pool", c.o_cmp[blk // 4], kc.ap, reads=[kc], semkey="ocmp", out=True)
            if smp:
                P.dma("pool", c.s_cmp, kc.ap[0:n, :], reads=[kc], semkey="ocmp", out=True)
            else:
                kw = kvw[blk % 2]
                P.op("dve", lambda e: e.tensor_tensor(kw.ap, kc.ap, c.wc_rep.ap, ALU.mult), [kc, c.wc_rep], [kw])
                P.op("pe", lambda e, blk=blk: e.matmul(P.bank(5, [128, 512], F32), c.indw.ap[:, 124 - 4 * blk:252 - 4 * blk], kw.ap,
                                                      start=(blk == 0), stop=(blk == NBLK - 1)), [c.indw, kw], [P.B[5]])
            for b, br in ((0, 1), (1, 2)):
                bkr = P.B[2 + br]
                bap = P.bank(2 + br, [128, 512], F32)
                rw = rows[b][0]
                rms_rows(P, c, bap[0:n, 0:256], n, 4, c.kg_rep.ap[:, br, :], rw.ap[0:n, 0:256], tmp, [bkr], [rw])
                P.op("act", lambda e, rw=rw, bap=bap: e.copy(rw.ap[0:n, 256:512], bap[0:n, 256:512]), [bkr], [rw])
                if b == 0:
                    if own:
                        P.dma("pool", c.o_sel[blk // 4], rw.ap, reads=[rw], semkey="osel", out=True)
                    if smp:
                        P.dma("pool", c.s_sel, rw.ap[0:n, :], reads=[rw], writes=[c.r_ssel], semkey="osel", out=True)
                else:
                    if blk == NBLK - 1:
                        P.dma("pool", c.o_win, rw.ap, reads=[rw], semkey="owin", out=True)
                    if smp:
                        P.dma("pool", c.s_win, rw.ap[0:n, :], reads=[rw], writes=[c.r_swin], semkey="owin", out=True)
                kb = knb2[b][blk % 2]
                P.op("pool", lambda e, kb=kb, rw=rw: e.tensor_copy(kb.ap[0:n].rearrange("p gi hf d -> p hf gi d"),
                                                                 rw.ap[0:n, 0:256].rearrange("p (hf gi d) -> p hf gi d", hf=2, gi=2)), [rw], [kb])
                V1 = c.Vsel1 if b == 0 else c.Vwin1
                P.op("pool", lambda e, V1=V1, rw=rw, blk=blk: e.tensor_copy(V1.ap[0:n, blk, :, 0:64], rw.ap[0:n, 256:512].rearrange("p (g d) -> p g d", g=4)), [rw], [V1])
                for gi in range(2):
                    tb = 6 + gi
                    P.op("pe", lambda e, kb=kb, gi=gi, tb=tb: e.transpose(P.bank(tb, [128, 128], BF16)[:, 0:n], kb.ap[0:n, gi].rearrange("p hf d -> p (hf d)"),
                                                                     c.ident.ap[0:n, 0:n]), [kb, c.ident], [P.B[tb]])
                    cols = slice(blk * 128, blk * 128 + n)
                    if b == 0:
                        P.op("act", lambda e, gi=gi, tb=tb, cols=cols: e.copy(c.Ksel.ap[0:64, gi, cols], P.bank(tb, [128, 128], BF16)[0:64, 0:n]), [P.B[tb]], [c.Ksel])
                        P.op("dve", lambda e, gi=gi, tb=tb, cols=cols: e.tensor_copy(c.Ksel.ap[64:128, 2 + gi, cols], P.bank(tb, [128, 128], BF16)[64:128, 0:n]), [P.B[tb]], [c.Ksel])
                    else:
                        P.op("act", lambda e, gi=gi, tb=tb, cols=cols: e.copy(c.Kwin.ap[:, gi, cols], P.bank(tb, [128, 128], BF16)[:, 0:n]), [P.B[tb]], [c.Kwin])
        kcr = rows[0][0]
        P.op("act", lambda e: e.copy(kcr.ap, P.bank(5, [128, 512], F32)), [P.B[5]], [kcr])
        kcn = rows[1][0]
        rms_rows(P, c, kcr.ap[:, 0:256], 128, 4, c.kg_rep.ap[:, 0, :], kcn.ap[:, 0:256], tmp, [kcr], [kcn])
        kb = knb2[0][0]
        P.op("pool", lambda e: e.tensor_copy(kb.ap.rearrange("p gi hf d -> p hf gi d"), kcn.ap[:, 0:256].rearrange("p (hf gi d) -> p hf gi d", hf=2, gi=2)), [kcn], [kb])
        P.op("pool", lambda e: e.tensor_copy(c.Vc1.ap[:, :, 0:64], kcr.ap[:, 256:512].rearrange("p (g d) -> p g d", g=4)), [kcr], [c.Vc1])
        for gi in range(2):
            tb = 6 + gi
            P.op("pe", lambda e, gi=gi, tb=tb: e.transpose(P.bank(tb, [128, 128], BF16), kb.ap[:, gi].rearrange("p hf d -> p (hf d)"), c.ident.ap), [kb, c.ident], [P.B[tb]])
            P.op("act", lambda e, gi=gi, tb=tb: e.copy(c.KcT.ap[:, gi, :], P.bank(tb, [128, 128], BF16)), [P.B[tb]], [c.KcT])
    P.barrier()


def bcn(ap, shape):
    a = ap
    for ax in range(2, len(shape)):
        a = a.unsqueeze(ax)
    return a.broadcast_to(list(shape))


def pass_r(P, c):
    with P.scope():
        wxr = P.sb("wxr", [128, 16, 1024], BF16)
        for k in range(16):
            P.dma("pool", wxr.ap[:, k, :], c.w_in[k * 128:(k + 1) * 128, C_XR:C_XR + 1024], writes=[wxr], semkey="wload")
        Wg = [P.sb("Wrg", [128, 8, 128], BF16), P.sb("Wig", [128, 8, 128], BF16)]
        P.dma("pool", Wg[0].ap, c.d_wrg, writes=[Wg[0]])
        P.dma("pool", Wg[1].ap, c.d_wig, writes=[Wg[1]])
        cw = P.sb("cw", [128, 8, 4], F32)
        sm = P.sb("rsm", [128, 5, 8], F32)
        vrow = P.sb("vrow", [128, 384], F32)
        h0T = P.sb("h0T", [128, 8, 4], F32)
        XRs = P.sb("XRs", [128, 8, 4, 11], F32)
        XRb = [P.sb(f"XR{i}", [128, 8, 131], F32) for i in range(2)]
        ST = P.sb("ST", [128, 8, 16], F32)
        rowsT = P.sb("rowsT", [16, 1024], F32)
        identf = c.identf
        P.dma("sp", cw.ap, c.d_cw, writes=[cw])
        P.dma("sp", sm.ap[:, 0:4, :], c.d_rsm, writes=[sm])
        P.dma("sp", vrow.ap, c.d_vrow, writes=[vrow])
        P.dma("sp", h0T.ap, c.d_h0T, writes=[h0T])
        P.dma("sp", XRs.ap[:, :, :, 0:3], c.d_sconvT, writes=[XRs])
        for XR_ in XRb:
            P.op("pool", lambda e: e.memset(XR_.ap, 0.0), [], [XR_])
        P.act(sm.ap[:, 4, :], sm.ap[:, 3, :], AF.Exp, scale=-1.0, reads=[sm], writes=[sm])
        P.op("dve", lambda e: e.tensor_scalar_add(sm.ap[:, 4, :], sm.ap[:, 4, :], 1.0), [sm], [sm])
        P.act(sm.ap[:, 4, :], sm.ap[:, 4, :], AF.Ln, reads=[sm], writes=[sm])
        P.op("dve", lambda e: e.tensor_scalar_mul(sm.ap[:, 4, :], sm.ap[:, 4, :], -8.0), [sm], [sm])
        xc = P.sb("xc", [128, 8, 128], F32)
        xcb = P.sb("xcb", [128, 8, 128], BF16)
        r = P.sb("rr", [128, 8, 128], F32)
        ig = P.sb("ig", [128, 8, 128], F32)
        a = P.sb("aa", [128, 8, 128], F32)
        t1 = P.sb("t1", [128, 8, 128], F32)
        hs = [P.sb(f"hs{i}", [128, 8, 128], F32) for i in range(2)]
        def stage_a(blk, xnT):
            smp = blk == NBLK
            n = NS if smp else 128
            XR = XRb[blk % 2]
            for cch in range(8):
                bk = 2 + cch // 4
                for k in range(16):
                    P.op("pe", lambda e: e.matmul(P.bank(bk, [128, 4, 128], F32)[:, cch % 4, 0:n], wxr.ap[:, k, cch * 128:(cch + 1) * 128],
                                                  xnT.ap[:, k, 0:n], start=(k == 0), stop=(k == 15)), [wxr, xnT], [P.B[bk]])
            if not smp:
                if blk > 0:
                    XRp = XRb[(blk + 1) % 2]
                    P.op("pool", lambda e: e.tensor_copy(XR.ap[:, :, 0:3], XRp.ap[:, :, 128:131]), [XRp], [XR])
                for h2 in range(2):
                    P.op("act", lambda e: e.copy(XR.ap[:, 4 * h2:4 * h2 + 4, 3:131], P.bank(2 + h2, [128, 4, 128], F32)), [P.B[2 + h2]], [XR])
            else:
                for h2 in range(2):
                    P.op("act", lambda e: e.copy(XRs.ap[:, 4 * h2:4 * h2 + 4, :, 3:11],
                                                 P.bank(2 + h2, [128, 4, 128], F32)[:, :, 0:n].rearrange("p c (b t) -> p c b t", b=4)), [P.B[2 + h2]], [XRs])

        def stage_b(blk):
            smp = blk == NBLK
            n = NS if smp else 128
            XR = XRb[blk % 2]
            if not smp:
                src = lambda k: XR.ap[:, :, k:k + 128]
                shp = [128, 8, 128]
                vw = lambda T: T.ap
                XRc = XR
            else:
                src = lambda k: XRs.ap[:, :, :, k:k + 8]
                shp = [128, 8, 4, 8]
                vw = lambda T: T.ap[:, :, 0:n].rearrange("p c (b t) -> p c b t", b=4)
                XRc = XRs
            P.op("pool", lambda e: e.tensor_tensor(vw(xc), src(3), bcn(cw.ap[:, :, 3], shp), ALU.mult), [XRc, cw], [xc])
            for k in (2, 1, 0):
                P.op("pool", lambda e: e.tensor_tensor(vw(t1), src(k), bcn(cw.ap[:, :, k], shp), ALU.mult), [XRc, cw], [t1])
                P.op("dve", lambda e: e.tensor_tensor(vw(xc), vw(xc), vw(t1), ALU.add), [xc, t1], [xc])
            P.op("dve", lambda e: e.tensor_tensor(vw(xc), vw(xc), bcn(sm.ap[:, 0, :], shp), ALU.add), [xc, sm], [xc])
            P.op("act", lambda e: e.copy(xcb.ap[:, :, 0:n], xc.ap[:, :, 0:n]), [xc], [xcb])
            for gi_ in range(2):
                for cch in range(8):
                    bk = 4 + 2 * gi_ + cch // 4
                    P.op("pe", lambda e: e.matmul(P.bank(bk, [128, 4, 128], F32)[:, cch % 4, 0:n], Wg[gi_].ap[:, cch, :], xcb.ap[:, cch, 0:n],
                                                  start=True, stop=True), [Wg[gi_], xcb], [P.B[bk]])
                dst = r if gi_ == 0 else ig
                for cch in range(8):
                    bk = 4 + 2 * gi_ + cch // 4
                    P.act(dst.ap[:, cch, 0:n], P.bank(bk, [128, 4, 128], F32)[:, cch % 4, 0:n], AF.Sigmoid, bias=sm.ap[:, 1 + gi_, cch:cch + 1],
                          reads=[P.B[bk], sm], writes=[dst])
            P.op("dve", lambda e: e.tensor_tensor(a.ap[:, :, 0:n], r.ap[:, :, 0:n], bcn(sm.ap[:, 4, :], [128, 8, n]), ALU.mult), [r, sm], [a])
            P.act(a.ap[:, :, 0:n], a.ap[:, :, 0:n], AF.Exp, reads=[a], writes=[a])
            P.op("pool", lambda e: e.tensor_tensor(t1.ap[:, :, 0:n], a.ap[:, :, 0:n], a.ap[:, :, 0:n], ALU.mult), [a], [t1])
            P.op("dve", lambda e: e.tensor_scalar(t1.ap[:, :, 0:n], t1.ap[:, :, 0:n], -1.0, 1.0, ALU.mult, ALU.add), [t1], [t1])
            P.act(t1.ap[:, :, 0:n], t1.ap[:, :, 0:n], AF.Sqrt, reads=[t1], writes=[t1])
            P.op("pool", lambda e: e.tensor_tensor(ig.ap[:, :, 0:n], ig.ap[:, :, 0:n], xc.ap[:, :, 0:n], ALU.mult), [ig, xc], [ig])
            P.op("dve", lambda e: e.tensor_tensor(t1.ap[:, :, 0:n], t1.ap[:, :, 0:n], ig.ap[:, :, 0:n], ALU.mult), [t1, ig], [t1])
            if blk < 3:
                P.op("dve", lambda e: e.tensor_tensor(t1.ap, t1.ap, bc(vrow.ap[:, blk * 128:(blk + 1) * 128], [128, 8, 128], 1), ALU.mult), [t1, vrow], [t1])
            hc, hp = hs[blk % 2], hs[(blk + 1) % 2]
            if not smp:
                for cch in range(8):
                    init = 0.0 if blk == 0 else hp.ap[:, cch, 127:128]
                    P.op("dve", lambda e: e.tensor_tensor_scan(hc.ap[:, cch, :], a.ap[:, cch, :], t1.ap[:, cch, :], init, ALU.mult, ALU.add),
                         [a, t1, hp], [hc])
                if blk % 4 == 3:
                    i = blk // 4
                    P.op("pool", lambda e: e.tensor_copy(c.hs_own.ap[:, :, i * 128:(i + 1) * 128], hc.ap), [hc], [c.hs_own])
                if blk == NBLK - 1:
                    P.op("pool", lambda e: e.tensor_copy(ST.ap[:, :, 0:1], hc.ap[:, :, 127:128]), [hc], [ST])
                    P.op("pool", lambda e: e.tensor_copy(ST.ap[:, :, 1:4], XR.ap[:, :, 128:131]), [XR], [ST])
                    for cch in range(8):
                        P.op("pe", lambda e: e.transpose(P.bank(2 + cch // 4, [16, 4, 128], F32)[0:4, cch % 4, :], ST.ap[:, cch, 0:4], identf.ap), [ST, identf], [P.B[2 + cch // 4]])
                    for h2 in range(2):
                        P.op("act", lambda e: e.copy(rowsT.ap[0:4, h2 * 512:(h2 + 1) * 512], P.bank(2 + h2, [16, 512], F32)[0:4, :]), [P.B[2 + h2]], [rowsT])
                    P.dma("pool", c.o_hp, rowsT.ap[0:1, :], reads=[rowsT], semkey="orn", out=True)
                    P.dma("pool", c.o_convp, rowsT.ap[1:4, :], reads=[rowsT], semkey="orn", out=True)
            else:
                for cch in range(8):
                    for b in range(4):
                        P.op("dve", lambda e: e.tensor_tensor_scan(hc.ap[:, cch, b * 8:(b + 1) * 8], a.ap[:, cch, b * 8:(b + 1) * 8],
                                                                   t1.ap[:, cch, b * 8:(b + 1) * 8], h0T.ap[:, cch, b:b + 1], ALU.mult, ALU.add),
                             [a, t1, h0T], [hc])
                P.op("pool", lambda e: e.tensor_copy(c.hs_own.ap[:, :, 1024:1024 + NS], hc.ap[:, :, 0:NS]), [hc], [c.hs_own])
                P.op("pool", lambda e: e.tensor_copy(ST.ap[:, :, 0:4], hc.ap[:, :, 0:NS].rearrange("p c (b t) -> p c b t", b=4)[:, :, :, 7]), [hc], [ST])
                P.op("pool", lambda e: e.tensor_copy(ST.ap[:, :, 4:16].rearrange("p c (b t) -> p c b t", b=4), XRs.ap[:, :, :, 8:11]), [XRs], [ST])
                for cch in range(8):
                    P.op("pe", lambda e: e.transpose(P.bank(2 + cch // 4, [16, 4, 128], F32)[0:16, cch % 4, :], ST.ap[:, cch, 0:16], identf.ap), [ST, identf], [P.B[2 + cch // 4]])
                for h2 in range(2):
                    P.op("act", lambda e: e.copy(rowsT.ap[0:16, h2 * 512:(h2 + 1) * 512], P.bank(2 + h2, [16, 512], F32)[0:16, :]), [P.B[2 + h2]], [rowsT])
                P.dma("pool", c.o_hs, rowsT.ap[0:4, :], reads=[rowsT], semkey="orn", out=True)
                P.dma("pool", c.o_convs.rearrange("b t c -> (b t) c"), rowsT.ap[4:16, :], reads=[rowsT], semkey="orn", out=True)

        nxt = norm_T(P, c, 0, 128)
        for blk in range(NBLK + 1):
            cur = nxt
            if blk + 1 <= NBLK:
                nxt = norm_T(P, c, blk + 1, 128 if blk + 1 < NBLK else NS)
            stage_a(blk, cur[0])
            if blk >= 1:
                stage_b(blk - 1)
        stage_b(NBLK)
    P.barrier()


def qproj(P, c, blk, n, wqs, qf, gts, tmp, resident=False):
    xnT, _ = norm_T(P, c, blk, n)
    for part, (c0, ncol, bk, r0) in enumerate(((C_Q, 512, 2, 0), (C_Q + 512, 512, 3, 512), (C_GN, 48, 4, 1024))):
        if not resident:
            r0 = 0
            for k in range(16):
                P.dma("pool", wqs.ap[:, k, 0:ncol], c.w_in[k * 128:(k + 1) * 128, c0:c0 + ncol], writes=[wqs], semkey="wload")
        for k in range(16):
            P.op("pe", lambda e: e.matmul(P.bank(bk, [128, 512], F32)[0:n, 0:ncol], xnT.ap[:, k, 0:n], wqs.ap[:, k, r0:r0 + ncol],
                                          start=(k == 0), stop=(k == 15)), [xnT, wqs], [P.B[bk]])
    for hh in range(2):
        rms_rows(P, c, P.bank(2 + hh, [128, 512], F32)[0:n, :], n, 8, c.qg_rep.ap, qf.ap[0:n, hh * 512:(hh + 1) * 512], tmp, [P.B[2 + hh]], [qf])
    P.act(gts.ap[0:n, :], P.bank(4, [128, 512], F32)[0:n, 0:48], AF.Sigmoid, reads=[P.B[4]], writes=[gts])


def combine(P, c, oT_bank, g, br, gts, oatt, tmp, n=128, first=False, tb=7):
    oTs, cf, otmp = tmp["oTs"], tmp["cf"], tmp["otmp"]
    W = 4 * n
    P.op("act", lambda e: e.copy(oTs.ap[0:65, 0:W], P.bank(oT_bank, [128, 512], F32)[0:65, 0:W]), [P.B[oT_bank]], [oTs])
    for r in range(4):
        P.op("pe", lambda e: e.transpose(P.bank(tb, [128, 4, 65], F32)[0:n, r, :], oTs.ap[0:65, r * n:(r + 1) * n], c.identf.ap[0:65, 0:65]),
             [oTs, c.identf], [P.B[tb]])
    O = P.bank(tb, [128, 4, 65], F32)
    P.op("dve", lambda e: e.tensor_scalar_max(cf.ap[0:n, 0:4], O[0:n, :, 64], 1e-30), [P.B[tb]], [cf])
    P.op("dve", lambda e: e.reciprocal(cf.ap[0:n, 4:8], cf.ap[0:n, 0:4]), [cf], [cf])
    P.op("dve", lambda e: e.tensor_tensor(cf.ap[0:n, 8:12], cf.ap[0:n, 4:8], gts.ap[0:n, br * 16 + g * 4:br * 16 + g * 4 + 4], ALU.mult), [cf, gts], [cf])
    dst = oatt.ap[0:n, g * 256:(g + 1) * 256].rearrange("p (r d) -> p r d", r=4)
    if first:
        P.op("dve", lambda e: e.tensor_tensor(dst, O[0:n, :, 0:64], bc(cf.ap[0:n, 8:12], [n, 4, 64], 2), ALU.mult), [P.B[tb], cf], [oatt])
    else:
        ot = otmp.ap[0:n, :].rearrange("p (r d) -> p r d", r=4)
        P.op("dve", lambda e: e.tensor_tensor(ot, O[0:n, :, 0:64], bc(cf.ap[0:n, 8:12], [n, 4, 64], 2), ALU.mult), [P.B[tb], cf], [otmp])
        P.op("pool", lambda e: e.tensor_tensor(dst, dst, ot, ALU.add), [oatt, otmp], [oatt])


def pass_k2(P, c):
    with P.scope():
        wqs = P.sb("wqs", [128, 16, 1072], BF16)
        for k in range(16):
            P.dma("pool", wqs.ap[:, k, 0:1024], c.w_in[k * 128:(k + 1) * 128, C_Q:C_Q + 1024], writes=[wqs], semkey="wload")
            P.dma("pool", wqs.ap[:, k, 1024:1072], c.w_in[k * 128:(k + 1) * 128, C_GN:C_GN + 48], writes=[wqs], semkey="wload")
        qf = P.sb("qf", [128, 1024], F32)
        gts = P.sb("gts", [128, 48], F32)
        tmp = dict(sq=P.sb("k2sq", [128, 512], F32), st=P.sb("k2st", [128, 64], F32), oTs=P.sb("oTs", [128, 512], F32),
                   cf=P.sb("cf", [128, 16], F32), otmp=P.sb("otmp", [128, 256], F32))
        QTin = P.sb("QTin", [128, 8, 128], BF16)
        BTin = P.sb("BTin", [128, 2, 128], BF16)
        QE = P.sb("QE", [128, 4, 4, 128], BF16)
        Ecm = tmp["sq"]
        Ecm3 = tmp["sq"].ap.rearrange("p (r n) -> p r n", r=4)
        impn = P.sb("impn", [128, 128], F32)
        sc = P.sb("sc", [128, 4, 64], F32)
        sc2 = P.sb("sc2", [128, 4, 64], F32)
        m8 = P.sb("m8", [128, 4, 24], F32)
        PT = [P.sb(f"PT{i}", [128, 512], BF16) for i in range(2)]
        oatt = P.sb("oatt", [128, 1024], F32)
        cmask = P.sb("cmask", [128, 8, 128], BF16)
        cmaskT = P.sb("cmaskT", [128, 8, 128], BF16)
        candm = P.sb("candm", [128, 8, 64], F32)
        addm = P.sb("addm", [128, 8, 64], F32)
        CB = P.sb("CB", [128, 2, 512], BF16)
        for dst, src in ((cmask, c.d_cmask), (cmaskT, c.d_cmaskT), (candm, c.d_candm), (addm, c.d_addm), (CB, c.d_CB)):
            P.dma("sp", dst.ap, src, writes=[dst])
        for i in range(8):
            v = 4 * i + 3
            n = 128
            qproj(P, c, v, n, wqs, qf, gts, tmp, resident=True)
            q4 = qf.ap.rearrange("p (hf gi r d) -> p hf gi r d", hf=2, gi=2, r=4)
            P.op("pool", lambda e: e.tensor_copy(QTin.ap.rearrange("p (gi r) (hf d) -> p hf gi r d", gi=2, hf=2), q4), [qf], [QTin])
            for t8 in range(8):
                P.op("pe", lambda e: e.transpose(P.bank(0, [128, 8, 128], BF16)[:, t8, :], QTin.ap[:, t8, :], c.ident.ap), [QTin, c.ident], [P.B[0]])
            QT = P.bank(0, [128, 8, 128], BF16)
            for gi in range(2):
                P.op("act", lambda e: e.copy(QE.ap[0:64, gi, :, :], QT[0:64, gi * 4:(gi + 1) * 4, :]), [P.B[0]], [QE])
                P.op("dve", lambda e: e.tensor_copy(QE.ap[64:128, 2 + gi, :, :], QT[64:128, gi * 4:(gi + 1) * 4, :]), [P.B[0]], [QE])
            for g in range(4):
                hf, gi = g // 2, g % 2
                hs_ = slice(hf * 64, hf * 64 + 64)
                for r in range(4):
                    P.op("pe", lambda e: e.matmul(P.bank(5, [128, 4, 128], F32)[:, r, :], QE.ap[hs_, g, r, :], c.KcT.ap[hs_, gi, :], start=True, stop=True),
                         [QE, c.KcT], [P.B[5]])
                P.act(Ecm3, P.bank(5, [128, 4, 128], F32), AF.Exp, scale=SCALE, reads=[P.B[5]], writes=[Ecm])
                P.op("dve", lambda e: e.tensor_tensor(Ecm3, Ecm3, bc(cmask.ap[:, i, :], [128, 4, 128], 1), ALU.mult), [Ecm, cmask], [Ecm])
                P.op("dve", lambda e: e.tensor_reduce(m8.ap[:, 0, 16:20], Ecm3, AX.X, ALU.add), [Ecm], [m8])
                P.op("dve", lambda e: e.tensor_scalar_max(m8.ap[:, 0, 16:20], m8.ap[:, 0, 16:20], 1e-30), [m8], [m8])
                P.op("dve", lambda e: e.reciprocal(m8.ap[:, 0, 20:24], m8.ap[:, 0, 16:20]), [m8], [m8])
                P.op("dve", lambda e: e.tensor_tensor(Ecm3, Ecm3, bc(m8.ap[:, 0, 20:24], [128, 4, 128], 2), ALU.mult), [Ecm, m8], [Ecm])
                P.op("dve", lambda e: e.tensor_reduce(impn.ap, Ecm3.rearrange("p r n -> p n r"), AX.X, ALU.add), [Ecm], [impn])
                P.op("dve", lambda e: e.tensor_reduce(sc.ap[:, g, :], impn.ap.rearrange("p (j two) -> p j two", two=2), AX.X, ALU.add), [impn], [sc])
                sb_ = 2 + g % 2
                P.op("pe", lambda e: e.matmul(P.bank(sb_, [128, 512], F32), c.KcT.ap[hs_, gi, :], QE.ap[hs_, g, :, :].rearrange("p r q -> p (r q)"),
                                              start=True, stop=True), [c.KcT, QE], [P.B[sb_]])
                pt = PT[g % 2]
                P.act(pt.ap, P.bank(sb_, [128, 512], F32), AF.Exp, scale=SCALE, reads=[P.B[sb_]], writes=[pt])
                P.op("dve", lambda e: e.tensor_tensor(pt.ap.rearrange("p (r q) -> p r q", r=4), pt.ap.rearrange("p (r q) -> p r q", r=4),
                                                      bc(cmaskT.ap[:, i, :], [128, 4, 128], 1), ALU.mult), [pt, cmaskT], [pt])
                ob = 4 if g % 2 == 0 else 6
                P.op("pe", lambda e: e.matmul(P.bank(ob, [128, 512], F32)[0:65, :], c.Vc1.ap[:, g, :], pt.ap, start=True, stop=True), [c.Vc1, pt], [P.B[ob]])
                combine(P, c, ob, g, 0, gts, oatt, tmp, first=True)
            P.op("dve", lambda e: e.tensor_tensor(sc.ap, sc.ap, bc(candm.ap[:, i, :], [128, 4, 64], 1), ALU.mult), [sc, candm], [sc])
            P.op("dve", lambda e: e.tensor_tensor(sc.ap, sc.ap, bc(addm.ap[:, i, :], [128, 4, 64], 1), ALU.add), [sc, addm], [sc])
            for g in range(4):
                P.op("dve", lambda e: e.max(m8.ap[:, g, 0:8], sc.ap[:, g, :]), [sc], [m8])
                P.op("dve", lambda e: e.match_replace(sc2.ap[:, g, :], m8.ap[:, g, 0:8], sc.ap[:, g, :], -2.0), [sc, m8], [sc2])
                P.op("dve", lambda e: e.max(m8.ap[:, g, 8:16], sc2.ap[:, g, :]), [sc2], [m8])
            P.op("dve", lambda e: e.tensor_scalar_max(m8.ap[:, :, 16:17], m8.ap[:, :, 15:16], 0.0), [m8], [m8])
            P.op("dve", lambda e: e.tensor_tensor(sc2.ap, sc.ap, bc(m8.ap[:, :, 16], [128, 4, 64], 2), ALU.is_ge), [sc, m8], [sc2])
            P.op("dve", lambda e: e.tensor_scalar(sc2.ap, sc2.ap, -NEGB, NEGB, ALU.mult, ALU.add), [sc2], [sc2])
            for slot in range(2):
                hf_ = 1 - slot
                P.op("pool", lambda e: e.tensor_copy(BTin.ap[:, :, slot * 64:(slot + 1) * 64], sc2.ap[:, hf_ * 2:hf_ * 2 + 2, :]), [sc2], [BTin])
            for gi in range(2):
                P.op("pe", lambda e: e.transpose(P.bank(1, [128, 8, 128], BF16)[:, gi, :], BTin.ap[:, gi, :], c.ident.ap), [BTin, c.ident], [P.B[1]])
            BT = P.bank(1, [128, 8, 128], BF16)
            for gi in range(2):
                P.op("act", lambda e: e.copy(QE.ap[64:128, gi, :, :], bc(BT[64:128, gi, :], [64, 4, 128], 1)), [P.B[1]], [QE])
                P.op("dve", lambda e: e.tensor_copy(QE.ap[0:64, 2 + gi, :, :], bc(BT[0:64, gi, :], [64, 4, 128], 1)), [P.B[1]], [QE])
            for g in range(4):
                hf, gi = g // 2, g % 2
                ob = 4 if g % 2 == 0 else 6
                rhs = QE.ap[:, g, :, :].rearrange("p r q -> p (r q)")
                for kt in range(v + 1):
                    sb_ = 2 + kt % 2
                    diag = kt == v
                    P.op("pe", lambda e: e.matmul(P.bank(sb_, [128, 512], F32), c.Ksel.ap[:, g, kt * 128:(kt + 1) * 128], rhs, start=True, stop=not diag),
                         [c.Ksel, QE], [P.B[sb_]])
                    if diag:
                        P.op("pe", lambda e: e.matmul(P.bank(sb_, [128, 512], F32), c.ident.ap, CB.ap[:, 0, :], start=False, stop=True), [c.ident, CB], [P.B[sb_]])
                    pt = PT[kt % 2]
                    P.act(pt.ap, P.bank(sb_, [128, 512], F32), AF.Exp, scale=SCALE, reads=[P.B[sb_]], writes=[pt])
                    P.op("pe", lambda e: e.matmul(P.bank(ob, [128, 512], F32)[0:65, :], c.Vsel1.ap[:, kt, g, :], pt.ap, start=(kt == 0), stop=(kt == v)),
                         [c.Vsel1, pt], [P.B[ob]])
                combine(P, c, ob, g, 1, gts, oatt, tmp)
            for g in range(4):
                hf, gi = g // 2, g % 2
                hs_ = slice(hf * 64, hf * 64 + 64)
                ob = 4 if g % 2 == 0 else 6
                rhs = QE.ap[hs_, g, :, :].rearrange("p r q -> p (r q)")
                kts = [kt for kt in range(v - 4, v + 1) if kt >= 0]
                for kt in kts:
                    sb_ = 2 + kt % 2
                    mk = 0 if kt == v else (1 if kt == v - 4 else None)
                    P.op("pe", lambda e: e.matmul(P.bank(sb_, [128, 512], F32), c.Kwin.ap[hs_, gi, kt * 128:(kt + 1) * 128], rhs, start=True, stop=(mk is None)),
                         [c.Kwin, QE], [P.B[sb_]])
                    if mk is not None:
                        P.op("pe", lambda e: e.matmul(P.bank(sb_, [128, 512], F32), c.ident.ap, CB.ap[:, mk, :], start=False, stop=True), [c.ident, CB], [P.B[sb_]])
                    pt = PT[kt % 2]
                    P.act(pt.ap, P.bank(sb_, [128, 512], F32), AF.Exp, scale=SCALE, reads=[P.B[sb_]], writes=[pt])
                    P.op("pe", lambda e: e.matmul(P.bank(ob, [128, 512], F32)[0:65, :], c.Vwin1.ap[:, kt, g, :], pt.ap, start=(kt == kts[0]), stop=(kt == kts[-1])),
                         [c.Vwin1, pt], [P.B[ob]])
                combine(P, c, ob, g, 2, gts, oatt, tmp)
            P.op("act", lambda e: e.copy(QTin.ap.rearrange("p a b -> p (a b)"), oatt.ap), [oatt], [QTin])
            for cch in range(8):
                P.op("pe", lambda e: e.transpose(P.bank(0, [128, 8, 128], BF16)[:, cch, :], QTin.ap[:, cch, :], c.ident.ap), [QTin, c.ident], [P.B[0]])
            P.op("act", lambda e: e.copy(c.oattT.ap[:, :, i * 128:(i + 1) * 128], P.bank(0, [128, 8, 128], BF16)), [P.B[0]], [c.oattT])
    P.barrier()


def pass_d(P, c):
    NTOK = 8 * 128 + NS
    segs = [(0, 512), (512, 512), (1024, NS)]
    with P.scope():
        xnTo = P.sb("xnTo", [128, 16, NTOK], BF16)
        mT = P.sb("mT", [128, 16, NTOK], BF16)
        for i in range(9):
            blk = 4 * i + 3 if i < 8 else NBLK
            n = 128 if i < 8 else NS
            xnT, _ = norm_T(P, c, blk, n)
            P.op("pool", lambda e: e.tensor_copy(xnTo.ap[:, :, i * 128:i * 128 + n], xnT.ap[:, :, 0:n]), [xnT], [xnTo])
        with P.scope():
            wg = [P.sb(f"wg{i}", [128, 16, 128], BF16) for i in range(2)]
            sg = [P.sb(f"sg{i}", [128, 512], BF16) for i in range(2)]
            it = 0
            for c0, tgt in ((C_GA, c.oattT), (C_GR, c.hs_own)):
                for cch in range(8):
                    w = wg[it % 2]
                    P.dma("pool", w.ap, c.w_in[:, c0 + cch * 128:c0 + (cch + 1) * 128].rearrange("(k p) c -> p k c", p=128), writes=[w])
                    for si, (t0, tn) in enumerate(segs):
                        bk = 2 + (it * 3 + si) % 2
                        for k in range(16):
                            P.op("pe", lambda e: e.matmul(P.bank(bk, [128, 512], F32)[:, 0:tn], w.ap[:, k, :], xnTo.ap[:, k, t0:t0 + tn],
                                                          start=(k == 0), stop=(k == 15)), [w, xnTo], [P.B[bk]])
                        sgt = sg[(it * 3 + si) % 2]
                        P.act(sgt.ap[:, 0:tn], P.bank(bk, [128, 512], F32)[:, 0:tn], AF.Silu, reads=[P.B[bk]], writes=[sgt])
                        P.op("dve", lambda e: e.tensor_tensor(tgt.ap[:, cch, t0:t0 + tn], tgt.ap[:, cch, t0:t0 + tn], sgt.ap[:, 0:tn], ALU.mult), [tgt, sgt], [tgt])
                    it += 1
        if c.dbg:
            P.dma("sp", c.dbg_g, c.oattT.ap, reads=[c.oattT], out=True)
            P.dma("sp", c.dbg_hg, c.hs_own.ap, reads=[c.hs_own], out=True)
            P.dma("sp", c.dbg_xn, xnTo.ap, reads=[xnTo], out=True)
        with P.scope():
            wa = [P.sb(f"wa{i}", [128, 8, 128], BF16) for i in range(2)]
            wr = [P.sb(f"wr{i}", [128, 8, 128], BF16) for i in range(2)]
            wma = [P.sb(f"wma{i}", [128, 16, 128], BF16) for i in range(2)]
            wmr = [P.sb(f"wmr{i}", [128, 16, 128], BF16) for i in range(2)]
            s1 = P.sb("s1", [128, 512], F32)
            s2 = P.sb("s2", [128, 512], F32)
            for cc in range(16):
                cs = slice(cc * 128, (cc + 1) * 128)
                b = cc % 2
                P.dma("pool", wa[b].ap, c.w_att_out[:, cs].rearrange("(k p) c -> p k c", p=128), writes=[wa[b]])
                P.dma("pool", wr[b].ap, c.w_rnn_out[:, cs].rearrange("(k p) c -> p k c", p=128), writes=[wr[b]])
                P.dma("pool", wma[b].ap, c.w_in[:, C_MA + cc * 128:C_MA + (cc + 1) * 128].rearrange("(k p) c -> p k c", p=128), writes=[wma[b]])
                P.dma("pool", wmr[b].ap, c.w_in[:, C_MR + cc * 128:C_MR + (cc + 1) * 128].rearrange("(k p) c -> p k c", p=128), writes=[wmr[b]])
                for (t0, tn) in segs:
                    for bk, w, src, nk in ((2, wa[b], c.oattT, 8), (3, wr[b], c.hs_own, 8), (4, wma[b], xnTo, 16), (5, wmr[b], xnTo, 16)):
                        for k in range(nk):
                            P.op("pe", lambda e: e.matmul(P.bank(bk, [128, 512], F32)[:, 0:tn], w.ap[:, k, :], src.ap[:, k, t0:t0 + tn],
                                                          start=(k == 0), stop=(k == nk - 1)), [w, src], [P.B[bk]])
                    P.act(s1.ap[:, 0:tn], P.bank(4, [128, 512], F32)[:, 0:tn], AF.Sigmoid, reads=[P.B[4]], writes=[s1])
                    P.act(s2.ap[:, 0:tn], P.bank(5, [128, 512], F32)[:, 0:tn], AF.Sigmoid, reads=[P.B[5]], writes=[s2])
                    P.op("dve", lambda e: e.tensor_tensor(s1.ap[:, 0:tn], s1.ap[:, 0:tn], P.bank(2, [128, 512], F32)[:, 0:tn], ALU.mult), [s1, P.B[2]], [s1])
                    P.op("dve", lambda e: e.tensor_tensor(s2.ap[:, 0:tn], s2.ap[:, 0:tn], P.bank(3, [128, 512], F32)[:, 0:tn], ALU.mult), [s2, P.B[3]], [s2])
                    P.op("pool", lambda e: e.tensor_tensor(mT.ap[:, cc, t0:t0 + tn], s1.ap[:, 0:tn], s2.ap[:, 0:tn], ALU.add), [s1, s2], [mT])
        if c.dbg:
            P.dma("sp", c.dbg_m, mT.ap, reads=[mT], out=True)
        with P.scope():
            wo = [P.sb(f"wo{i}", [128, 16, 512], BF16) for i in range(2)]
            xr = [P.sb(f"xres{i}", [128, 512], F32) for i in range(2)]
            yt = [P.sb(f"yt{i}", [128, 512], F32) for i in range(2)]
            it = 0
            for oc in range(4):
                w = wo[oc % 2]
                for k in range(16):
                    P.dma("pool", w.ap[:, k, :], c.w_out[k * 128:(k + 1) * 128, oc * 512:(oc + 1) * 512], writes=[w], semkey=f"wo{oc % 2}")
                for i in range(9):
                    blk = 4 * i + 3 if i < 8 else NBLK
                    n = 128 if i < 8 else NS
                    bk = 2 + it % 2
                    for k in range(16):
                        P.op("pe", lambda e: e.matmul(P.bank(bk, [128, 512], F32)[0:n, :], mT.ap[:, k, i * 128:i * 128 + n], w.ap[:, k, :],
                                                      start=(k == 0), stop=(k == 15)), [mT, w], [P.B[bk]])
                    x_ = xr[it % 2]
                    y_ = yt[it % 2]
                    P.dma("sp", x_.ap[0:n, :], c.xv[blk * 128:blk * 128 + n, oc * 512:(oc + 1) * 512], writes=[x_])
                    P.op("dve", lambda e: e.tensor_tensor(y_.ap[0:n, :], x_.ap[0:n, :], P.bank(bk, [128, 512], F32)[0:n, :], ALU.add), [x_, P.B[bk]], [y_])
                    if i < 8:
                        P.dma("sp", c.o_y[i, :, oc * 512:(oc + 1) * 512], y_.ap, reads=[y_], out=True)
                    else:
                        P.dma("sp", c.o_ys[:, oc * 512:(oc + 1) * 512], y_.ap[0:n, :], reads=[y_], out=True)
                    it += 1
    P.barrier()


def pass_s(P, c):
    def b2(ap, shape):
        return ap.unsqueeze(1).unsqueeze(1).broadcast_to(list(shape))
    with P.scope():
        wqs = P.sb("s_wqs", [128, 16, 512], BF16)
        qf = P.sb("s_qf", [128, 1024], F32)
        gts = P.sb("s_gts", [128, 48], F32)
        tmp = dict(sq=P.sb("s_sq", [128, 512], F32), st=P.sb("s_st", [128, 64], F32), oTs=P.sb("s_oTs", [128, 512], F32),
                   cf=P.sb("s_cf", [128, 16], F32), otmp=P.sb("s_otmp", [128, 256], F32))
        qproj(P, c, NBLK, NS, wqs, qf, gts, tmp)
        QTin = P.sb("s_QTin", [128, 8, 128], BF16)
        QEs = P.sb("s_QEs", [128, 8, NS], BF16)
        P.op("pool", lambda e: e.tensor_copy(QTin.ap[0:NS].rearrange("p (slot r) (gh d) -> p slot gh r d", slot=2, gh=2),
                                             qf.ap[0:NS].rearrange("p (slot gh r d) -> p slot gh r d", slot=2, gh=2, r=4)), [qf], [QTin])
        for t8 in range(8):
            P.op("pe", lambda e: e.transpose(P.bank(0, [128, 8, 128], BF16)[:, t8, 0:NS], QTin.ap[0:NS, t8, :], c.ident.ap[0:NS, 0:NS]), [QTin, c.ident], [P.B[0]])
        P.op("act", lambda e: e.copy(QEs.ap, P.bank(0, [128, 8, 128], BF16)[:, :, 0:NS]), [P.B[0]], [QEs])
        wc16 = P.sb("wc16", [128, 32, 2, 64], F32)
        iota16 = P.sb("iota16", [128, 16], F32)
        mwin0 = P.sb("mwin0", [128, 8], BF16)
        caus8 = P.sb("caus8", [128, 8], BF16)
        for dst, src in ((wc16, c.d_wc16), (iota16, c.d_iota16), (mwin0, c.d_mwin0), (caus8, c.d_caus8)):
            P.dma("sp", dst.ap, src, writes=[dst])
        idx = P.sb("s_idx", [128, 1], I32)
        idf = P.sb("s_idf", [128, 2], F32)
        ia = P.sb("s_ia", [128, 16], F32)
        ii = P.sb("s_ii", [128, 16], I32)
        gb = P.sb("s_gb", [128, 48], F32)
        ck = [P.sb(f"s_ck{i}", [128, 4096], F32) for i in range(2)]
        tm = P.sb("s_tm", [128, 8, 512], F32)
        part = P.sb("s_part", [128, 512], F32)
        acc = P.sb("s_acc", [128, 4, 512], F32)
        kcn = P.sb("s_kcn", [128, 4, 256], F32)
        kcb = P.sb("s_kcb", [128, 4, 256], BF16)
        KcTs = P.sb("s_KcTs", [128, 4, 2, 128], BF16)
        Vc1s = P.sb("s_Vc1s", [128, 4, 4, 65], BF16)
        Es = P.sb("s_Es", [128, 512], F32)
        impa = P.sb("s_impa", [128, 512], F32)
        sc = P.sb("s_sc", [128, 4, 256], F32)
        sc2 = P.sb("s_sc2", [128, 4, 256], F32)
        smk = P.sb("s_smk", [128, 4, 2, 128], BF16)
        m8 = P.sb("s_m8", [128, 4, 24], F32)
        mT = P.sb("s_mT", [128, 8, 8], BF16)
        V1c = P.sb("s_V1c", [128, 8, 4, 65], BF16)
        kcs = P.sb("s_kcs", [128, 8, 256], BF16)
        KT = P.sb("s_KT", [128, 8, 2, 128], BF16)
        PT = [P.sb(f"s_PT{i}", [128, 256], BF16) for i in range(2)]
        rwn = P.sb("s_rwn", [128, 512], F32)
        knn = P.sb("s_knn", [128, 256], BF16)
        V1n = P.sb("s_V1n", [128, 4, 65], BF16)
        KTn = P.sb("s_KTn", [128, 2, 8], BF16)
        PTn = P.sb("s_PTn", [128, 32], BF16)
        oatt = P.sb("s_oatt", [128, 1024], F32)
        oattb = P.sb("s_oattb", [128, 1024], BF16)
        for V in (Vc1s, V1c, V1n):
            P.op("pool", lambda e: e.memset(V.ap, 1.0), [], [V])
        rows_cmp = c.d_ccmp.rearrange("n (c t) f -> (n c) (t f)", c=16)
        rows_sel = c.d_csel.rearrange("n (c t) f -> (n c) (t f)", c=16)
        ngather = [0]

        def gather(rows, ch):
            t = ck[ngather[0] % 2]
            ngather[0] += 1
            P.dma_fn("pool", lambda e: e.indirect_dma_start(out=t.ap, out_offset=None, in_=rows,
                                                            in_offset=bass.IndirectOffsetOnAxis(ap=ii.ap[:, ch:ch + 1], axis=0)),
                     reads=[ii], writes=[t], semkey=("gather", ngather[0] % 2))
            return t

        def new_tokens(dram_rows, res, b, br, caus=True):
            P.dma("sp", rwn.ap[0:8, :], dram_rows[b * 8:(b + 1) * 8, :], reads=[res], writes=[rwn])
            P.op("act", lambda e: e.copy(knn.ap[0:8, :], rwn.ap[0:8, 0:256]), [rwn], [knn])
            P.op("pool", lambda e: e.tensor_copy(V1n.ap[0:8, :, 0:64], rwn.ap[0:8, 256:512].rearrange("p (g d) -> p g d", g=4)), [rwn], [V1n])
            for slot in range(2):
                P.op("pe", lambda e: e.transpose(P.bank(1, [128, 8, 128], BF16)[:, slot, 0:8], knn.ap[0:8, slot * 128:(slot + 1) * 128], c.ident.ap[0:8, 0:8]),
                     [knn, c.ident], [P.B[1]])
            P.op("act", lambda e: e.copy(KTn.ap, P.bank(1, [128, 8, 128], BF16)[:, 0:2, 0:8]), [P.B[1]], [KTn])
            for g in range(4):
                slot, hf = g // 2, g % 2
                hs_ = slice(hf * 64, hf * 64 + 64)
                sb_ = 2 + g % 2
                P.op("pe", lambda e: e.matmul(P.bank(sb_, [128, 512], F32)[0:8, 0:32], KTn.ap[hs_, slot, :], QEs.ap[hs_, slot * 4:(slot + 1) * 4, b * 8:(b + 1) * 8],
                                              start=True, stop=True), [KTn, QEs], [P.B[sb_]])
                P.act(PTn.ap[0:8, :], P.bank(sb_, [128, 512], F32)[0:8, 0:32], AF.Exp, scale=SCALE, reads=[P.B[sb_]], writes=[PTn])
                P.op("dve", lambda e: e.tensor_tensor(PTn.ap[0:8, :].rearrange("p (r q) -> p r q", r=4), PTn.ap[0:8, :].rearrange("p (r q) -> p r q", r=4),
                                                      bc(caus8.ap[0:8, :], [8, 4, 8], 1), ALU.mult), [PTn, caus8], [PTn])
                P.op("pe", lambda e: e.matmul(P.bank(4 + g, [128, 512], F32)[0:65, 0:32], V1n.ap[0:8, g, :], PTn.ap[0:8, :], start=False, stop=True),
                     [V1n, PTn], [P.B[4 + g]])
                combine(P, c, 4 + g, g, br, gb, oatt, tmp, n=8, tb=0)

        for b in range(4):
            P.dma("sp", idx.ap, c.d_pt[b].rearrange("(p o) -> p o", o=1), writes=[idx])
            P.op("dve", lambda e: e.tensor_copy(idf.ap[:, 0:1], idx.ap), [idx], [idf])
            P.op("dve", lambda e: e.tensor_scalar_mul(idf.ap[:, 1:2], idf.ap[:, 0:1], 16.0), [idf], [idf])
            P.op("dve", lambda e: e.tensor_scalar(ia.ap, iota16.ap, idf.ap[:, 1:2], None, ALU.add), [idf, iota16], [ia])
            P.op("dve", lambda e: e.tensor_copy(ii.ap, ia.ap), [ia], [ii])
            P.dma("sp", gb.ap[0:8, :], gts.ap[b * 8:(b + 1) * 8, :], reads=[gts], writes=[gb])
            tn = gather(rows_cmp, 0)
            for ch in range(16):
                blk4, cq = ch // 4, ch % 4
                t = tn
                if ch + 1 < 16:
                    tn = gather(rows_cmp, ch + 1)
                P.op("pool", lambda e: e.tensor_tensor(tm.ap.rearrange("p j (s g d) -> p j s g d", s=2, g=4),
                                                       t.ap.rearrange("p (j s g d) -> p j s g d", j=8, s=2, g=4),
                                                       wc16.ap[:, cq * 8:(cq + 1) * 8, :, :].unsqueeze(3).broadcast_to([128, 8, 2, 4, 64]), ALU.mult),
                     [t, wc16], [tm])
                if cq == 0:
                    P.op("dve", lambda e: e.tensor_reduce(acc.ap[:, blk4, :], tm.ap.rearrange("p j f -> p f j"), AX.X, ALU.add), [tm], [acc])
                else:
                    P.op("dve", lambda e: e.tensor_reduce(part.ap, tm.ap.rearrange("p j f -> p f j"), AX.X, ALU.add), [tm], [part])
                    P.op("dve", lambda e: e.tensor_tensor(acc.ap[:, blk4, :], acc.ap[:, blk4, :], part.ap, ALU.add), [acc, part], [acc])
            for blk4 in range(4):
                rms_rows(P, c, acc.ap[:, blk4, 0:256], 128, 4, c.kg_rep.ap[:, 0, :], kcn.ap[:, blk4, :], tmp, [acc], [kcn])
            P.op("act", lambda e: e.copy(kcb.ap, kcn.ap), [kcn], [kcb])
            P.op("pool", lambda e: e.tensor_copy(Vc1s.ap[:, :, :, 0:64], acc.ap[:, :, 256:512].rearrange("p b (g d) -> p b g d", g=4)), [acc], [Vc1s])
            for blk4 in range(4):
                for slot in range(2):
                    P.op("pe", lambda e: e.transpose(P.bank(0, [128, 8, 128], BF16)[:, blk4 * 2 + slot, :], kcb.ap[:, blk4, slot * 128:(slot + 1) * 128], c.ident.ap),
                         [kcb, c.ident], [P.B[0]])
            P.op("act", lambda e: e.copy(KcTs.ap.rearrange("p b s n -> p (b s) n"), P.bank(0, [128, 8, 128], BF16)), [P.B[0]], [KcTs])
            for g in range(4):
                slot, hf = g // 2, g % 2
                hs_ = slice(hf * 64, hf * 64 + 64)
                for r in range(4):
                    sb_ = 2 + r % 2
                    for blk4 in range(4):
                        P.op("pe", lambda e: e.matmul(P.bank(sb_, [128, 4, 128], F32)[0:8, blk4, :], QEs.ap[hs_, slot * 4 + r, b * 8:(b + 1) * 8], KcTs.ap[hs_, blk4, slot, :],
                                                      start=True, stop=True), [QEs, KcTs], [P.B[sb_]])
                    P.act(Es.ap[0:8, :], P.bank(sb_, [128, 512], F32)[0:8, :], AF.Exp, scale=SCALE, accum_out=m8.ap[0:8, 0, 16:17], reads=[P.B[sb_]], writes=[Es, m8])
                    P.op("dve", lambda e: e.reciprocal(m8.ap[0:8, 0, 17:18], m8.ap[0:8, 0, 16:17]), [m8], [m8])
                    if r == 0:
                        P.op("dve", lambda e: e.tensor_scalar(impa.ap[0:8, :], Es.ap[0:8, :], m8.ap[0:8, 0, 17:18], None, ALU.mult), [Es, m8], [impa])
                    else:
                        P.op("dve", lambda e: e.tensor_scalar(Es.ap[0:8, :], Es.ap[0:8, :], m8.ap[0:8, 0, 17:18], None, ALU.mult), [Es, m8], [Es])
                        P.op("dve", lambda e: e.tensor_tensor(impa.ap[0:8, :], impa.ap[0:8, :], Es.ap[0:8, :], ALU.add), [impa, Es], [impa])
                iv = impa.ap[0:8, :].rearrange("p (h two n) -> p h two n", h=2, two=2)
                P.op("dve", lambda e: e.tensor_tensor(sc.ap[0:8, g, :].rearrange("p (h n) -> p h n", h=2), iv[:, :, 0, :], iv[:, :, 1, :], ALU.add), [impa], [sc])
                sb_ = 2 + g % 2
                for blk4 in range(4):
                    P.op("pe", lambda e: e.matmul(P.bank(sb_, [128, 4, 32], F32)[:, blk4, :], KcTs.ap[hs_, blk4, slot, :], QEs.ap[hs_, slot * 4:(slot + 1) * 4, b * 8:(b + 1) * 8],
                                                  start=True, stop=True), [KcTs, QEs], [P.B[sb_]])
                pt = PT[g % 2]
                P.act(pt.ap[:, 0:128], P.bank(sb_, [128, 128], F32), AF.Exp, scale=SCALE, reads=[P.B[sb_]], writes=[pt])
                for blk4 in range(4):
                    P.op("pe", lambda e: e.matmul(P.bank(4 + g, [128, 512], F32)[0:65, 0:32], Vc1s.ap[:, blk4, g, :], pt.ap[:, blk4 * 32:(blk4 + 1) * 32],
                                                  start=(blk4 == 0), stop=(blk4 == 3)), [Vc1s, pt], [P.B[4 + g]])
                combine(P, c, 4 + g, g, 0, gb, oatt, tmp, n=8, first=True, tb=0)
            P.op("pool", lambda e: e.memset(sc.ap[0:8, :, 0:1], 10.0), [sc], [sc])
            P.op("pool", lambda e: e.memset(sc.ap[0:8, :, 255:256], 10.0), [sc], [sc])
            for g in range(4):
                P.op("dve", lambda e: e.max(m8.ap[0:8, g, 0:8], sc.ap[0:8, g, :]), [sc], [m8])
                P.op("dve", lambda e: e.match_replace(sc2.ap[0:8, g, :], m8.ap[0:8, g, 0:8], sc.ap[0:8, g, :], -2.0), [sc, m8], [sc2])
                P.op("dve", lambda e: e.max(m8.ap[0:8, g, 8:16], sc2.ap[0:8, g, :]), [sc2], [m8])
            P.op("dve", lambda e: e.tensor_tensor(smk.ap[0:8].rearrange("p g h n -> p g (h n)"), sc.ap[0:8], bc(m8.ap[0:8, :, 14], [8, 4, 256], 2), ALU.is_ge), [sc, m8], [smk])
            for g in range(4):
                for h2 in range(2):
                    P.op("pe", lambda e: e.transpose(P.bank(1, [128, 8, 128], BF16)[:, g * 2 + h2, 0:8], smk.ap[0:8, g, h2, :], c.ident.ap[0:8, 0:8]), [smk, c.ident], [P.B[1]])
            P.op("act", lambda e: e.copy(mT.ap, P.bank(1, [128, 8, 128], BF16)[:, :, 0:8]), [P.B[1]], [mT])
            tn = gather(rows_sel, 0)
            for ch in range(16):
                h2 = ch // 8
                t = tn
                if ch + 1 < 16:
                    tn = gather(rows_sel, ch + 1)
                t5 = t.ap.rearrange("p (j s g d) -> p j s g d", j=8, s=2, g=4)
                P.op("pool", lambda e: e.tensor_copy(V1c.ap[:, :, :, 0:64], t5[:, :, 1]), [t], [V1c])
                P.op("act", lambda e: e.copy(kcs.ap, t.ap.rearrange("p (j f) -> p j f", j=8)[:, :, 0:256]), [t], [kcs])
                for tt in range(8):
                    for slot in range(2):
                        P.op("pe", lambda e: e.transpose(P.bank(tt // 4, [128, 8, 128], BF16)[:, (tt % 4) * 2 + slot, :], kcs.ap[:, tt, slot * 128:(slot + 1) * 128], c.ident.ap),
                             [kcs, c.ident], [P.B[tt // 4]])
                P.op("act", lambda e: e.copy(KT.ap[:, 0:4].rearrange("p t s n -> p (t s) n"), P.bank(0, [128, 8, 128], BF16)), [P.B[0]], [KT])
                P.op("dve", lambda e: e.tensor_copy(KT.ap[:, 4:8].rearrange("p t s n -> p (t s) n"), P.bank(1, [128, 8, 128], BF16)), [P.B[1]], [KT])
                for g in range(4):
                    slot, hf = g // 2, g % 2
                    hs_ = slice(hf * 64, hf * 64 + 64)
                    sb_ = 2 + g % 2
                    for tt in range(8):
                        P.op("pe", lambda e: e.matmul(P.bank(sb_, [128, 8, 32], F32)[:, tt, :], KT.ap[hs_, tt, slot, :], QEs.ap[hs_, slot * 4:(slot + 1) * 4, b * 8:(b + 1) * 8],
                                                      start=True, stop=True), [KT, QEs], [P.B[sb_]])
                    pt = PT[g % 2]
                    P.act(pt.ap, P.bank(sb_, [128, 256], F32), AF.Exp, scale=SCALE, reads=[P.B[sb_]], writes=[pt])
                    P.op("dve", lambda e: e.tensor_tensor(pt.ap.rearrange("p (t r q) -> p t r q", t=8, r=4), pt.ap.rearrange("p (t r q) -> p t r q", t=8, r=4),
                                                          b2(mT.ap[:, g * 2 + h2, :], [128, 8, 4, 8]), ALU.mult), [pt, mT], [pt])
                    for tt in range(8):
                        P.op("pe", lambda e: e.matmul(P.bank(4 + g, [128, 512], F32)[0:65, 0:32], V1c.ap[:, tt, g, :], pt.ap[:, tt * 32:(tt + 1) * 32],
                                                      start=(ch == 0 and tt == 0), stop=False), [V1c, pt], [P.B[4 + g]])
            new_tokens(c.s_sel, c.r_ssel, b, 1)
            cw = ck[0]
            P.dma("sp", cw.ap.rearrange("p (t f) -> p t f", t=8)[:, 0:4, :], c.d_cwin[b].rearrange("(t p) f -> p t f", p=128), writes=[cw], semkey=("gather", 1))
            ngather[0] = 1
            cw3 = cw.ap.rearrange("p (t s g d) -> p t s g d", t=8, s=2, g=4)
            P.op("pool", lambda e: e.tensor_copy(V1c.ap[:, 0:4, :, 0:64], cw3[:, 0:4, 1]), [cw], [V1c])
            P.op("act", lambda e: e.copy(kcs.ap[:, 0:4, :], cw.ap.rearrange("p (t f) -> p t f", t=8)[:, 0:4, 0:256]), [cw], [kcs])
            for tt in range(4):
                for slot in range(2):
                    P.op("pe", lambda e: e.transpose(P.bank(0, [128, 8, 128], BF16)[:, tt * 2 + slot, :], kcs.ap[:, tt, slot * 128:(slot + 1) * 128], c.ident.ap),
                         [kcs, c.ident], [P.B[0]])
            P.op("act", lambda e: e.copy(KT.ap[:, 0:4].rearrange("p t s n -> p (t s) n"), P.bank(0, [128, 8, 128], BF16)), [P.B[0]], [KT])
            for g in range(4):
                slot, hf = g // 2, g % 2
                hs_ = slice(hf * 64, hf * 64 + 64)
                sb_ = 2 + g % 2
                for tt in range(4):
                    P.op("pe", lambda e: e.matmul(P.bank(sb_, [128, 8, 32], F32)[:, tt, :], KT.ap[hs_, tt, slot, :], QEs.ap[hs_, slot * 4:(slot + 1) * 4, b * 8:(b + 1) * 8],
                                                  start=True, stop=True), [KT, QEs], [P.B[sb_]])
                pt = PT[g % 2]
                P.act(pt.ap[:, 0:128], P.bank(sb_, [128, 128], F32), AF.Exp, scale=SCALE, reads=[P.B[sb_]], writes=[pt])
                P.op("dve", lambda e: e.tensor_tensor(pt.ap[:, 0:32].rearrange("p (r q) -> p r q", r=4), pt.ap[:, 0:32].rearrange("p (r q) -> p r q", r=4),
                                                      bc(mwin0.ap, [128, 4, 8], 1), ALU.mult), [pt, mwin0], [pt])
                for tt in range(4):
                    P.op("pe", lambda e: e.matmul(P.bank(4 + g, [128, 512], F32)[0:65, 0:32], V1c.ap[:, tt, g, :], pt.ap[:, tt * 32:(tt + 1) * 32],
                                                  start=(tt == 0), stop=False), [V1c, pt], [P.B[4 + g]])
            new_tokens(c.s_win, c.r_swin, b, 2)
            P.dma("sp", c.o_wins[b, 0:504, :], c.d_cwin[b, 8:512, :], semkey="owins", reads=[c.r_swin], out=True)
            P.dma("sp", c.o_wins[b, 504:512, :], rwn.ap[0:8, :], reads=[rwn], semkey="owins", out=True)
            P.op("act", lambda e: e.copy(oattb.ap[0:8, :], oatt.ap[0:8, :]), [oatt], [oattb])
            for cch in range(8):
                P.op("pe", lambda e: e.transpose(P.bank(0, [128, 8, 128], BF16)[:, cch, 0:8], oattb.ap[0:8, cch * 128:(cch + 1) * 128], c.ident.ap[0:8, 0:8]), [oattb, c.ident], [P.B[0]])
            P.op("act", lambda e: e.copy(c.oattT.ap[:, :, 1024 + b * 8:1024 + (b + 1) * 8], P.bank(0, [128, 8, 128], BF16)[:, :, 0:8]), [P.B[0]], [c.oattT])


def build_program(passes=("k1", "k2", "s", "r", "d"), pool_pages=5120):
    nc = bass.Bass("TRN2", target_bir_lowering=False)
    P = Prog(nc)
    c = Ctx()
    c.xv = dram_in(nc, "xv", [(NBLK + 1) * 128, D])
    c.w_in = dram_in(nc, "w_in", [D, 9776])
    g_rep = dram_in(nc, "g_rep", [128, D])
    qg_rep = dram_in(nc, "qg_rep", [128, 64])
    kg_rep = dram_in(nc, "kg_rep", [128, 3, 64])
    wc_rep = dram_in(nc, "wc_rep", [128, 512])
    ident = dram_in(nc, "ident", [128, 128], BF16)
    indw = dram_in(nc, "indw", [128, 252])
    ehost = dram_in(nc, "ehost", [64, 4096], BF16)
    padcol = dram_in(nc, "padcol", [128, NBLK + 1])
    c.o_cmp = dram_out(nc, "o_cmp", [8, 128, 512])
    c.o_sel = dram_out(nc, "o_sel", [8, 128, 512])
    c.o_win = dram_out(nc, "o_win", [128, 512])
    c.s_cmp = dram_out(nc, "s_cmp", [NS, 512])
    c.s_sel = dram_out(nc, "s_sel", [NS, 512])
    c.s_win = dram_out(nc, "s_win", [NS, 512])
    c.d_wrg = dram_in(nc, "wrg_bd", [128, 8, 128])
    c.d_wig = dram_in(nc, "wig_bd", [128, 8, 128])
    c.d_cw = dram_in(nc, "convw_T", [128, 8, 4])
    c.d_rsm = dram_in(nc, "rsm_T", [128, 4, 8])
    c.d_vrow = dram_in(nc, "vrow", [128, 384])
    c.d_h0T = dram_in(nc, "h0T", [128, 8, 4])
    c.d_sconvT = dram_in(nc, "sconvT", [128, 8, 4, 3])
    c.d_identf = dram_in(nc, "identf", [128, 128])
    c.d_cmask = dram_in(nc, "cmask", [128, 8, 128], BF16)
    c.d_cmaskT = dram_in(nc, "cmaskT", [128, 8, 128], BF16)
    c.d_candm = dram_in(nc, "candm", [128, 8, 64])
    c.d_addm = dram_in(nc, "addm", [128, 8, 64])
    c.d_CB = dram_in(nc, "CB", [128, 2, 512], BF16)
    padcmp = dram_in(nc, "padcmp", [128, 1])
    c.w_att_out = dram_in(nc, "w_att_out", [1024, D])
    c.w_rnn_out = dram_in(nc, "w_rnn_out", [1024, D])
    c.w_out = dram_in(nc, "w_out", [D, D])
    c.r_ssel, c.r_swin = Res("r_ssel"), Res("r_swin")
    if "s" in passes:
        NPOOL = pool_pages
        c.d_ccmp = dram_in(nc, "cache_cmp", [NPOOL, 128, 512])
        c.d_csel = dram_in(nc, "cache_sel", [NPOOL, 128, 512])
        c.d_cwin = dram_in(nc, "cache_win", [4, 512, 512])
        c.d_pt = dram_in(nc, "ptab", [4, 128], I32)
        c.d_wc16 = dram_in(nc, "wc16", [128, 32, 2, 64])
        c.d_iota16 = dram_in(nc, "iota16", [128, 16])
        c.d_mwin0 = dram_in(nc, "mwin0", [128, 8], BF16)
        c.d_caus8 = dram_in(nc, "caus8", [128, 8], BF16)
        c.o_wins = dram_out(nc, "o_wins", [4, 512, 512])
    c.o_y = dram_out(nc, "o_y", [8, 128, D])
    c.o_ys = dram_out(nc, "o_ys", [NS, D])
    c.dbg = "dbg" in passes
    if c.dbg:
        c.dbg_oatt = dram_out(nc, "dbg_oatt", [128, 8, 8 * 128 + NS], BF16)
        c.dbg_hs = dram_out(nc, "dbg_hs", [128, 8, 8 * 128 + NS], BF16)
        c.dbg_g = dram_out(nc, "dbg_g", [128, 8, 8 * 128 + NS], BF16)
        c.dbg_hg = dram_out(nc, "dbg_hg", [128, 8, 8 * 128 + NS], BF16)
        c.dbg_m = dram_out(nc, "dbg_m", [128, 16, 8 * 128 + NS], BF16)
        c.dbg_xn = dram_out(nc, "dbg_xn", [128, 16, 8 * 128 + NS], BF16)
    c.o_hp = dram_out(nc, "o_hp", [1, DR])
    c.o_convp = dram_out(nc, "o_convp", [3, DR])
    c.o_hs = dram_out(nc, "o_hs", [4, DR])
    c.o_convs = dram_out(nc, "o_convs", [4, 3, DR])
    P.init_banks()
    c.g_rep = P.sb("g_rep", [128, D], F32)
    c.qg_rep = P.sb("qg_rep", [128, 64], F32)
    c.kg_rep = P.sb("kg_rep", [128, 3, 64], F32)
    c.wc_rep = P.sb("wc_rep", [128, 512], F32)
    c.ident = P.sb("ident", [128, 128], BF16)
    c.indw = P.sb("indw", [128, 252], F32)
    c.padcol = P.sb("padcol", [128, NBLK + 1], F32)
    c.padcmp = P.sb("padcmp", [128, 1], F32)
    c.identf = P.sb("identf_g", [128, 128], F32)
    for dst, src in ((c.g_rep, g_rep), (c.qg_rep, qg_rep), (c.kg_rep, kg_rep), (c.wc_rep, wc_rep), (c.ident, ident),
                     (c.indw, indw), (c.padcol, padcol), (c.padcmp, padcmp), (c.identf, c.d_identf)):
        P.dma("sp", dst.ap, src, writes=[dst])
    c.nt = [dict(xt=(P.sb(f"xt{i}", [128, D], F32) if i == 0 else None), xs=None, xn=P.sb(f"xn{i}", [128, D], BF16),
                 ss=P.sb(f"ss{i}", [128, 8], F32), xnT=P.sb(f"xnT{i}", [128, 16, 128], BF16)) for i in range(2)]
    NTOK = 8 * 128 + NS
    c.oattT = P.sb("oattT", [128, 8, NTOK], BF16)
    with P.scope():
        NK = NBLK * 128 + NS
        c.Ksel = P.sb("Ksel", [128, 4, NK], BF16)
        c.Kwin = P.sb("Kwin", [128, 2, NK], BF16)
        c.Vsel1 = P.sb("Vsel1", [128, NBLK + 1, 4, 65], BF16)
        c.Vwin1 = P.sb("Vwin1", [128, NBLK + 1, 4, 65], BF16)
        c.KcT = P.sb("KcT", [128, 2, 128], BF16)
        c.Vc1 = P.sb("Vc1", [128, 4, 65], BF16)
        if "k1" in passes:
            for g in range(4):
                rows = slice(64, 128) if g < 2 else slice(0, 64)
                P.dma("sp", c.Ksel.ap[rows, g, 0:4096], ehost, writes=[c.Ksel], semkey="ksel_e")
            for V1 in (c.Vsel1, c.Vwin1):
                P.op("pool", lambda e: e.tensor_copy(V1.ap[:, :, :, 64], bc(c.padcol.ap, [128, NBLK + 1, 4], 2)), [c.padcol], [V1])
            P.op("pool", lambda e: e.tensor_copy(c.Vc1.ap[:, :, 64], bc(c.padcmp.ap[:, 0], [128, 4], 1)), [c.padcmp], [c.Vc1])
            pass_k1(P, c)
        if "k2" in passes:
            pass_k2(P, c)
    if "s" in passes:
        pass_s(P, c)
    c.hs_own = P.sb("hs_own", [128, 8, NTOK], BF16)
    if "r" in passes:
        pass_r(P, c)
    if c.dbg:
        P.dma("sp", c.dbg_oatt, c.oattT.ap, reads=[c.oattT], out=True)
        P.dma("sp", c.dbg_hs, c.hs_own.ap, reads=[c.hs_own], out=True)
        P.barrier()
    if "d" in passes:
        pass_d(P, c)
    print("sbuf peak bytes", P.sb_peak * 4, "ops", len(P.ops))
    P.finish()
    print("sems", P.nsem)
    return nc


def host_prep(inputs, core):
    b, j = core // 4, core % 4
    f32 = np.float32
    m = {}
    xp = inputs["x_prompt"][b]
    pad = (3 - j) * 128
    xv = np.zeros(((NBLK + 1) * 128, D), f32)
    xv[pad:NBLK * 128] = xp[:NBLK * 128 - pad]
    xv[NBLK * 128:NBLK * 128 + NS] = inputs["x_sample"][4 * core:4 * core + 4].reshape(NS, D)
    m["xv"] = xv
    m["w_in"] = np.ascontiguousarray(inputs["w_in"][0])
    m["g_rep"] = np.ascontiguousarray(np.broadcast_to(inputs["norm_g"][0][None, :], (128, D)))
    m["qg_rep"] = np.ascontiguousarray(np.broadcast_to(inputs["q_norm_g"][0][None, :], (128, 64)))
    m["kg_rep"] = np.ascontiguousarray(np.broadcast_to(inputs["k_norm_g"][0][None], (128, 3, 64)))
    wc = inputs["w_cmp"][0]
    wc_rep = np.broadcast_to(wc[:, :, None, :], (32, 2, 4, 64)).reshape(32, 512)
    m["wc_rep"] = np.ascontiguousarray(np.tile(wc_rep, (4, 1)))
    m["ident"] = np.eye(128).astype(ml_dtypes.bfloat16)
    m["identf"] = np.eye(128).astype(f32)
    t = np.arange(128)[:, None]
    cc = np.arange(252)[None, :]
    m["indw"] = (cc - 124 == t // 32).astype(f32)
    key = np.arange(4096)[None, :]
    m["ehost"] = (key // 64 == np.arange(64)[:, None]).astype(ml_dtypes.bfloat16)
    padcol = np.ones((128, NBLK + 1), f32)
    padcol[:, :3 - j] = 0.0
    m["padcol"] = padcol
    def fm(v):
        v = np.asarray(v, f32)
        lead = v.shape[:-1]
        return np.ascontiguousarray(np.moveaxis(v.reshape(lead + (8, 128)), (-2, -1), (1, 0)).reshape((128, 8) + lead))
    for nm, w in (("wrg_bd", inputs["w_rg"][0]), ("wig_bd", inputs["w_ig"][0])):
        bd = np.zeros((128, 8, 128), f32)
        for cch in range(8):
            for hb in range(2):
                bd[hb * 64:(hb + 1) * 64, cch, hb * 64:(hb + 1) * 64] = w[2 * cch + hb]
        m[nm] = bd
    m["convw_T"] = fm(inputs["conv_w"][0])
    m["rsm_T"] = np.ascontiguousarray(np.stack([fm(inputs["conv_b"][0]), fm(inputs["b_rg"][0]), fm(inputs["b_ig"][0]),
                                                  fm(inputs["lru_lambda"][0])], axis=1))
    vrow = np.ones((128, 384), f32)
    vrow[:, :pad] = 0.0
    m["vrow"] = vrow
    bf = ml_dtypes.bfloat16
    n0, jb0 = 4 * (3 - j), 2 * (3 - j)
    p = np.arange(128)
    cmask = np.zeros((128, 8, 128), f32)
    candm = np.zeros((128, 8, 64), f32)
    addm = np.zeros((128, 8, 64), f32)
    nn = np.arange(128)[None, :]
    jj = np.arange(64)[None, :]
    for i in range(8):
        tq = ((4 * i + 3) * 128 + p)[:, None]
        cur = tq // 64
        cmask[:, i, :] = ((nn >= n0) & ((nn + 1) * 32 - 1 <= tq)).astype(f32)
        cand = (jj >= jb0) & (jj < cur)
        forced = cand & ((jj == jb0) | (jj == cur - 1))
        candm[:, i, :] = (cand & ~forced).astype(f32)
        addm[:, i, :] = np.where(forced | (jj == cur), 10.0, np.where(cand, 0.0, -1.0)).astype(f32)
    m["cmask"] = cmask.astype(bf)
    m["cmaskT"] = np.ascontiguousarray(cmask.transpose(2, 1, 0)).astype(bf)
    m["candm"] = candm
    m["addm"] = addm
    kk = np.arange(128)[:, None]
    qq = np.arange(128)[None, :]
    cb0 = np.where(kk <= qq, 0.0, NEGB).astype(f32)
    cb1 = np.where(kk > qq, 0.0, NEGB).astype(f32)
    m["CB"] = np.ascontiguousarray(np.stack([np.tile(cb0, (1, 4)), np.tile(cb1, (1, 4))], axis=1)).astype(bf)
    m["padcmp"] = (np.arange(128)[:, None] >= n0).astype(f32)
    m["w_att_out"] = np.ascontiguousarray(inputs["w_att_out"][0])
    m["w_rnn_out"] = np.ascontiguousarray(inputs["w_rnn_out"][0])
    m["w_out"] = np.ascontiguousarray(inputs["w_out"][0])
    m["cache_win"] = np.ascontiguousarray(inputs["cache_win"][0][4 * core:4 * core + 4]).reshape(4, 512, 512)
    m["ptab"] = np.ascontiguousarray(inputs["page_table"][4 * core:4 * core + 4]).astype(np.int32)
    m["wc16"] = np.ascontiguousarray(np.broadcast_to(wc[None], (128, 32, 2, 64)))
    m["iota16"] = np.ascontiguousarray(np.broadcast_to(np.arange(16, dtype=f32)[None], (128, 16)))
    m["mwin0"] = (np.arange(128)[:, None] > np.arange(8)[None, :]).astype(bf)
    m["caus8"] = (np.arange(128)[:, None] <= np.arange(8)[None, :]).astype(bf)
    m["h0T"] = fm(inputs["state_h"][0][4 * core:4 * core + 4])
    m["sconvT"] = fm(inputs["state_conv"][0][4 * core:4 * core + 4])
    return m


_NC_CACHE = {}


def kernel(**inputs):
    inputs = {k: np.asarray(v) for k, v in inputs.items()}
    npool = inputs["cache_cmp"].shape[1]
    if npool not in _NC_CACHE:
        _NC_CACHE[npool] = build_program(pool_pages=npool)
    nc = _NC_CACHE[npool]
    ccmp = np.ascontiguousarray(inputs["cache_cmp"][0]).reshape(npool, 128, 512)
    csel = np.ascontiguousarray(inputs["cache_sel"][0]).reshape(npool, 128, 512)
    in_maps = []
    for core in range(8):
        m = host_prep(inputs, core)
        m["cache_cmp"] = ccmp
        m["cache_sel"] = csel
        in_maps.append(m)
    res = run_bass_kernel_spmd(nc, in_maps, core_ids=list(range(8)))
    R = res.results
    f32 = np.float32
    B, T = 2, 4096
    y_p = np.empty((B, T, D), f32)
    cmp_p = np.empty((1, B, T, 512), f32)
    sel_p = np.empty((1, B, T, 512), f32)
    win_p = np.empty((1, B, 512, 512), f32)
    h_p = np.empty((1, B, DR), f32)
    conv_p = np.empty((1, B, 3, DR), f32)
    y_s = np.empty((32, 8, D), f32)
    cmp_s = np.empty((1, 32, 8, 512), f32)
    sel_s = np.empty((1, 32, 8, 512), f32)
    win_s = np.empty((1, 32, 512, 512), f32)
    h_s = np.empty((1, 32, DR), f32)
    conv_s = np.empty((1, 32, 3, DR), f32)
    for core in range(8):
        b, j = core // 4, core % 4
        r = R[core]
        for i in range(8):
            qb = 4 * i + j
            sl = slice(qb * 128, (qb + 1) * 128)
            y_p[b, sl] = r["o_y"][i]
            cmp_p[0, b, sl] = r["o_cmp"][i]
            sel_p[0, b, sl] = r["o_sel"][i]
        win_p[0, b, j * 128:(j + 1) * 128] = r["o_win"]
        if j == 3:
            h_p[0, b] = r["o_hp"][0]
            conv_p[0, b] = r["o_convp"]
        bs = slice(4 * core, 4 * core + 4)
        y_s[bs] = np.asarray(r["o_ys"]).reshape(4, 8, D)
        cmp_s[0, bs] = np.asarray(r["s_cmp"]).reshape(4, 8, 512)
        sel_s[0, bs] = np.asarray(r["s_sel"]).reshape(4, 8, 512)
        win_s[0, bs] = r["o_wins"]
        h_s[0, bs] = r["o_hs"]
        conv_s[0, bs] = r["o_convs"]
    sh = (2, 4, 64)
    return (y_p, y_s, cmp_p.reshape(1, B, T, *sh), sel_p.reshape(1, B, T, *sh), win_p.reshape(1, B, 512, *sh), h_p, conv_p,
            cmp_s.reshape(1, 32, 8, *sh), sel_s.reshape(1, 32, 8, *sh), win_s.reshape(1, 32, 512, *sh), h_s, conv_s)
```

```python
import numpy as np
import ml_dtypes
from contextlib import ExitStack
import concourse.bass as bass
import concourse.mybir as mybir
from concourse.bass_utils import run_bass_kernel_spmd

F32 = mybir.dt.float32
BF16 = mybir.dt.bfloat16
I32 = mybir.dt.int32
AF = mybir.ActivationFunctionType
ALU = mybir.AluOpType
AX = mybir.AxisListType


class Res:
    __slots__ = ("name", "lw", "rd", "ap")

    def __init__(self, name, ap=None):
        self.name = name
        self.lw = None
        self.rd = []
        self.ap = ap


class _Rec:
    def __getattr__(self, name):
        def f(*a, **k):
            self.__dict__["call"] = (name, a, k)
            return None
        return f


class Prog:
    ENG = ("pe", "act", "dve", "pool", "sp")

    def __init__(self, nc):
        self.nc = nc
        self.ops = []
        self.es = ExitStack()
        self.nsem = 0

    SB_WORDS = 52736

    def _init_mem(self):
        self.sb_all = self.es.enter_context(self.nc.sbuf_tensor("sb_all", [128, self.SB_WORDS], F32))
        self.ps_all = self.es.enter_context(self.nc.psum_tensor("ps_all", [128, 4096], F32))
        self.sb_off = 0
        self.sb_peak = 0
        self.ps_off = 0

    def _carve(self, base, off, shape, dt):
        esz = 2 if dt == BF16 else 4
        n = int(np.prod(shape[1:]))
        words = (n * esz + 3) // 4
        words = (words + 7) // 8 * 8
        ap = base[0:shape[0], off:off + words]
        if dt != F32:
            ap = ap.bitcast(dt)
        ap = ap[:, 0:n]
        if len(shape) > 2:
            names = " ".join(f"d{i}" for i in range(1, len(shape)))
            kw = {f"d{i}": int(shape[i]) for i in range(1, len(shape))}
            ap = ap.rearrange(f"p ({names}) -> p {names}", **kw)
        return ap, words

    def sb(self, name, shape, dt):
        if not hasattr(self, "sb_all"):
            self._init_mem()
        ap, words = self._carve(self.sb_all, self.sb_off, shape, dt)
        self.sb_off += words
        self.sb_peak = max(self.sb_peak, self.sb_off)
        assert self.sb_off <= self.SB_WORDS, f"SBUF overflow allocating {name}: {self.sb_off * 4} bytes"
        return Res(name, ap)

    def ps(self, name, shape, dt, bank=None):
        if not hasattr(self, "sb_all"):
            self._init_mem()
        if bank is not None:
            ap, words = self._carve(self.ps_all, bank * 512, shape, dt)
            assert words <= 512
            return Res(name, ap)
        ap, words = self._carve(self.ps_all, self.ps_off, shape, dt)
        self.ps_off += (words + 511) // 512 * 512
        assert self.ps_off <= 4096, "PSUM overflow"
        return Res(name, ap)

    def scope(self):
        return _Scope(self)

    def bank(self, b, shape, dt, off=0):
        ap, words = self._carve(self.ps_all, b * 512 + off, shape, dt)
        assert off + words <= 512
        return ap

    def init_banks(self):
        if not hasattr(self, "sb_all"):
            self._init_mem()
        self.B = [Res(f"bank{b}") for b in range(8)]
        self.fz = self.sb("fz", [128, 16], BF16)
        self.fz2 = self.sb("fz2", [128, 16], F32)
        self.fence = {e: Res("fence_" + e) for e in self.ENG}
        self.dma_since = []
        self.op("pool", lambda e: e.memset(self.fz.ap, 0.0), writes=[self.fz])
        self.op("pool", lambda e: e.memset(self.fz2.ap, 0.0), writes=[self.fz2])

    def barrier(self):
        fz, fz2 = self.fz, self.fz2
        a = {}
        a["pe"] = self._add("pe", lambda e: e.matmul(self.bank(7, [1, 2], F32, off=496), fz.ap[0:1, 0:1], fz.ap[0:1, 2:4],
                                                     start=True, stop=True), [fz], [self.fence["pe"], self.B[7]])
        a["act"] = self._add("act", lambda e: e.copy(fz2.ap[0:1, 0:1], fz2.ap[0:1, 1:2]), [], [self.fence["act"]])
        a["dve"] = self._add("dve", lambda e: e.tensor_copy(fz2.ap[0:1, 2:3], fz2.ap[0:1, 3:4]), [], [self.fence["dve"]])
        a["pool"] = self._add("pool", lambda e: e.tensor_copy(fz2.ap[0:1, 4:5], fz2.ap[0:1, 5:6]), [], [self.fence["pool"]])
        extra = {d: "raw" for d in self.dma_since}
        self.dma_since = []
        fl = list(self.fence.values())
        for eng, fn in (("pe", lambda e: e.matmul(self.bank(7, [1, 2], F32, off=496), fz.ap[0:1, 0:1], fz.ap[0:1, 2:4], start=True, stop=True)),
                        ("act", lambda e: e.copy(fz2.ap[0:1, 6:7], fz2.ap[0:1, 7:8])),
                        ("dve", lambda e: e.tensor_copy(fz2.ap[0:1, 8:9], fz2.ap[0:1, 9:10])),
                        ("pool", lambda e: e.tensor_copy(fz2.ap[0:1, 10:11], fz2.ap[0:1, 11:12])),
                        ("sp", lambda e: e.nop())):
            i = self._add(eng, fn, [f for f in fl if f.lw is not None], [])
            self.ops[i]["deps"].update(extra)
            self.ops[i]["force"] = True
        for f in fl:
            f.rd = []

    def _add(self, eng, fn, reads, writes, dma=False, semkey=None, out=False):
        deps = {}
        for r in reads:
            if r.lw is not None:
                deps[r.lw] = "raw"
        for w in writes:
            if w.lw is not None:
                deps[w.lw] = "waw"
            for o in w.rd:
                deps.setdefault(o, "war")
        rec = _Rec()
        fn(rec)
        name_, a_, k_ = rec.call
        fn = lambda e: getattr(e, name_)(*a_, **k_)
        i = len(self.ops)
        for r in reads:
            r.rd.append(i)
        for w in writes:
            w.lw = i
            w.rd = []
        self.ops.append(dict(eng=eng, fn=fn, deps=deps, dma=dma, semkey=semkey, out=out, sig=dma, val=None))
        if dma and hasattr(self, "dma_since"):
            self.dma_since.append(i)
        return i

    def op(self, eng, fn, reads=(), writes=()):
        return self._add(eng, fn, list(reads), list(writes))

    def act(self, out, in_, func, reads=(), writes=(), **kw):
        return self._add("act", lambda e: e.activation(out, in_, func, **kw), list(reads), list(writes))

    def dma(self, q, dst, src, reads=(), writes=(), semkey=None, out=False, **kw):
        if semkey is None:
            semkey = ("w", id(writes[0])) if writes else ("r", id(reads[0]))
        return self._add(q, lambda e: e.dma_start(dst, src, **kw), list(reads), list(writes), dma=True,
                         semkey=semkey, out=out)

    def dma_fn(self, q, fn, reads=(), writes=(), semkey=None, out=False):
        if semkey is None:
            semkey = ("w", id(writes[0])) if writes else ("r", id(reads[0]))
        return self._add(q, fn, list(reads), list(writes), dma=True, semkey=semkey, out=out)

    def finish(self):
        nc = self.nc
        ops = self.ops
        for i, o in enumerate(ops):
            need = {}
            for d, kind in o["deps"].items():
                od = ops[d]
                if not od["dma"] and od["eng"] == o["eng"] and not o["dma"]:
                    if o["eng"] == "pe" or kind == "war" or o.get("force"):
                        continue
                need[d] = kind
                od["sig"] = True
            o["need"] = need
        semkeys = {}
        counts = {}
        for o in ops:
            if not o["sig"]:
                continue
            key = o["semkey"] if o["dma"] else ("eng", o["eng"])
            o["key"] = key
            if key not in semkeys:
                semkeys[key] = None
            counts[key] = counts.get(key, 0) + (16 if o["dma"] else 1)
            o["val"] = counts[key]
        assert len(semkeys) <= 100, f"too many semaphores {len(semkeys)}"
        for k in semkeys:
            semkeys[k] = self.es.enter_context(nc.semaphore(f"s{len([1 for v in semkeys.values() if v is not None])}"))
        self.nsem = len(semkeys)
        streams = {e: [] for e in self.ENG}
        for i, o in enumerate(ops):
            streams[o["eng"]].append(i)
        final_waits = {}
        for o in ops:
            if o["dma"] and o["out"]:
                final_waits[o["key"]] = (o["eng"], counts[o["key"]])

        def emit(engname, eng):
            waited = {}
            for i in streams[engname]:
                o = ops[i]
                wl = {}
                for d in o["need"]:
                    od = ops[d]
                    k = od["key"]
                    if od["val"] > wl.get(k, 0):
                        wl[k] = od["val"]
                for k, v in wl.items():
                    if waited.get(k, 0) >= v:
                        continue
                    eng.wait_ge(semkeys[k], v)
                    waited[k] = v
                ins = o["fn"](eng)
                if o["sig"]:
                    ins.then_inc(semkeys[o["key"]], 16 if o["dma"] else 1)
            for k, (qe, v) in final_waits.items():
                if qe == engname and waited.get(k, 0) < v:
                    eng.wait_ge(semkeys[k], v)

        with nc.Block() as block:
            @block.tensor
            def _(e):
                emit("pe", e)

            @block.scalar
            def _(e):
                emit("act", e)

            @block.vector
            def _(e):
                emit("dve", e)

            @block.gpsimd
            def _(e):
                emit("pool", e)

            @block.sync
            def _(e):
                emit("sp", e)
        self.es.close()


class _Scope:
    def __init__(self, P):
        self.P = P

    def __enter__(self):
        if not hasattr(self.P, "sb_all"):
            self.P._init_mem()
        self.saved = self.P.sb_off
        return self

    def __exit__(self, *a):
        self.P.sb_off = self.saved
        if a[0] is None:
            self.P.barrier()
        return False


D = 2048
NH, HD, G = 16, 64, 4
DR = 1024
EPS = 1e-6
SCALE = HD ** -0.5
NBLK = 32
NS = 32
C_Q, C_KV, C_GN, C_GA, C_XR, C_GR, C_MA, C_MR = 0, 1024, 2560, 2608, 3632, 4656, 5680, 7728
NEGB = -30000.0


class Ctx:
    pass


def dram_in(nc, name, shape, dt=F32):
    return nc.dram_tensor(name, list(shape), dt, kind="ExternalInput").ap()


def dram_out(nc, name, shape, dt=F32):
    return nc.dram_tensor(name, list(shape), dt, kind="ExternalOutput").ap()


def bc(ap, shape, axis):
    return ap.unsqueeze(axis).broadcast_to(list(shape))


def norm_T(P, c, blk, n, keep_x=False):
    t = c.nt[blk % 2]
    xt, xn, ss, xnT = c.nt[0]["xt"], t["xn"], t["ss"], t["xnT"]
    P.dma("sp", xt.ap[0:n, :], c.xv[blk * 128:blk * 128 + n, :], writes=[xt])
    P.act(xn.ap[0:n, :], xt.ap[0:n, :], AF.Square, accum_out=ss.ap[0:n, 0:1], reads=[xt], writes=[xn, ss])
    P.op("dve", lambda e: e.tensor_scalar(ss.ap[0:n, 1:2], ss.ap[0:n, 0:1], 1.0 / D, EPS, ALU.mult, ALU.add), [ss], [ss])
    P.act(ss.ap[0:n, 2:3], ss.ap[0:n, 1:2], AF.Sqrt, reads=[ss], writes=[ss])
    P.op("dve", lambda e: e.reciprocal(ss.ap[0:n, 3:4], ss.ap[0:n, 2:3]), [ss], [ss])
    xs = t["xs"] if keep_x else xt
    P.act(xs.ap[0:n, :], xt.ap[0:n, :], AF.Copy, scale=ss.ap[0:n, 3:4], reads=[xt, ss], writes=[xs])
    P.op("dve", lambda e: e.tensor_tensor(xn.ap[0:n, :], xs.ap[0:n, :], c.g_rep.ap[0:n, :], ALU.mult), [xs, c.g_rep], [xn])
    for h in range(2):
        bk = P.B[h]
        for k in range(8):
            kk = h * 8 + k
            P.op("pe", lambda e, kk=kk, k=k, h=h: e.transpose(P.bank(h, [128, 8, 128], BF16)[:, k, 0:n],
                                                          xn.ap[0:n, kk * 128:(kk + 1) * 128], c.ident.ap[0:n, 0:n]),
                 [xn, c.ident], [bk])
        eng = "act" if h == 0 else "dve"
        if eng == "act":
            P.op("act", lambda e, h=h: e.copy(xnT.ap[:, h * 8:(h + 1) * 8, 0:n], P.bank(h, [128, 8, 128], BF16)[:, :, 0:n]), [bk], [xnT])
        else:
            P.op("dve", lambda e, h=h: e.tensor_copy(xnT.ap[:, h * 8:(h + 1) * 8, 0:n], P.bank(h, [128, 8, 128], BF16)[:, :, 0:n]), [bk], [xnT])
    return xnT, xt


def rms_rows(P, c, src_ap, n, ng, gain_ap, dst_ap, tmp, reads, writes):
    sq, st = tmp["sq"], tmp["st"]
    P.act(sq.ap[0:n, 0:ng * 64], src_ap, AF.Square, reads=reads, writes=[sq])
    P.op("dve", lambda e: e.tensor_reduce(st.ap[0:n, 0:ng], sq.ap[0:n, 0:ng * 64].rearrange("p (g d) -> p g d", g=ng), AX.X, ALU.add), [sq], [st])
    P.op("dve", lambda e: e.tensor_scalar(st.ap[0:n, 16:16 + ng], st.ap[0:n, 0:ng], 1.0 / HD, EPS, ALU.mult, ALU.add), [st], [st])
    P.act(st.ap[0:n, 32:32 + ng], st.ap[0:n, 16:16 + ng], AF.Sqrt, reads=[st], writes=[st])
    P.op("dve", lambda e: e.reciprocal(st.ap[0:n, 48:48 + ng], st.ap[0:n, 32:32 + ng]), [st], [st])
    s3 = src_ap.rearrange("p (g d) -> p g d", g=ng)
    d3 = dst_ap.rearrange("p (g d) -> p g d", g=ng)
    P.op("dve", lambda e: e.tensor_tensor(d3, s3, bc(st.ap[0:n, 48:48 + ng], [n, ng, 64], 2), ALU.mult), list(reads) + [st], list(writes))
    P.op("dve", lambda e: e.tensor_tensor(d3, d3, bc(gain_ap[0:n, :], [n, ng, 64], 1), ALU.mult), list(writes) + [c.kg_rep, c.qg_rep], list(writes))


def pass_k1(P, c):
    nc = P.nc
    with P.scope():
        wkv = P.sb("wkv", [128, 16, 1536], BF16)
        for k in range(16):
            P.dma("pool", wkv.ap[:, k, :], c.w_in[k * 128:(k + 1) * 128, C_KV:C_KV + 1536], writes=[wkv], semkey="wload")
        kvc = [P.sb(f"kvc{i}", [128, 512], F32) for i in range(1)] * 2
        kvw = [P.sb(f"kvw{i}", [128, 512], F32) for i in range(1)] * 2
        rows = [[P.sb(f"rows{b}", [128, 512], F32)] for b in range(2)]
        knb2 = [[P.sb(f"knb{b}{i}", [128, 2, 2, 64], BF16) for i in range(2)] for b in range(2)]
        tmp = dict(sq=P.sb("k1sq", [128, 256], F32), st=P.sb("k1st", [128, 64], F32))
        nxt = norm_T(P, c, 0, 128)
        for blk in range(NBLK + 1):
            n = 128 if blk < NBLK else NS
            own = (blk % 4 == 3) and blk < NBLK
            smp = blk == NBLK
            xnT, _ = nxt
            if blk + 1 <= NBLK:
                nxt = norm_T(P, c, blk + 1, 128 if blk + 1 < NBLK else NS)
            for br in range(3):
                bk = P.B[2 + br]
                for k in range(16):
                    P.op("pe", lambda e, k=k, br=br: e.matmul(P.bank(2 + br, [128, 512], F32)[0:n, :], xnT.ap[:, k, 0:n],
                                                            wkv.ap[:, k, br * 512:(br + 1) * 512], start=(k == 0), stop=(k == 15)),
                         [xnT, wkv], [bk])
            kc = kvc[blk % 2]
            P.op("act", lambda e: e.copy(kc.ap[0:n, :], P.bank(2, [128, 512], F32)[0:n, :]), [P.B[2]], [kc])
            if own:
                P.dma("pool", c.o_cmp[blk // 4], kc.ap, reads=[kc], semkey="ocmp", out=True)
            if smp:
                P.dma("pool", c.s_cmp, kc.ap[0:n, :], reads=[kc], semkey="ocmp", out=True)
            else:
                kw = kvw[blk % 2]
                P.op("dve", lambda e: e.tensor_tensor(kw.ap, kc.ap, c.wc_rep.ap, ALU.mult), [kc, c.wc_rep], [kw])
                P.op("pe", lambda e, blk=blk: e.matmul(P.bank(5, [128, 512], F32), c.indw.ap[:, 124 - 4 * blk:252 - 4 * blk], kw.ap,
                                                      start=(blk == 0), stop=(blk == NBLK - 1)), [c.indw, kw], [P.B[5]])
            for b, br in ((0, 1), (1, 2)):
                bkr = P.B[2 + br]
                bap = P.bank(2 + br, [128, 512], F32)
                rw = rows[b][0]
                rms_rows(P, c, bap[0:n, 0:256], n, 4, c.kg_rep.ap[:, br, :], rw.ap[0:n, 0:256], tmp, [bkr], [rw])
                P.op("act", lambda e, rw=rw, bap=bap: e.copy(rw.ap[0:n, 256:512], bap[0:n, 256:512]), [bkr], [rw])
                if b == 0:
                    if own:
                        P.dma("pool", c.o_sel[blk // 4], rw.ap, reads=[rw], semkey="osel", out=True)
                    if smp:
                        P.dma("pool", c.s_sel, rw.ap[0:n, :], reads=[rw], writes=[c.r_ssel], semkey="osel", out=True)
                else:
                    if blk == NBLK - 1:
                        P.dma("pool", c.o_win, rw.ap, reads=[rw], semkey="owin", out=True)
                    if smp:
                        P.dma("pool", c.s_win, rw.ap[0:n, :], reads=[rw], writes=[c.r_swin], semkey="owin", out=True)
                kb = knb2[b][blk % 2]
                P.op("pool", lambda e, kb=kb, rw=rw: e.tensor_copy(kb.ap[0:n].rearrange("p gi hf d -> p hf gi d"),
                                                                 rw.ap[0:n, 0:256].rearrange("p (hf gi d) -> p hf gi d", hf=2, gi=2)), [rw], [kb])
                V1 = c.Vsel1 if b == 0 else c.Vwin1
                P.op("pool", lambda e, V1=V1, rw=rw, blk=blk: e.tensor_copy(V1.ap[0:n, blk, :, 0:64], rw.ap[0:n, 256:512].rearrange("p (g d) -> p g d", g=4)), [rw], [V1])
                for gi in range(2):
                    tb = 6 + gi
                    P.op("pe", lambda e, kb=kb, gi=gi, tb=tb: e.transpose(P.bank(tb, [128, 128], BF16)[:, 0:n], kb.ap[0:n, gi].rearrange("p hf d -> p (hf d)"),
                                                                     c.ident.ap[0:n, 0:n]), [kb, c.ident], [P.B[tb]])
                    cols = slice(blk * 128, blk * 128 + n)
                    if b == 0:
                        P.op("act", lambda e, gi=gi, tb=tb, cols=cols: e.copy(c.Ksel.ap[0:64, gi, cols], P.bank(tb, [128, 128], BF16)[0:64, 0:n]), [P.B[tb]], [c.Ksel])
                        P.op("dve", lambda e, gi=gi, tb=tb, cols=cols: e.tensor_copy(c.Ksel.ap[64:128, 2 + gi, cols], P.bank(tb, [128, 128], BF16)[64:128, 0:n]), [P.B[tb]], [c.Ksel])
                    else:
                        P.op("act", lambda e, gi=gi, tb=tb, cols=cols: e.copy(c.Kwin.ap[:, gi, cols], P.bank(tb, [128, 128], BF16)[:, 0:n]), [P.B[tb]], [c.Kwin])
        kcr = rows[0][0]
        P.op("act", lambda e: e.copy(kcr.ap, P.bank(5, [128, 512], F32)), [P.B[5]], [kcr])
        kcn = rows[1][0]
        rms_rows(P, c, kcr.ap[:, 0:256], 128, 4, c.kg_rep.ap[:, 0, :], kcn.ap[:, 0:256], tmp, [kcr], [kcn])
        kb = knb2[0][0]
        P.op("pool", lambda e: e.tensor_copy(kb.ap.rearrange("p gi hf d -> p hf gi d"), kcn.ap[:, 0:256].rearrange("p (hf gi d) -> p hf gi d", hf=2, gi=2)), [kcn], [kb])
        P.op("pool", lambda e: e.tensor_copy(c.Vc1.ap[:, :, 0:64], kcr.ap[:, 256:512].rearrange("p (g d) -> p g d", g=4)), [kcr], [c.Vc1])
        for gi in range(2):
            tb = 6 + gi
            P.op("pe", lambda e, gi=gi, tb=tb: e.transpose(P.bank(tb, [128, 128], BF16), kb.ap[:, gi].rearrange("p hf d -> p (hf d)"), c.ident.ap), [kb, c.ident], [P.B[tb]])
            P.op("act", lambda e, gi=gi, tb=tb: e.copy(c.KcT.ap[:, gi, :], P.bank(tb, [128, 128], BF16)), [P.B[tb]], [c.KcT])
    P.barrier()


def bcn(ap, shape):
    a = ap
    for ax in range(2, len(shape)):
        a = a.unsqueeze(ax)
    return a.broadcast_to(list(shape))


def pass_r(P, c):
    with P.scope():
        wxr = P.sb("wxr", [128, 16, 1024], BF16)
        for k in range(16):
            P.dma("pool", wxr.ap[:, k, :], c.w_in[k * 128:(k + 1) * 128, C_XR:C_XR + 1024], writes=[wxr], semkey="wload")
        Wg = [P.sb("Wrg", [128, 8, 128], BF16), P.sb("Wig", [128, 8, 128], BF16)]
        P.dma("pool", Wg[0].ap, c.d_wrg, writes=[Wg[0]])
        P.dma("pool", Wg[1].ap, c.d_wig, writes=[Wg[1]])
        cw = P.sb("cw", [128, 8, 4], F32)
        sm = P.sb("rsm", [128, 5, 8], F32)
        vrow = P.sb("vrow", [128, 384], F32)
        h0T = P.sb("h0T", [128, 8, 4], F32)
        XRs = P.sb("XRs", [128, 8, 4, 11], F32)
        XRb = [P.sb(f"XR{i}", [128, 8, 131], F32) for i in range(2)]
        ST = P.sb("ST", [128, 8, 16], F32)
        rowsT = P.sb("rowsT", [16, 1024], F32)
        identf = c.identf
        P.dma("sp", cw.ap, c.d_cw, writes=[cw])
        P.dma("sp", sm.ap[:, 0:4, :], c.d_rsm, writes=[sm])
        P.dma("sp", vrow.ap, c.d_vrow, writes=[vrow])
        P.dma("sp", h0T.ap, c.d_h0T, writes=[h0T])
        P.dma("sp", XRs.ap[:, :, :, 0:3], c.d_sconvT, writes=[XRs])
        for XR_ in XRb:
            P.op("pool", lambda e: e.memset(XR_.ap, 0.0), [], [XR_])
        P.act(sm.ap[:, 4, :], sm.ap[:, 3, :], AF.Exp, scale=-1.0, reads=[sm], writes=[sm])
        P.op("dve", lambda e: e.tensor_scalar_add(sm.ap[:, 4, :], sm.ap[:, 4, :], 1.0), [sm], [sm])
        P.act(sm.ap[:, 4, :], sm.ap[:, 4, :], AF.Ln, reads=[sm], writes=[sm])
        P.op("dve", lambda e: e.tensor_scalar_mul(sm.ap[:, 4, :], sm.ap[:, 4, :], -8.0), [sm], [sm])
        xc = P.sb("xc", [128, 8, 128], F32)
        xcb = P.sb("xcb", [128, 8, 128], BF16)
        r = P.sb("rr", [128, 8, 128], F32)
        ig = P.sb("ig", [128, 8, 128], F32)
        a = P.sb("aa", [128, 8, 128], F32)
        t1 = P.sb("t1", [128, 8, 128], F32)
        hs = [P.sb(f"hs{i}", [128, 8, 128], F32) for i in range(2)]
        def stage_a(blk, xnT):
            smp = blk == NBLK
            n = NS if smp else 128
            XR = XRb[blk % 2]
            for cch in range(8):
                bk = 2 + cch // 4
                for k in range(16):
                    P.op("pe", lambda e: e.matmul(P.bank(bk, [128, 4, 128], F32)[:, cch % 4, 0:n], wxr.ap[:, k, cch * 128:(cch + 1) * 128],
                                                  xnT.ap[:, k, 0:n], start=(k == 0), stop=(k == 15)), [wxr, xnT], [P.B[bk]])
            if not smp:
                if blk > 0:
                    XRp = XRb[(blk + 1) % 2]
                    P.op("pool", lambda e: e.tensor_copy(XR.ap[:, :, 0:3], XRp.ap[:, :, 128:131]), [XRp], [XR])
                for h2 in range(2):
                    P.op("act", lambda e: e.copy(XR.ap[:, 4 * h2:4 * h2 + 4, 3:131], P.bank(2 + h2, [128, 4, 128], F32)), [P.B[2 + h2]], [XR])
            else:
                for h2 in range(2):
                    P.op("act", lambda e: e.copy(XRs.ap[:, 4 * h2:4 * h2 + 4, :, 3:11],
                                                 P.bank(2 + h2, [128, 4, 128], F32)[:, :, 0:n].rearrange("p c (b t) -> p c b t", b=4)), [P.B[2 + h2]], [XRs])

        def stage_b(blk):
            smp = blk == NBLK
            n = NS if smp else 128
            XR = XRb[blk % 2]
            if not smp:
                src = lambda k: XR.ap[:, :, k:k + 128]
                shp = [128, 8, 128]
                vw = lambda T: T.ap
                XRc = XR
            else:
                src = lambda k: XRs.ap[:, :, :, k:k + 8]
                shp = [128, 8, 4, 8]
                vw = lambda T: T.ap[:, :, 0:n].rearrange("p c (b t) -> p c b t", b=4)
                XRc = XRs
            P.op("pool", lambda e: e.tensor_tensor(vw(xc), src(3), bcn(cw.ap[:, :, 3], shp), ALU.mult), [XRc, cw], [xc])
            for k in (2, 1, 0):
                P.op("pool", lambda e: e.tensor_tensor(vw(t1), src(k), bcn(cw.ap[:, :, k], shp), ALU.mult), [XRc, cw], [t1])
                P.op("dve", lambda e: e.tensor_tensor(vw(xc), vw(xc), vw(t1), ALU.add), [xc, t1], [xc])
            P.op("dve", lambda e: e.tensor_tensor(vw(xc), vw(xc), bcn(sm.ap[:, 0, :], shp), ALU.add), [xc, sm], [xc])
            P.op("act", lambda e: e.copy(xcb.ap[:, :, 0:n], xc.ap[:, :, 0:n]), [xc], [xcb])
            for gi_ in range(2):
                for cch in range(8):
                    bk = 4 + 2 * gi_ + cch // 4
                    P.op("pe", lambda e: e.matmul(P.bank(bk, [128, 4, 128], F32)[:, cch % 4, 0:n], Wg[gi_].ap[:, cch, :], xcb.ap[:, cch, 0:n],
                                                  start=True, stop=True), [Wg[gi_], xcb], [P.B[bk]])
                dst = r if gi_ == 0 else ig
                for cch in range(8):
                    bk = 4 + 2 * gi_ + cch // 4
                    P.act(dst.ap[:, cch, 0:n], P.bank(bk, [128, 4, 128], F32)[:, cch % 4, 0:n], AF.Sigmoid, bias=sm.ap[:, 1 + gi_, cch:cch + 1],
                          reads=[P.B[bk], sm], writes=[dst])
            P.op("dve", lambda e: e.tensor_tensor(a.ap[:, :, 0:n], r.ap[:, :, 0:n], bcn(sm.ap[:, 4, :], [128, 8, n]), ALU.mult), [r, sm], [a])
            P.act(a.ap[:, :, 0:n], a.ap[:, :, 0:n], AF.Exp, reads=[a], writes=[a])
            P.op("pool", lambda e: e.tensor_tensor(t1.ap[:, :, 0:n], a.ap[:, :, 0:n], a.ap[:, :, 0:n], ALU.mult), [a], [t1])
            P.op("dve", lambda e: e.tensor_scalar(t1.ap[:, :, 0:n], t1.ap[:, :, 0:n], -1.0, 1.0, ALU.mult, ALU.add), [t1], [t1])
            P.act(t1.ap[:, :, 0:n], t1.ap[:, :, 0:n], AF.Sqrt, reads=[t1], writes=[t1])
            P.op("pool", lambda e: e.tensor_tensor(ig.ap[:, :, 0:n], ig.ap[:, :, 0:n], xc.ap[:, :, 0:n], ALU.mult), [ig, xc], [ig])
            P.op("dve", lambda e: e.tensor_tensor(t1.ap[:, :, 0:n], t1.ap[:, :, 0:n], ig.ap[:, :, 0:n], ALU.mult), [t1, ig], [t1])
            if blk < 3:
                P.op("dve", lambda e: e.tensor_tensor(t1.ap, t1.ap, bc(vrow.ap[:, blk * 128:(blk + 1) * 128], [128, 8, 128], 1), ALU.mult), [t1, vrow], [t1])
            hc, hp = hs[blk % 2], hs[(blk + 1) % 2]
            if not smp:
                for cch in range(8):
                    init = 0.0 if blk == 0 else hp.ap[:, cch, 127:128]
                    P.op("dve", lambda e: e.tensor_tensor_scan(hc.ap[:, cch, :], a.ap[:, cch, :], t1.ap[:, cch, :], init, ALU.mult, ALU.add),
                         [a, t1, hp], [hc])
                if blk % 4 == 3:
                    i = blk // 4
                    P.op("pool", lambda e: e.tensor_copy(c.hs_own.ap[:, :, i * 128:(i + 1) * 128], hc.ap), [hc], [c.hs_own])
                if blk == NBLK - 1:
                    P.op("pool", lambda e: e.tensor_copy(ST.ap[:, :, 0:1], hc.ap[:, :, 127:128]), [hc], [ST])
                    P.op("pool", lambda e: e.tensor_copy(ST.ap[:, :, 1:4], XR.ap[:, :, 128:131]), [XR], [ST])
                    for cch in range(8):
                        P.op("pe", lambda e: e.transpose(P.bank(2 + cch // 4, [16, 4, 128], F32)[0:4, cch % 4, :], ST.ap[:, cch, 0:4], identf.ap), [ST, identf], [P.B[2 + cch // 4]])
                    for h2 in range(2):
                        P.op("act", lambda e: e.copy(rowsT.ap[0:4, h2 * 512:(h2 + 1) * 512], P.bank(2 + h2, [16, 512], F32)[0:4, :]), [P.B[2 + h2]], [rowsT])
                    P.dma("pool", c.o_hp, rowsT.ap[0:1, :], reads=[rowsT], semkey="orn", out=True)
                    P.dma("pool", c.o_convp, rowsT.ap[1:4, :], reads=[rowsT], semkey="orn", out=True)
            else:
                for cch in range(8):
                    for b in range(4):
                        P.op("dve", lambda e: e.tensor_tensor_scan(hc.ap[:, cch, b * 8:(b + 1) * 8], a.ap[:, cch, b * 8:(b + 1) * 8],
                                                                   t1.ap[:, cch, b * 8:(b + 1) * 8], h0T.ap[:, cch, b:b + 1], ALU.mult, ALU.add),
                             [a, t1, h0T], [hc])
                P.op("pool", lambda e: e.tensor_copy(c.hs_own.ap[:, :, 1024:1024 + NS], hc.ap[:, :, 0:NS]), [hc], [c.hs_own])
                P.op("pool", lambda e: e.tensor_copy(ST.ap[:, :, 0:4], hc.ap[:, :, 0:NS].rearrange("p c (b t) -> p c b t", b=4)[:, :, :, 7]), [hc], [ST])
                P.op("pool", lambda e: e.tensor_copy(ST.ap[:, :, 4:16].rearrange("p c (b t) -> p c b t", b=4), XRs.ap[:, :, :, 8:11]), [XRs], [ST])
                for cch in range(8):
                    P.op("pe", lambda e: e.transpose(P.bank(2 + cch // 4, [16, 4, 128], F32)[0:16, cch % 4, :], ST.ap[:, cch, 0:16], identf.ap), [ST, identf], [P.B[2 + cch // 4]])
                for h2 in range(2):
                    P.op("act", lambda e: e.copy(rowsT.ap[0:16, h2 * 512:(h2 + 1) * 512], P.bank(2 + h2, [16, 512], F32)[0:16, :]), [P.B[2 + h2]], [rowsT])
                P.dma("pool", c.o_hs, rowsT.ap[0:4, :], reads=[rowsT], semkey="orn", out=True)
                P.dma("pool", c.o_convs.rearrange("b t c -> (b t) c"), rowsT.ap[4:16, :], reads=[rowsT], semkey="orn", out=True)

        nxt = norm_T(P, c, 0, 128)
        for blk in range(NBLK + 1):
            cur = nxt
            if blk + 1 <= NBLK:
                nxt = norm_T(P, c, blk + 1, 128 if blk + 1 < NBLK else NS)
            stage_a(blk, cur[0])
            if blk >= 1:
                stage_b(blk - 1)
        stage_b(NBLK)
    P.barrier()


def qproj(P, c, blk, n, wqs, qf, gts, tmp, resident=False):
    xnT, _ = norm_T(P, c, blk, n)
    for part, (c0, ncol, bk, r0) in enumerate(((C_Q, 512, 2, 0), (C_Q + 512, 512, 3, 512), (C_GN, 48, 4, 1024))):
        if not resident:
            r0 = 0
            for k in range(16):
                P.dma("pool", wqs.ap[:, k, 0:ncol], c.w_in[k * 128:(k + 1) * 128, c0:c0 + ncol], writes=[wqs], semkey="wload")
        for k in range(16):
            P.op("pe", lambda e: e.matmul(P.bank(bk, [128, 512], F32)[0:n, 0:ncol], xnT.ap[:, k, 0:n], wqs.ap[:, k, r0:r0 + ncol],
                                          start=(k == 0), stop=(k == 15)), [xnT, wqs], [P.B[bk]])
    for hh in range(2):
        rms_rows(P, c, P.bank(2 + hh, [128, 512], F32)[0:n, :], n, 8, c.qg_rep.ap, qf.ap[0:n, hh * 512:(hh + 1) * 512], tmp, [P.B[2 + hh]], [qf])
    P.act(gts.ap[0:n, :], P.bank(4, [128, 512], F32)[0:n, 0:48], AF.Sigmoid, reads=[P.B[4]], writes=[gts])


def combine(P, c, oT_bank, g, br, gts, oatt, tmp, n=128, first=False, tb=7):
    oTs, cf, otmp = tmp["oTs"], tmp["cf"], tmp["otmp"]
    W = 4 * n
    P.op("act", lambda e: e.copy(oTs.ap[0:65, 0:W], P.bank(oT_bank, [128, 512], F32)[0:65, 0:W]), [P.B[oT_bank]], [oTs])
    for r in range(4):
        P.op("pe", lambda e: e.transpose(P.bank(tb, [128, 4, 65], F32)[0:n, r, :], oTs.ap[0:65, r * n:(r + 1) * n], c.identf.ap[0:65, 0:65]),
             [oTs, c.identf], [P.B[tb]])
    O = P.bank(tb, [128, 4, 65], F32)
    P.op("dve", lambda e: e.tensor_scalar_max(cf.ap[0:n, 0:4], O[0:n, :, 64], 1e-30), [P.B[tb]], [cf])
    P.op("dve", lambda e: e.reciprocal(cf.ap[0:n, 4:8], cf.ap[0:n, 0:4]), [cf], [cf])
    P.op("dve", lambda e: e.tensor_tensor(cf.ap[0:n, 8:12], cf.ap[0:n, 4:8], gts.ap[0:n, br * 16 + g * 4:br * 16 + g * 4 + 4], ALU.mult), [cf, gts], [cf])
    dst = oatt.ap[0:n, g * 256:(g + 1) * 256].rearrange("p (r d) -> p r d", r=4)
    if first:
        P.op("dve", lambda e: e.tensor_tensor(dst, O[0:n, :, 0:64], bc(cf.ap[0:n, 8:12], [n, 4, 64], 2), ALU.mult), [P.B[tb], cf], [oatt])
    else:
        ot = otmp.ap[0:n, :].rearrange("p (r d) -> p r d", r=4)
        P.op("dve", lambda e: e.tensor_tensor(ot, O[0:n, :, 0:64], bc(cf.ap[0:n, 8:12], [n, 4, 64], 2), ALU.mult), [P.B[tb], cf], [otmp])
        P.op("pool", lambda e: e.tensor_tensor(dst, dst, ot, ALU.add), [oatt, otmp], [oatt])


def pass_k2(P, c):
    with P.scope():
        wqs = P.sb("wqs", [128, 16, 1072], BF16)
        for k in range(16):
            P.dma("pool", wqs.ap[:, k, 0:1024], c.w_in[k * 128:(k + 1) * 128, C_Q:C_Q + 1024], writes=[wqs], semkey="wload")
            P.dma("pool", wqs.ap[:, k, 1024:1072], c.w_in[k * 128:(k + 1) * 128, C_GN:C_GN + 48], writes=[wqs], semkey="wload")
        qf = P.sb("qf", [128, 1024], F32)
        gts = P.sb("gts", [128, 48], F32)
        tmp = dict(sq=P.sb("k2sq", [128, 512], F32), st=P.sb("k2st", [128, 64], F32), oTs=P.sb("oTs", [128, 512], F32),
                   cf=P.sb("cf", [128, 16], F32), otmp=P.sb("otmp", [128, 256], F32))
        QTin = P.sb("QTin", [128, 8, 128], BF16)
        BTin = P.sb("BTin", [128, 2, 128], BF16)
        QE = P.sb("QE", [128, 4, 4, 128], BF16)
        Ecm = tmp["sq"]
        Ecm3 = tmp["sq"].ap.rearrange("p (r n) -> p r n", r=4)
        impn = P.sb("impn", [128, 128], F32)
        sc = P.sb("sc", [128, 4, 64], F32)
        sc2 = P.sb("sc2", [128, 4, 64], F32)
        m8 = P.sb("m8", [128, 4, 24], F32)
        PT = [P.sb(f"PT{i}", [128, 512], BF16) for i in range(2)]
        oatt = P.sb("oatt", [128, 1024], F32)
        cmask = P.sb("cmask", [128, 8, 128], BF16)
        cmaskT = P.sb("cmaskT", [128, 8, 128], BF16)
        candm = P.sb("candm", [128, 8, 64], F32)
        addm = P.sb("addm", [128, 8, 64], F32)
        CB = P.sb("CB", [128, 2, 512], BF16)
        for dst, src in ((cmask, c.d_cmask), (cmaskT, c.d_cmaskT), (candm, c.d_candm), (addm, c.d_addm), (CB, c.d_CB)):
            P.dma("sp", dst.ap, src, writes=[dst])
        pend = []

        def combine_later(*args, **kw):
            if pend:
                a_, k_ = pend.pop()
                combine(*a_, **k_)
            pend.append((args, kw))

        def combine_flush():
            while pend:
                a_, k_ = pend.pop(0)
                combine(*a_, **k_)

        for i in range(8):
            v = 4 * i + 3
            n = 128
            qproj(P, c, v, n, wqs, qf, gts, tmp, resident=True)
            q4 = qf.ap.rearrange("p (hf gi r d) -> p hf gi r d", hf=2, gi=2, r=4)
            P.op("pool", lambda e: e.tensor_copy(QTin.ap.rearrange("p (gi r) (hf d) -> p hf gi r d", gi=2, hf=2), q4), [qf], [QTin])
            for t8 in range(8):
                P.op("pe", lambda e: e.transpose(P.bank(0, [128, 8, 128], BF16)[:, t8, :], QTin.ap[:, t8, :], c.ident.ap), [QTin, c.ident], [P.B[0]])
            QT = P.bank(0, [128, 8, 128], BF16)
            for gi in range(2):
                P.op("act", lambda e: e.copy(QE.ap[0:64, gi, :, :], QT[0:64, gi * 4:(gi + 1) * 4, :]), [P.B[0]], [QE])
                P.op("dve", lambda e: e.tensor_copy(QE.ap[64:128, 2 + gi, :, :], QT[64:128, gi * 4:(gi + 1) * 4, :]), [P.B[0]], [QE])
            for g in range(4):
                hf, gi = g // 2, g % 2
                hs_ = slice(hf * 64, hf * 64 + 64)
                for r in range(4):
                    P.op("pe", lambda e: e.matmul(P.bank(5, [128, 4, 128], F32)[:, r, :], QE.ap[hs_, g, r, :], c.KcT.ap[hs_, gi, :], start=True, stop=True),
                         [QE, c.KcT], [P.B[5]])
                P.act(Ecm3, P.bank(5, [128, 4, 128], F32), AF.Exp, scale=SCALE, reads=[P.B[5]], writes=[Ecm])
                P.op("dve", lambda e: e.tensor_tensor(Ecm3, Ecm3, bc(cmask.ap[:, i, :], [128, 4, 128], 1), ALU.mult), [Ecm, cmask], [Ecm])
                P.op("dve", lambda e: e.tensor_reduce(m8.ap[:, 0, 16:20], Ecm3, AX.X, ALU.add), [Ecm], [m8])
                P.op("dve", lambda e: e.tensor_scalar_max(m8.ap[:, 0, 16:20], m8.ap[:, 0, 16:20], 1e-30), [m8], [m8])
                P.op("dve", lambda e: e.reciprocal(m8.ap[:, 0, 20:24], m8.ap[:, 0, 16:20]), [m8], [m8])
                P.op("dve", lambda e: e.tensor_tensor(Ecm3, Ecm3, bc(m8.ap[:, 0, 20:24], [128, 4, 128], 2), ALU.mult), [Ecm, m8], [Ecm])
                P.op("dve", lambda e: e.tensor_reduce(impn.ap, Ecm3.rearrange("p r n -> p n r"), AX.X, ALU.add), [Ecm], [impn])
                P.op("dve", lambda e: e.tensor_reduce(sc.ap[:, g, :], impn.ap.rearrange("p (j two) -> p j two", two=2), AX.X, ALU.add), [impn], [sc])
                sb_ = 2 + g % 2
                P.op("pe", lambda e: e.matmul(P.bank(sb_, [128, 512], F32), c.KcT.ap[hs_, gi, :], QE.ap[hs_, g, :, :].rearrange("p r q -> p (r q)"),
                                              start=True, stop=True), [c.KcT, QE], [P.B[sb_]])
                pt = PT[g % 2]
                P.act(pt.ap, P.bank(sb_, [128, 512], F32), AF.Exp, scale=SCALE, reads=[P.B[sb_]], writes=[pt])
                P.op("dve", lambda e: e.tensor_tensor(pt.ap.rearrange("p (r q) -> p r q", r=4), pt.ap.rearrange("p (r q) -> p r q", r=4),
                                                      bc(cmaskT.ap[:, i, :], [128, 4, 128], 1), ALU.mult), [pt, cmaskT], [pt])
                ob = 4 if g % 2 == 0 else 6
                P.op("pe", lambda e: e.matmul(P.bank(ob, [128, 512], F32)[0:65, :], c.Vc1.ap[:, g, :], pt.ap, start=True, stop=True), [c.Vc1, pt], [P.B[ob]])
                combine_later(P, c, ob, g, 0, gts, oatt, tmp, first=True)
            P.op("dve", lambda e: e.tensor_tensor(sc.ap, sc.ap, bc(candm.ap[:, i, :], [128, 4, 64], 1), ALU.mult), [sc, candm], [sc])
            P.op("dve", lambda e: e.tensor_tensor(sc.ap, sc.ap, bc(addm.ap[:, i, :], [128, 4, 64], 1), ALU.add), [sc, addm], [sc])
            for g in range(4):
                P.op("dve", lambda e: e.max(m8.ap[:, g, 0:8], sc.ap[:, g, :]), [sc], [m8])
                P.op("dve", lambda e: e.match_replace(sc2.ap[:, g, :], m8.ap[:, g, 0:8], sc.ap[:, g, :], -2.0), [sc, m8], [sc2])
                P.op("dve", lambda e: e.max(m8.ap[:, g, 8:16], sc2.ap[:, g, :]), [sc2], [m8])
            P.op("dve", lambda e: e.tensor_scalar_max(m8.ap[:, :, 16:17], m8.ap[:, :, 15:16], 0.0), [m8], [m8])
            P.op("dve", lambda e: e.tensor_tensor(sc2.ap, sc.ap, bc(m8.ap[:, :, 16], [128, 4, 64], 2), ALU.is_ge), [sc, m8], [sc2])
            P.op("dve", lambda e: e.tensor_scalar(sc2.ap, sc2.ap, -NEGB, NEGB, ALU.mult, ALU.add), [sc2], [sc2])
            for slot in range(2):
                hf_ = 1 - slot
                P.op("pool", lambda e: e.tensor_copy(BTin.ap[:, :, slot * 64:(slot + 1) * 64], sc2.ap[:, hf_ * 2:hf_ * 2 + 2, :]), [sc2], [BTin])
            for gi in range(2):
                P.op("pe", lambda e: e.transpose(P.bank(1, [128, 8, 128], BF16)[:, gi, :], BTin.ap[:, gi, :], c.ident.ap), [BTin, c.ident], [P.B[1]])
            BT = P.bank(1, [128, 8, 128], BF16)
            for gi in range(2):
                P.op("act", lambda e: e.copy(QE.ap[64:128, gi, :, :], bc(BT[64:128, gi, :], [64, 4, 128], 1)), [P.B[1]], [QE])
                P.op("dve", lambda e: e.tensor_copy(QE.ap[0:64, 2 + gi, :, :], bc(BT[0:64, gi, :], [64, 4, 128], 1)), [P.B[1]], [QE])
            for g in range(4):
                hf, gi = g // 2, g % 2
                ob = 4 if g % 2 == 0 else 6
                rhs = QE.ap[:, g, :, :].rearrange("p r q -> p (r q)")
                for kt in range(v + 1):
                    sb_ = 2 + kt % 2
                    diag = kt == v
                    P.op("pe", lambda e: e.matmul(P.bank(sb_, [128, 512], F32), c.Ksel.ap[:, g, kt * 128:(kt + 1) * 128], rhs, start=True, stop=not diag),
                         [c.Ksel, QE], [P.B[sb_]])
                    if diag:
                        P.op("pe", lambda e: e.matmul(P.bank(sb_, [128, 512], F32), c.ident.ap, CB.ap[:, 0, :], start=False, stop=True), [c.ident, CB], [P.B[sb_]])
                    pt = PT[kt % 2]
                    P.act(pt.ap, P.bank(sb_, [128, 512], F32), AF.Exp, scale=SCALE, reads=[P.B[sb_]], writes=[pt])
                    P.op("pe", lambda e: e.matmul(P.bank(ob, [128, 512], F32)[0:65, :], c.Vsel1.ap[:, kt, g, :], pt.ap, start=(kt == 0), stop=(kt == v)),
                         [c.Vsel1, pt], [P.B[ob]])
                combine_later(P, c, ob, g, 1, gts, oatt, tmp)
            for g in range(4):
                hf, gi = g // 2, g % 2
                hs_ = slice(hf * 64, hf * 64 + 64)
                ob = 4 if g % 2 == 0 else 6
                rhs = QE.ap[hs_, g, :, :].rearrange("p r q -> p (r q)")
                kts = [kt for kt in range(v - 4, v + 1) if kt >= 0]
                for kt in kts:
                    sb_ = 2 + kt % 2
                    mk = 0 if kt == v else (1 if kt == v - 4 else None)
                    P.op("pe", lambda e: e.matmul(P.bank(sb_, [128, 512], F32), c.Kwin.ap[hs_, gi, kt * 128:(kt + 1) * 128], rhs, start=True, stop=(mk is None)),
                         [c.Kwin, QE], [P.B[sb_]])
                    if mk is not None:
                        P.op("pe", lambda e: e.matmul(P.bank(sb_, [128, 512], F32), c.ident.ap, CB.ap[:, mk, :], start=False, stop=True), [c.ident, CB], [P.B[sb_]])
                    pt = PT[kt % 2]
                    P.act(pt.ap, P.bank(sb_, [128, 512], F32), AF.Exp, scale=SCALE, reads=[P.B[sb_]], writes=[pt])
                    P.op("pe", lambda e: e.matmul(P.bank(ob, [128, 512], F32)[0:65, :], c.Vwin1.ap[:, kt, g, :], pt.ap, start=(kt == kts[0]), stop=(kt == kts[-1])),
                         [c.Vwin1, pt], [P.B[ob]])
                combine_later(P, c, ob, g, 2, gts, oatt, tmp)
            combine_flush()
            P.op("act", lambda e: e.copy(QTin.ap.rearrange("p a b -> p (a b)"), oatt.ap), [oatt], [QTin])
            for cch in range(8):
                P.op("pe", lambda e: e.transpose(P.bank(0, [128, 8, 128], BF16)[:, cch, :], QTin.ap[:, cch, :], c.ident.ap), [QTin, c.ident], [P.B[0]])
            P.op("act", lambda e: e.copy(c.oattT.ap[:, :, i * 128:(i + 1) * 128], P.bank(0, [128, 8, 128], BF16)), [P.B[0]], [c.oattT])
    P.barrier()


def pass_d(P, c):
    NTOK = 8 * 128 + NS
    segs = [(0, 512), (512, 512), (1024, NS)]
    with P.scope():
        xnTo = P.sb("xnTo", [128, 16, NTOK], BF16)
        mT = P.sb("mT", [128, 16, NTOK], BF16)
        for i in range(9):
            blk = 4 * i + 3 if i < 8 else NBLK
            n = 128 if i < 8 else NS
            xnT, _ = norm_T(P, c, blk, n)
            P.op("pool", lambda e: e.tensor_copy(xnTo.ap[:, :, i * 128:i * 128 + n], xnT.ap[:, :, 0:n]), [xnT], [xnTo])
        with P.scope():
            wg = [P.sb(f"wg{i}", [128, 16, 128], BF16) for i in range(2)]
            sg = [P.sb(f"sg{i}", [128, 512], BF16) for i in range(2)]
            it = 0
            for c0, tgt in ((C_GA, c.oattT), (C_GR, c.hs_own)):
                for cch in range(8):
                    w = wg[it % 2]
                    P.dma("pool", w.ap, c.w_in[:, c0 + cch * 128:c0 + (cch + 1) * 128].rearrange("(k p) c -> p k c", p=128), writes=[w])
                    for si, (t0, tn) in enumerate(segs):
                        bk = 2 + (it * 3 + si) % 2
                        for k in range(16):
                            P.op("pe", lambda e: e.matmul(P.bank(bk, [128, 512], F32)[:, 0:tn], w.ap[:, k, :], xnTo.ap[:, k, t0:t0 + tn],
                                                          start=(k == 0), stop=(k == 15)), [w, xnTo], [P.B[bk]])
                        sgt = sg[(it * 3 + si) % 2]
                        P.act(sgt.ap[:, 0:tn], P.bank(bk, [128, 512], F32)[:, 0:tn], AF.Silu, reads=[P.B[bk]], writes=[sgt])
                        P.op("dve", lambda e: e.tensor_tensor(tgt.ap[:, cch, t0:t0 + tn], tgt.ap[:, cch, t0:t0 + tn], sgt.ap[:, 0:tn], ALU.mult), [tgt, sgt], [tgt])
                    it += 1
        if c.dbg:
            P.dma("sp", c.dbg_g, c.oattT.ap, reads=[c.oattT], out=True)
            P.dma("sp", c.dbg_hg, c.hs_own.ap, reads=[c.hs_own], out=True)
            P.dma("sp", c.dbg_xn, xnTo.ap, reads=[xnTo], out=True)
        with P.scope():
            wa = [P.sb(f"wa{i}", [128, 8, 128], BF16) for i in range(2)]
            wr = [P.sb(f"wr{i}", [128, 8, 128], BF16) for i in range(2)]
            wma = [P.sb(f"wma{i}", [128, 16, 128], BF16) for i in range(2)]
            wmr = [P.sb(f"wmr{i}", [128, 16, 128], BF16) for i in range(2)]
            s1 = P.sb("s1", [128, 512], F32)
            s2 = P.sb("s2", [128, 512], F32)
            for cc in range(16):
                cs = slice(cc * 128, (cc + 1) * 128)
                b = cc % 2
                P.dma("pool", wa[b].ap, c.w_att_out[:, cs].rearrange("(k p) c -> p k c", p=128), writes=[wa[b]])
                P.dma("pool", wr[b].ap, c.w_rnn_out[:, cs].rearrange("(k p) c -> p k c", p=128), writes=[wr[b]])
                P.dma("pool", wma[b].ap, c.w_in[:, C_MA + cc * 128:C_MA + (cc + 1) * 128].rearrange("(k p) c -> p k c", p=128), writes=[wma[b]])
                P.dma("pool", wmr[b].ap, c.w_in[:, C_MR + cc * 128:C_MR + (cc + 1) * 128].rearrange("(k p) c -> p k c", p=128), writes=[wmr[b]])
                for (t0, tn) in segs:
                    for bk, w, src, nk in ((2, wa[b], c.oattT, 8), (3, wr[b], c.hs_own, 8), (4, wma[b], xnTo, 16), (5, wmr[b], xnTo, 16)):
                        for k in range(nk):
                            P.op("pe", lambda e: e.matmul(P.bank(bk, [128, 512], F32)[:, 0:tn], w.ap[:, k, :], src.ap[:, k, t0:t0 + tn],
                                                          start=(k == 0), stop=(k == nk - 1)), [w, src], [P.B[bk]])
                    P.act(s1.ap[:, 0:tn], P.bank(4, [128, 512], F32)[:, 0:tn], AF.Sigmoid, reads=[P.B[4]], writes=[s1])
                    P.act(s2.ap[:, 0:tn], P.bank(5, [128, 512], F32)[:, 0:tn], AF.Sigmoid, reads=[P.B[5]], writes=[s2])
                    P.op("dve", lambda e: e.tensor_tensor(s1.ap[:, 0:tn], s1.ap[:, 0:tn], P.bank(2, [128, 512], F32)[:, 0:tn], ALU.mult), [s1, P.B[2]], [s1])
                    P.op("dve", lambda e: e.tensor_tensor(s2.ap[:, 0:tn], s2.ap[:, 0:tn], P.bank(3, [128, 512], F32)[:, 0:tn], ALU.mult), [s2, P.B[3]], [s2])
                    P.op("pool", lambda e: e.tensor_tensor(mT.ap[:, cc, t0:t0 + tn], s1.ap[:, 0:tn], s2.ap[:, 0:tn], ALU.add), [s1, s2], [mT])
        if c.dbg:
            P.dma("sp", c.dbg_m, mT.ap, reads=[mT], out=True)
        with P.scope():
            wo = [P.sb(f"wo{i}", [128, 16, 512], BF16) for i in range(2)]
            xr = [P.sb(f"xres{i}", [128, 512], F32) for i in range(2)]
            yt = [P.sb(f"yt{i}", [128, 512], F32) for i in range(2)]
            it = 0
            for oc in range(4):
                w = wo[oc % 2]
                for k in range(16):
                    P.dma("pool", w.ap[:, k, :], c.w_out[k * 128:(k + 1) * 128, oc * 512:(oc + 1) * 512], writes=[w], semkey=f"wo{oc % 2}")
                for i in range(9):
                    blk = 4 * i + 3 if i < 8 else NBLK
                    n = 128 if i < 8 else NS
                    bk = 2 + it % 2
                    for k in range(16):
                        P.op("pe", lambda e: e.matmul(P.bank(bk, [128, 512], F32)[0:n, :], mT.ap[:, k, i * 128:i * 128 + n], w.ap[:, k, :],
                                                      start=(k == 0), stop=(k == 15)), [mT, w], [P.B[bk]])
                    x_ = xr[it % 2]
                    y_ = yt[it % 2]
                    P.dma("sp", x_.ap[0:n, :], c.xv[blk * 128:blk * 128 + n, oc * 512:(oc + 1) * 512], writes=[x_])
                    P.op("dve", lambda e: e.tensor_tensor(y_.ap[0:n, :], x_.ap[0:n, :], P.bank(bk, [128, 512], F32)[0:n, :], ALU.add), [x_, P.B[bk]], [y_])
                    if i < 8:
                        P.dma("sp", c.o_y[i, :, oc * 512:(oc + 1) * 512], y_.ap, reads=[y_], out=True)
                    else:
                        P.dma("sp", c.o_ys[:, oc * 512:(oc + 1) * 512], y_.ap[0:n, :], reads=[y_], out=True)
                    it += 1
    P.barrier()


def pass_s(P, c):
    def b2(ap, shape):
        return ap.unsqueeze(1).unsqueeze(1).broadcast_to(list(shape))
    with P.scope():
        wqs = P.sb("s_wqs", [128, 16, 512], BF16)
        qf = P.sb("s_qf", [128, 1024], F32)
        gts = P.sb("s_gts", [128, 48], F32)
        tmp = dict(sq=P.sb("s_sq", [128, 512], F32), st=P.sb("s_st", [128, 64], F32), oTs=P.sb("s_oTs", [128, 512], F32),
                   cf=P.sb("s_cf", [128, 16], F32), otmp=P.sb("s_otmp", [128, 256], F32))
        qproj(P, c, NBLK, NS, wqs, qf, gts, tmp)
        QTin = P.sb("s_QTin", [128, 8, 128], BF16)
        QEs = P.sb("s_QEs", [128, 8, NS], BF16)
        P.op("pool", lambda e: e.tensor_copy(QTin.ap[0:NS].rearrange("p (slot r) (gh d) -> p slot gh r d", slot=2, gh=2),
                                             qf.ap[0:NS].rearrange("p (slot gh r d) -> p slot gh r d", slot=2, gh=2, r=4)), [qf], [QTin])
        for t8 in range(8):
            P.op("pe", lambda e: e.transpose(P.bank(0, [128, 8, 128], BF16)[:, t8, 0:NS], QTin.ap[0:NS, t8, :], c.ident.ap[0:NS, 0:NS]), [QTin, c.ident], [P.B[0]])
        P.op("act", lambda e: e.copy(QEs.ap, P.bank(0, [128, 8, 128], BF16)[:, :, 0:NS]), [P.B[0]], [QEs])
        wc16 = P.sb("wc16", [128, 32, 2, 64], F32)
        iota16 = P.sb("iota16", [128, 16], F32)
        mwin0 = P.sb("mwin0", [128, 8], BF16)
        caus8 = P.sb("caus8", [128, 8], BF16)
        for dst, src in ((wc16, c.d_wc16), (iota16, c.d_iota16), (mwin0, c.d_mwin0), (caus8, c.d_caus8)):
            P.dma("sp", dst.ap, src, writes=[dst])
        idx = P.sb("s_idx", [128, 1], I32)
        idf = P.sb("s_idf", [128, 2], F32)
        ia = P.sb("s_ia", [128, 16], F32)
        ii = P.sb("s_ii", [128, 16], I32)
        gb = P.sb("s_gb", [128, 48], F32)
        ck = [P.sb(f"s_ck{i}", [128, 4096], F32) for i in range(2)]
        tm = P.sb("s_tm", [128, 8, 512], F32)
        part = P.sb("s_part", [128, 512], F32)
        acc = P.sb("s_acc", [128, 4, 512], F32)
        kcn = P.sb("s_kcn", [128, 4, 256], F32)
        kcb = P.sb("s_kcb", [128, 4, 256], BF16)
        KcTs = P.sb("s_KcTs", [128, 4, 2, 128], BF16)
        Vc1s = P.sb("s_Vc1s", [128, 4, 4, 65], BF16)
        Es = P.sb("s_Es", [128, 512], F32)
        impa = P.sb("s_impa", [128, 512], F32)
        sc = P.sb("s_sc", [128, 4, 256], F32)
        sc2 = P.sb("s_sc2", [128, 4, 256], F32)
        smk = P.sb("s_smk", [128, 4, 2, 128], BF16)
        m8 = P.sb("s_m8", [128, 4, 24], F32)
        mT = P.sb("s_mT", [128, 8, 8], BF16)
        V1c = P.sb("s_V1c", [128, 8, 4, 65], BF16)
        kcs = P.sb("s_kcs", [128, 8, 256], BF16)
        KT = P.sb("s_KT", [128, 8, 2, 128], BF16)
        PT = [P.sb(f"s_PT{i}", [128, 256], BF16) for i in range(2)]
        rwn = P.sb("s_rwn", [128, 512], F32)
        knn = P.sb("s_knn", [128, 256], BF16)
        V1n = P.sb("s_V1n", [128, 4, 65], BF16)
        KTn = P.sb("s_KTn", [128, 2, 8], BF16)
        PTn = P.sb("s_PTn", [128, 32], BF16)
        oatt = P.sb("s_oatt", [128, 1024], F32)
        oattb = P.sb("s_oattb", [128, 1024], BF16)
        for V in (Vc1s, V1c, V1n):
            P.op("pool", lambda e: e.memset(V.ap, 1.0), [], [V])
        rows_cmp = c.d_ccmp.rearrange("n (c t) f -> (n c) (t f)", c=16)
        rows_sel = c.d_csel.rearrange("n (c t) f -> (n c) (t f)", c=16)
        ngather = [0]

        def gather(rows, ch):
            t = ck[ngather[0] % 2]
            ngather[0] += 1
            P.dma_fn("pool", lambda e: e.indirect_dma_start(out=t.ap, out_offset=None, in_=rows,
                                                            in_offset=bass.IndirectOffsetOnAxis(ap=ii.ap[:, ch:ch + 1], axis=0)),
                     reads=[ii], writes=[t], semkey=("gather", ngather[0] % 2))
            return t

        def new_tokens(dram_rows, res, b, br, caus=True):
            P.dma("sp", rwn.ap[0:8, :], dram_rows[b * 8:(b + 1) * 8, :], reads=[res], writes=[rwn])
            P.op("act", lambda e: e.copy(knn.ap[0:8, :], rwn.ap[0:8, 0:256]), [rwn], [knn])
            P.op("pool", lambda e: e.tensor_copy(V1n.ap[0:8, :, 0:64], rwn.ap[0:8, 256:512].rearrange("p (g d) -> p g d", g=4)), [rwn], [V1n])
            for slot in range(2):
                P.op("pe", lambda e: e.transpose(P.bank(1, [128, 8, 128], BF16)[:, slot, 0:8], knn.ap[0:8, slot * 128:(slot + 1) * 128], c.ident.ap[0:8, 0:8]),
                     [knn, c.ident], [P.B[1]])
            P.op("act", lambda e: e.copy(KTn.ap, P.bank(1, [128, 8, 128], BF16)[:, 0:2, 0:8]), [P.B[1]], [KTn])
            for g in range(4):
                slot, hf = g // 2, g % 2
                hs_ = slice(hf * 64, hf * 64 + 64)
                sb_ = 2 + g % 2
                P.op("pe", lambda e: e.matmul(P.bank(sb_, [128, 512], F32)[0:8, 0:32], KTn.ap[hs_, slot, :], QEs.ap[hs_, slot * 4:(slot + 1) * 4, b * 8:(b + 1) * 8],
                                              start=True, stop=True), [KTn, QEs], [P.B[sb_]])
                P.act(PTn.ap[0:8, :], P.bank(sb_, [128, 512], F32)[0:8, 0:32], AF.Exp, scale=SCALE, reads=[P.B[sb_]], writes=[PTn])
                P.op("dve", lambda e: e.tensor_tensor(PTn.ap[0:8, :].rearrange("p (r q) -> p r q", r=4), PTn.ap[0:8, :].rearrange("p (r q) -> p r q", r=4),
                                                      bc(caus8.ap[0:8, :], [8, 4, 8], 1), ALU.mult), [PTn, caus8], [PTn])
                P.op("pe", lambda e: e.matmul(P.bank(4 + g, [128, 512], F32)[0:65, 0:32], V1n.ap[0:8, g, :], PTn.ap[0:8, :], start=False, stop=True),
                     [V1n, PTn], [P.B[4 + g]])
                combine(P, c, 4 + g, g, br, gb, oatt, tmp, n=8, tb=0)

        for b in range(4):
            P.dma("sp", idx.ap, c.d_pt[b].rearrange("(p o) -> p o", o=1), writes=[idx])
            P.op("dve", lambda e: e.tensor_copy(idf.ap[:, 0:1], idx.ap), [idx], [idf])
            P.op("dve", lambda e: e.tensor_scalar_mul(idf.ap[:, 1:2], idf.ap[:, 0:1], 16.0), [idf], [idf])
            P.op("dve", lambda e: e.tensor_scalar(ia.ap, iota16.ap, idf.ap[:, 1:2], None, ALU.add), [idf, iota16], [ia])
            P.op("dve", lambda e: e.tensor_copy(ii.ap, ia.ap), [ia], [ii])
            P.dma("sp", gb.ap[0:8, :], gts.ap[b * 8:(b + 1) * 8, :], reads=[gts], writes=[gb])
            tn = gather(rows_cmp, 0)
            for ch in range(16):
                blk4, cq = ch // 4, ch % 4
                t = tn
                if ch + 1 < 16:
                    tn = gather(rows_cmp, ch + 1)
                P.op("pool", lambda e: e.tensor_tensor(tm.ap.rearrange("p j (s g d) -> p j s g d", s=2, g=4),
                                                       t.ap.rearrange("p (j s g d) -> p j s g d", j=8, s=2, g=4),
                                                       wc16.ap[:, cq * 8:(cq + 1) * 8, :, :].unsqueeze(3).broadcast_to([128, 8, 2, 4, 64]), ALU.mult),
                     [t, wc16], [tm])
                if cq == 0:
                    P.op("dve", lambda e: e.tensor_reduce(acc.ap[:, blk4, :], tm.ap.rearrange("p j f -> p f j"), AX.X, ALU.add), [tm], [acc])
                else:
                    P.op("dve", lambda e: e.tensor_reduce(part.ap, tm.ap.rearrange("p j f -> p f j"), AX.X, ALU.add), [tm], [part])
                    P.op("dve", lambda e: e.tensor_tensor(acc.ap[:, blk4, :], acc.ap[:, blk4, :], part.ap, ALU.add), [acc, part], [acc])
            for blk4 in range(4):
                rms_rows(P, c, acc.ap[:, blk4, 0:256], 128, 4, c.kg_rep.ap[:, 0, :], kcn.ap[:, blk4, :], tmp, [acc], [kcn])
            P.op("act", lambda e: e.copy(kcb.ap, kcn.ap), [kcn], [kcb])
            P.op("pool", lambda e: e.tensor_copy(Vc1s.ap[:, :, :, 0:64], acc.ap[:, :, 256:512].rearrange("p b (g d) -> p b g d", g=4)), [acc], [Vc1s])
            for blk4 in range(4):
                for slot in range(2):
                    P.op("pe", lambda e: e.transpose(P.bank(0, [128, 8, 128], BF16)[:, blk4 * 2 + slot, :], kcb.ap[:, blk4, slot * 128:(slot + 1) * 128], c.ident.ap),
                         [kcb, c.ident], [P.B[0]])
            P.op("act", lambda e: e.copy(KcTs.ap.rearrange("p b s n -> p (b s) n"), P.bank(0, [128, 8, 128], BF16)), [P.B[0]], [KcTs])
            for g in range(4):
                slot, hf = g // 2, g % 2
                hs_ = slice(hf * 64, hf * 64 + 64)
                for r in range(4):
                    sb_ = 2 + r % 2
                    for blk4 in range(4):
                        P.op("pe", lambda e: e.matmul(P.bank(sb_, [128, 4, 128], F32)[0:8, blk4, :], QEs.ap[hs_, slot * 4 + r, b * 8:(b + 1) * 8], KcTs.ap[hs_, blk4, slot, :],
                                                      start=True, stop=True), [QEs, KcTs], [P.B[sb_]])
                    P.act(Es.ap[0:8, :], P.bank(sb_, [128, 512], F32)[0:8, :], AF.Exp, scale=SCALE, accum_out=m8.ap[0:8, 0, 16:17], reads=[P.B[sb_]], writes=[Es, m8])
                    P.op("dve", lambda e: e.reciprocal(m8.ap[0:8, 0, 17:18], m8.ap[0:8, 0, 16:17]), [m8], [m8])
                    if r == 0:
                        P.op("dve", lambda e: e.tensor_scalar(impa.ap[0:8, :], Es.ap[0:8, :], m8.ap[0:8, 0, 17:18], None, ALU.mult), [Es, m8], [impa])
                    else:
                        P.op("dve", lambda e: e.tensor_scalar(Es.ap[0:8, :], Es.ap[0:8, :], m8.ap[0:8, 0, 17:18], None, ALU.mult), [Es, m8], [Es])
                        P.op("dve", lambda e: e.tensor_tensor(impa.ap[0:8, :], impa.ap[0:8, :], Es.ap[0:8, :], ALU.add), [impa, Es], [impa])
                iv = impa.ap[0:8, :].rearrange("p (h two n) -> p h two n", h=2, two=2)
                P.op("dve", lambda e: e.tensor_tensor(sc.ap[0:8, g, :].rearrange("p (h n) -> p h n", h=2), iv[:, :, 0, :], iv[:, :, 1, :], ALU.add), [impa], [sc])
                sb_ = 2 + g % 2
                for blk4 in range(4):
                    P.op("pe", lambda e: e.matmul(P.bank(sb_, [128, 4, 32], F32)[:, blk4, :], KcTs.ap[hs_, blk4, slot, :], QEs.ap[hs_, slot * 4:(slot + 1) * 4, b * 8:(b + 1) * 8],
                                                  start=True, stop=True), [KcTs, QEs], [P.B[sb_]])
                pt = PT[g % 2]
                P.act(pt.ap[:, 0:128], P.bank(sb_, [128, 128], F32), AF.Exp, scale=SCALE, reads=[P.B[sb_]], writes=[pt])
                for blk4 in range(4):
                    P.op("pe", lambda e: e.matmul(P.bank(4 + g, [128, 512], F32)[0:65, 0:32], Vc1s.ap[:, blk4, g, :], pt.ap[:, blk4 * 32:(blk4 + 1) * 32],
                                                  start=(blk4 == 0), stop=(blk4 == 3)), [Vc1s, pt], [P.B[4 + g]])
                combine(P, c, 4 + g, g, 0, gb, oatt, tmp, n=8, first=True, tb=0)
            P.op("pool", lambda e: e.memset(sc.ap[0:8, :, 0:1], 10.0), [sc], [sc])
            P.op("pool", lambda e: e.memset(sc.ap[0:8, :, 255:256], 10.0), [sc], [sc])
            for g in range(4):
                P.op("dve", lambda e: e.max(m8.ap[0:8, g, 0:8], sc.ap[0:8, g, :]), [sc], [m8])
                P.op("dve", lambda e: e.match_replace(sc2.ap[0:8, g, :], m8.ap[0:8, g, 0:8], sc.ap[0:8, g, :], -2.0), [sc, m8], [sc2])
                P.op("dve", lambda e: e.max(m8.ap[0:8, g, 8:16], sc2.ap[0:8, g, :]), [sc2], [m8])
            P.op("dve", lambda e: e.tensor_tensor(smk.ap[0:8].rearrange("p g h n -> p g (h n)"), sc.ap[0:8], bc(m8.ap[0:8, :, 14], [8, 4, 256], 2), ALU.is_ge), [sc, m8], [smk])
            for g in range(4):
                for h2 in range(2):
                    P.op("pe", lambda e: e.transpose(P.bank(1, [128, 8, 128], BF16)[:, g * 2 + h2, 0:8], smk.ap[0:8, g, h2, :], c.ident.ap[0:8, 0:8]), [smk, c.ident], [P.B[1]])
            P.op("act", lambda e: e.copy(mT.ap, P.bank(1, [128, 8, 128], BF16)[:, :, 0:8]), [P.B[1]], [mT])
            tn = gather(rows_sel, 0)
            for ch in range(16):
                h2 = ch // 8
                t = tn
                if ch + 1 < 16:
                    tn = gather(rows_sel, ch + 1)
                t5 = t.ap.rearrange("p (j s g d) -> p j s g d", j=8, s=2, g=4)
                P.op("pool", lambda e: e.tensor_copy(V1c.ap[:, :, :, 0:64], t5[:, :, 1]), [t], [V1c])
                P.op("act", lambda e: e.copy(kcs.ap, t.ap.rearrange("p (j f) -> p j f", j=8)[:, :, 0:256]), [t], [kcs])
                for tt in range(8):
                    for slot in range(2):
                        P.op("pe", lambda e: e.transpose(P.bank(tt // 4, [128, 8, 128], BF16)[:, (tt % 4) * 2 + slot, :], kcs.ap[:, tt, slot * 128:(slot + 1) * 128], c.ident.ap),
                             [kcs, c.ident], [P.B[tt // 4]])
                P.op("act", lambda e: e.copy(KT.ap[:, 0:4].rearrange("p t s n -> p (t s) n"), P.bank(0, [128, 8, 128], BF16)), [P.B[0]], [KT])
                P.op("dve", lambda e: e.tensor_copy(KT.ap[:, 4:8].rearrange("p t s n -> p (t s) n"), P.bank(1, [128, 8, 128], BF16)), [P.B[1]], [KT])
                for g in range(4):
                    slot, hf = g // 2, g % 2
                    hs_ = slice(hf * 64, hf * 64 + 64)
                    sb_ = 2 + g % 2
                    for tt in range(8):
                        P.op("pe", lambda e: e.matmul(P.bank(sb_, [128, 8, 32], F32)[:, tt, :], KT.ap[hs_, tt, slot, :], QEs.ap[hs_, slot * 4:(slot + 1) * 4, b * 8:(b + 1) * 8],
                                                      start=True, stop=True), [KT, QEs], [P.B[sb_]])
                    pt = PT[g % 2]
                    P.act(pt.ap, P.bank(sb_, [128, 256], F32), AF.Exp, scale=SCALE, reads=[P.B[sb_]], writes=[pt])
                    P.op("dve", lambda e: e.tensor_tensor(pt.ap.rearrange("p (t r q) -> p t r q", t=8, r=4), pt.ap.rearrange("p (t r q) -> p t r q", t=8, r=4),
                                                          b2(mT.ap[:, g * 2 + h2, :], [128, 8, 4, 8]), ALU.mult), [pt, mT], [pt])
                    for tt in range(8):
                        P.op("pe", lambda e: e.matmul(P.bank(4 + g, [128, 512], F32)[0:65, 0:32], V1c.ap[:, tt, g, :], pt.ap[:, tt * 32:(tt + 1) * 32],
                                                      start=(ch == 0 and tt == 0), stop=False), [V1c, pt], [P.B[4 + g]])
            new_tokens(c.s_sel, c.r_ssel, b, 1)
            cw = ck[0]
            P.dma("sp", cw.ap.rearrange("p (t f) -> p t f", t=8)[:, 0:4, :], c.d_cwin[b].rearrange("(t p) f -> p t f", p=128), writes=[cw], semkey=("gather", 1))
            ngather[0] = 1
            cw3 = cw.ap.rearrange("p (t s g d) -> p t s g d", t=8, s=2, g=4)
            P.op("pool", lambda e: e.tensor_copy(V1c.ap[:, 0:4, :, 0:64], cw3[:, 0:4, 1]), [cw], [V1c])
            P.op("act", lambda e: e.copy(kcs.ap[:, 0:4, :], cw.ap.rearrange("p (t f) -> p t f", t=8)[:, 0:4, 0:256]), [cw], [kcs])
            for tt in range(4):
                for slot in range(2):
                    P.op("pe", lambda e: e.transpose(P.bank(0, [128, 8, 128], BF16)[:, tt * 2 + slot, :], kcs.ap[:, tt, slot * 128:(slot + 1) * 128], c.ident.ap),
                         [kcs, c.ident], [P.B[0]])
            P.op("act", lambda e: e.copy(KT.ap[:, 0:4].rearrange("p t s n -> p (t s) n"), P.bank(0, [128, 8, 128], BF16)), [P.B[0]], [KT])
            for g in range(4):
                slot, hf = g // 2, g % 2
                hs_ = slice(hf * 64, hf * 64 + 64)
                sb_ = 2 + g % 2
                for tt in range(4):
                    P.op("pe", lambda e: e.matmul(P.bank(sb_, [128, 8, 32], F32)[:, tt, :], KT.ap[hs_, tt, slot, :], QEs.ap[hs_, slot * 4:(slot + 1) * 4, b * 8:(b + 1) * 8],
                                                  start=True, stop=True), [KT, QEs], [P.B[sb_]])
                pt = PT[g % 2]
                P.act(pt.ap[:, 0:128], P.bank(sb_, [128, 128], F32), AF.Exp, scale=SCALE, reads=[P.B[sb_]], writes=[pt])
                P.op("dve", lambda e: e.tensor_tensor(pt.ap[:, 0:32].rearrange("p (r q) -> p r q", r=4), pt.ap[:, 0:32].rearrange("p (r q) -> p r q", r=4),
                                                      bc(mwin0.ap, [128, 4, 8], 1), ALU.mult), [pt, mwin0], [pt])
                for tt in range(4):
                    P.op("pe", lambda e: e.matmul(P.bank(4 + g, [128, 512], F32)[0:65, 0:32], V1c.ap[:, tt, g, :], pt.ap[:, tt * 32:(tt + 1) * 32],
                                                  start=(tt == 0), stop=False), [V1c, pt], [P.B[4 + g]])
            new_tokens(c.s_win, c.r_swin, b, 2)
            P.dma("sp", c.o_wins[b, 0:504, :], c.d_cwin[b, 8:512, :], semkey="owins", reads=[c.r_swin], out=True)
            P.dma("sp", c.o_wins[b, 504:512, :], rwn.ap[0:8, :], reads=[rwn], semkey="owins", out=True)
            P.op("act", lambda e: e.copy(oattb.ap[0:8, :], oatt.ap[0:8, :]), [oatt], [oattb])
            for cch in range(8):
                P.op("pe", lambda e: e.transpose(P.bank(0, [128, 8, 128], BF16)[:, cch, 0:8], oattb.ap[0:8, cch * 128:(cch + 1) * 128], c.ident.ap[0:8, 0:8]), [oattb, c.ident], [P.B[0]])
            P.op("act", lambda e: e.copy(c.oattT.ap[:, :, 1024 + b * 8:1024 + (b + 1) * 8], P.bank(0, [128, 8, 128], BF16)[:, :, 0:8]), [P.B[0]], [c.oattT])


def build_program(passes=("k1", "k2", "s", "r", "d"), pool_pages=5120):
    nc = bass.Bass("TRN2", target_bir_lowering=False)
    P = Prog(nc)
    c = Ctx()
    c.xv = dram_in(nc, "xv", [(NBLK + 1) * 128, D])
    c.w_in = dram_in(nc, "w_in", [D, 9776])
    g_rep = dram_in(nc, "g_rep", [128, D])
    qg_rep = dram_in(nc, "qg_rep", [128, 64])
    kg_rep = dram_in(nc, "kg_rep", [128, 3, 64])
    wc_rep = dram_in(nc, "wc_rep", [128, 512])
    ident = dram_in(nc, "ident", [128, 128], BF16)
    indw = dram_in(nc, "indw", [128, 252])
    ehost = dram_in(nc, "ehost", [64, 4096], BF16)
    padcol = dram_in(nc, "padcol", [128, NBLK + 1])
    c.o_cmp = dram_out(nc, "o_cmp", [8, 128, 512])
    c.o_sel = dram_out(nc, "o_sel", [8, 128, 512])
    c.o_win = dram_out(nc, "o_win", [128, 512])
    c.s_cmp = dram_out(nc, "s_cmp", [NS, 512])
    c.s_sel = dram_out(nc, "s_sel", [NS, 512])
    c.s_win = dram_out(nc, "s_win", [NS, 512])
    c.d_wrg = dram_in(nc, "wrg_bd", [128, 8, 128])
    c.d_wig = dram_in(nc, "wig_bd", [128, 8, 128])
    c.d_cw = dram_in(nc, "convw_T", [128, 8, 4])
    c.d_rsm = dram_in(nc, "rsm_T", [128, 4, 8])
    c.d_vrow = dram_in(nc, "vrow", [128, 384])
    c.d_h0T = dram_in(nc, "h0T", [128, 8, 4])
    c.d_sconvT = dram_in(nc, "sconvT", [128, 8, 4, 3])
    c.d_identf = dram_in(nc, "identf", [128, 128])
    c.d_cmask = dram_in(nc, "cmask", [128, 8, 128], BF16)
    c.d_cmaskT = dram_in(nc, "cmaskT", [128, 8, 128], BF16)
    c.d_candm = dram_in(nc, "candm", [128, 8, 64])
    c.d_addm = dram_in(nc, "addm", [128, 8, 64])
    c.d_CB = dram_in(nc, "CB", [128, 2, 512], BF16)
    padcmp = dram_in(nc, "padcmp", [128, 1])
    c.w_att_out = dram_in(nc, "w_att_out", [1024, D])
    c.w_rnn_out = dram_in(nc, "w_rnn_out", [1024, D])
    c.w_out = dram_in(nc, "w_out", [D, D])
    c.r_ssel, c.r_swin = Res("r_ssel"), Res("r_swin")
    if "s" in passes:
        NPOOL = pool_pages
        c.d_ccmp = dram_in(nc, "cache_cmp", [NPOOL, 128, 512])
        c.d_csel = dram_in(nc, "cache_sel", [NPOOL, 128, 512])
        c.d_cwin = dram_in(nc, "cache_win", [4, 512, 512])
        c.d_pt = dram_in(nc, "ptab", [4, 128], I32)
        c.d_wc16 = dram_in(nc, "wc16", [128, 32, 2, 64])
        c.d_iota16 = dram_in(nc, "iota16", [128, 16])
        c.d_mwin0 = dram_in(nc, "mwin0", [128, 8], BF16)
        c.d_caus8 = dram_in(nc, "caus8", [128, 8], BF16)
        c.o_wins = dram_out(nc, "o_wins", [4, 512, 512])
    c.o_y = dram_out(nc, "o_y", [8, 128, D])
    c.o_ys = dram_out(nc, "o_ys", [NS, D])
    c.dbg = "dbg" in passes
    if c.dbg:
        c.dbg_oatt = dram_out(nc, "dbg_oatt", [128, 8, 8 * 128 + NS], BF16)
        c.dbg_hs = dram_out(nc, "dbg_hs", [128, 8, 8 * 128 + NS], BF16)
        c.dbg_g = dram_out(nc, "dbg_g", [128, 8, 8 * 128 + NS], BF16)
        c.dbg_hg = dram_out(nc, "dbg_hg", [128, 8, 8 * 128 + NS], BF16)
        c.dbg_m = dram_out(nc, "dbg_m", [128, 16, 8 * 128 + NS], BF16)
        c.dbg_xn = dram_out(nc, "dbg_xn", [128, 16, 8 * 128 + NS], BF16)
    c.o_hp = dram_out(nc, "o_hp", [1, DR])
    c.o_convp = dram_out(nc, "o_convp", [3, DR])
    c.o_hs = dram_out(nc, "o_hs", [4, DR])
    c.o_convs = dram_out(nc, "o_convs", [4, 3, DR])
    P.init_banks()
    c.g_rep = P.sb("g_rep", [128, D], F32)
    c.qg_rep = P.sb("qg_rep", [128, 64], F32)
    c.kg_rep = P.sb("kg_rep", [128, 3, 64], F32)
    c.wc_rep = P.sb("wc_rep", [128, 512], F32)
    c.ident = P.sb("ident", [128, 128], BF16)
    c.indw = P.sb("indw", [128, 252], F32)
    c.padcol = P.sb("padcol", [128, NBLK + 1], F32)
    c.padcmp = P.sb("padcmp", [128, 1], F32)
    c.identf = P.sb("identf_g", [128, 128], F32)
    for dst, src in ((c.g_rep, g_rep), (c.qg_rep, qg_rep), (c.kg_rep, kg_rep), (c.wc_rep, wc_rep), (c.ident, ident),
                     (c.indw, indw), (c.padcol, padcol), (c.padcmp, padcmp), (c.identf, c.d_identf)):
        P.dma("sp", dst.ap, src, writes=[dst])
    c.nt = [dict(xt=(P.sb(f"xt{i}", [128, D], F32) if i == 0 else None), xs=None, xn=P.sb(f"xn{i}", [128, D], BF16),
                 ss=P.sb(f"ss{i}", [128, 8], F32), xnT=P.sb(f"xnT{i}", [128, 16, 128], BF16)) for i in range(2)]
    NTOK = 8 * 128 + NS
    c.oattT = P.sb("oattT", [128, 8, NTOK], BF16)
    with P.scope():
        NK = NBLK * 128 + NS
        c.Ksel = P.sb("Ksel", [128, 4, NK], BF16)
        c.Kwin = P.sb("Kwin", [128, 2, NK], BF16)
        c.Vsel1 = P.sb("Vsel1", [128, NBLK + 1, 4, 65], BF16)
        c.Vwin1 = P.sb("Vwin1", [128, NBLK + 1, 4, 65], BF16)
        c.KcT = P.sb("KcT", [128, 2, 128], BF16)
        c.Vc1 = P.sb("Vc1", [128, 4, 65], BF16)
        if "k1" in passes:
            for g in range(4):
                rows = slice(64, 128) if g < 2 else slice(0, 64)
                P.dma("sp", c.Ksel.ap[rows, g, 0:4096], ehost, writes=[c.Ksel], semkey="ksel_e")
            for V1 in (c.Vsel1, c.Vwin1):
                P.op("pool", lambda e: e.tensor_copy(V1.ap[:, :, :, 64], bc(c.padcol.ap, [128, NBLK + 1, 4], 2)), [c.padcol], [V1])
            P.op("pool", lambda e: e.tensor_copy(c.Vc1.ap[:, :, 64], bc(c.padcmp.ap[:, 0], [128, 4], 1)), [c.padcmp], [c.Vc1])
            pass_k1(P, c)
        if "k2" in passes:
            pass_k2(P, c)
    if "s" in passes:
        pass_s(P, c)
    c.hs_own = P.sb("hs_own", [128, 8, NTOK], BF16)
    if "r" in passes:
        pass_r(P, c)
    if c.dbg:
        P.dma("sp", c.dbg_oatt, c.oattT.ap, reads=[c.oattT], out=True)
        P.dma("sp", c.dbg_hs, c.hs_own.ap, reads=[c.hs_own], out=True)
        P.barrier()
    if "d" in passes:
        pass_d(P, c)
    print("sbuf peak bytes", P.sb_peak * 4, "ops", len(P.ops))
    P.finish()
    print("sems", P.nsem)
    return nc


def host_prep(inputs, core):
    b, j = core // 4, core % 4
    f32 = np.float32
    m = {}
    xp = inputs["x_prompt"][b]
    pad = (3 - j) * 128
    xv = np.zeros(((NBLK + 1) * 128, D), f32)
    xv[pad:NBLK * 128] = xp[:NBLK * 128 - pad]
    xv[NBLK * 128:NBLK * 128 + NS] = inputs["x_sample"][4 * core:4 * core + 4].reshape(NS, D)
    m["xv"] = xv
    m["w_in"] = np.ascontiguousarray(inputs["w_in"][0])
    m["g_rep"] = np.ascontiguousarray(np.broadcast_to(inputs["norm_g"][0][None, :], (128, D)))
    m["qg_rep"] = np.ascontiguousarray(np.broadcast_to(inputs["q_norm_g"][0][None, :], (128, 64)))
    m["kg_rep"] = np.ascontiguousarray(np.broadcast_to(inputs["k_norm_g"][0][None], (128, 3, 64)))
    wc = inputs["w_cmp"][0]
    wc_rep = np.broadcast_to(wc[:, :, None, :], (32, 2, 4, 64)).reshape(32, 512)
    m["wc_rep"] = np.ascontiguousarray(np.tile(wc_rep, (4, 1)))
    m["ident"] = np.eye(128).astype(ml_dtypes.bfloat16)
    m["identf"] = np.eye(128).astype(f32)
    t = np.arange(128)[:, None]
    cc = np.arange(252)[None, :]
    m["indw"] = (cc - 124 == t // 32).astype(f32)
    key = np.arange(4096)[None, :]
    m["ehost"] = (key // 64 == np.arange(64)[:, None]).astype(ml_dtypes.bfloat16)
    padcol = np.ones((128, NBLK + 1), f32)
    padcol[:, :3 - j] = 0.0
    m["padcol"] = padcol
    def fm(v):
        v = np.asarray(v, f32)
        lead = v.shape[:-1]
        return np.ascontiguousarray(np.moveaxis(v.reshape(lead + (8, 128)), (-2, -1), (1, 0)).reshape((128, 8) + lead))
    for nm, w in (("wrg_bd", inputs["w_rg"][0]), ("wig_bd", inputs["w_ig"][0])):
        bd = np.zeros((128, 8, 128), f32)
        for cch in range(8):
            for hb in range(2):
                bd[hb * 64:(hb + 1) * 64, cch, hb * 64:(hb + 1) * 64] = w[2 * cch + hb]
        m[nm] = bd
    m["convw_T"] = fm(inputs["conv_w"][0])
    m["rsm_T"] = np.ascontiguousarray(np.stack([fm(inputs["conv_b"][0]), fm(inputs["b_rg"][0]), fm(inputs["b_ig"][0]),
                                                  fm(inputs["lru_lambda"][0])], axis=1))
    vrow = np.ones((128, 384), f32)
    vrow[:, :pad] = 0.0
    m["vrow"] = vrow
    bf = ml_dtypes.bfloat16
    n0, jb0 = 4 * (3 - j), 2 * (3 - j)
    p = np.arange(128)
    cmask = np.zeros((128, 8, 128), f32)
    candm = np.zeros((128, 8, 64), f32)
    addm = np.zeros((128, 8, 64), f32)
    nn = np.arange(128)[None, :]
    jj = np.arange(64)[None, :]
    for i in range(8):
        tq = ((4 * i + 3) * 128 + p)[:, None]
        cur = tq // 64
        cmask[:, i, :] = ((nn >= n0) & ((nn + 1) * 32 - 1 <= tq)).astype(f32)
        cand = (jj >= jb0) & (jj < cur)
        forced = cand & ((jj == jb0) | (jj == cur - 1))
        candm[:, i, :] = (cand & ~forced).astype(f32)
        addm[:, i, :] = np.where(forced | (jj == cur), 10.0, np.where(cand, 0.0, -1.0)).astype(f32)
    m["cmask"] = cmask.astype(bf)
    m["cmaskT"] = np.ascontiguousarray(cmask.transpose(2, 1, 0)).astype(bf)
    m["candm"] = candm
    m["addm"] = addm
    kk = np.arange(128)[:, None]
    qq = np.arange(128)[None, :]
    cb0 = np.where(kk <= qq, 0.0, NEGB).astype(f32)
    cb1 = np.where(kk > qq, 0.0, NEGB).astype(f32)
    m["CB"] = np.ascontiguousarray(np.stack([np.tile(cb0, (1, 4)), np.tile(cb1, (1, 4))], axis=1)).astype(bf)
    m["padcmp"] = (np.arange(128)[:, None] >= n0).astype(f32)
    m["w_att_out"] = np.ascontiguousarray(inputs["w_att_out"][0])
    m["w_rnn_out"] = np.ascontiguousarray(inputs["w_rnn_out"][0])
    m["w_out"] = np.ascontiguousarray(inputs["w_out"][0])
    m["cache_win"] = np.ascontiguousarray(inputs["cache_win"][0][4 * core:4 * core + 4]).reshape(4, 512, 512)
    m["ptab"] = np.ascontiguousarray(inputs["page_table"][4 * core:4 * core + 4]).astype(np.int32)
    m["wc16"] = np.ascontiguousarray(np.broadcast_to(wc[None], (128, 32, 2, 64)))
    m["iota16"] = np.ascontiguousarray(np.broadcast_to(np.arange(16, dtype=f32)[None], (128, 16)))
    m["mwin0"] = (np.arange(128)[:, None] > np.arange(8)[None, :]).astype(bf)
    m["caus8"] = (np.arange(128)[:, None] <= np.arange(8)[None, :]).astype(bf)
    m["h0T"] = fm(inputs["state_h"][0][4 * core:4 * core + 4])
    m["sconvT"] = fm(inputs["state_conv"][0][4 * core:4 * core + 4])
    return m


_NC_CACHE = {}


def kernel(**inputs):
    inputs = {k: np.asarray(v) for k, v in inputs.items()}
    npool = inputs["cache_cmp"].shape[1]
    if npool not in _NC_CACHE:
        _NC_CACHE[npool] = build_program(pool_pages=npool)
    nc = _NC_CACHE[npool]
    ccmp = np.ascontiguousarray(inputs["cache_cmp"][0]).reshape(npool, 128, 512)
    csel = np.ascontiguousarray(inputs["cache_sel"][0]).reshape(npool, 128, 512)
    in_maps = []
    for core in range(8):
        m = host_prep(inputs, core)
        m["cache_cmp"] = ccmp
        m["cache_sel"] = csel
        in_maps.append(m)
    res = run_bass_kernel_spmd(nc, in_maps, core_ids=list(range(8)))
    R = res.results
    f32 = np.float32
    B, T = 2, 4096
    y_p = np.empty((B, T, D), f32)
    cmp_p = np.empty((1, B, T, 512), f32)
    sel_p = np.empty((1, B, T, 512), f32)
    win_p = np.empty((1, B, 512, 512), f32)
    h_p = np.empty((1, B, DR), f32)
    conv_p = np.empty((1, B, 3, DR), f32)
    y_s = np.empty((32, 8, D), f32)
    cmp_s = np.empty((1, 32, 8, 512), f32)
    sel_s = np.empty((1, 32, 8, 512), f32)
    win_s = np.empty((1, 32, 512, 512), f32)
    h_s = np.empty((1, 32, DR), f32)
    conv_s = np.empty((1, 32, 3, DR), f32)
    for core in range(8):
        b, j = core // 4, core % 4
        r = R[core]
        for i in range(8):
            qb = 4 * i + j
            sl = slice(qb * 128, (qb + 1) * 128)
            y_p[b, sl] = r["o_y"][i]
            cmp_p[0, b, sl] = r["o_cmp"][i]
            sel_p[0, b, sl] = r["o_sel"][i]
        win_p[0, b, j * 128:(j + 1) * 128] = r["o_win"]
        if j == 3:
            h_p[0, b] = r["o_hp"][0]
            conv_p[0, b] = r["o_convp"]
        bs = slice(4 * core, 4 * core + 4)
        y_s[bs] = np.asarray(r["o_ys"]).reshape(4, 8, D)
        cmp_s[0, bs] = np.asarray(r["s_cmp"]).reshape(4, 8, 512)
        sel_s[0, bs] = np.asarray(r["s_sel"]).reshape(4, 8, 512)
        win_s[0, bs] = r["o_wins"]
        h_s[0, bs] = r["o_hs"]
        conv_s[0, bs] = r["o_convs"]
    sh = (2, 4, 64)
    return (y_p, y_s, cmp_p.reshape(1, B, T, *sh), sel_p.reshape(1, B, T, *sh), win_p.reshape(1, B, 512, *sh), h_p, conv_p,
            cmp_s.reshape(1, 32, 8, *sh), sel_s.reshape(1, 32, 8, *sh), win_s.reshape(1, 32, 512, *sh), h_s, conv_s)
```
